# Optimizing a Trainium2 kernel written in Bass

```python
import jax, jax.numpy as jnp
from jax import lax
import numpy as np

D_MODEL = 1024
BATCH = 2
SEQ = 8192
DEPTH = 1
DEC_BATCH = 128
DEC_SEQ = 1
PAST_LEN = 2048
PAGE_SIZE = 128

D_MIX = D_MODEL
D_ATTN = D_MIX // 2
D_POOL = D_MIX - D_ATTN
HEAD_DIM = 64
N_HEADS = D_ATTN // HEAD_DIM
N_KV_HEADS = 2
GQA = N_HEADS // N_KV_HEADS
CMP_BLOCK = 32
CMP_STRIDE = 16
CMP_HIDDEN = 2 * HEAD_DIM
SEL_BLOCK = 64
N_SEL = 16
WINDOW = 512
Q_BLOCK = 128
N_BRANCH = 3
POOL_WINDOWS = (2, 4, 8, 16)
N_POOL_GROUPS = len(POOL_WINDOWS)
POOL_GROUP_DIM = D_POOL // N_POOL_GROUPS
POOL_STATE = max(POOL_WINDOWS) - 1
KV_W = 2 * N_KV_HEADS * HEAD_DIM
P_TOT = D_ATTN + 3 * KV_W + N_HEADS * N_BRANCH + D_ATTN + D_POOL + D_POOL
RMS_EPS = 1e-6
MASK_BIG = 1e9

kernel_name = 'nsa_pool_parallel_hybrid_step'


def rmsnorm(x, g):
    xf = x.astype(jnp.float32)
    y = xf * lax.rsqrt(jnp.mean(xf * xf, axis=-1, keepdims=True) + RMS_EPS)
    return (y * g.astype(jnp.float32)).astype(x.dtype)


def project(x, norm_g, w_in):
    n, l = x.shape[:2]
    p = jnp.einsum('nld,dp->nlp', rmsnorm(x, norm_g), w_in)
    sizes = [D_ATTN, KV_W, KV_W, KV_W, N_HEADS * N_BRANCH, D_ATTN, D_POOL]
    q, ckv, skv, wkv, gl, za, u, zp = jnp.split(p, np.cumsum(sizes).tolist(), axis=-1)
    kv_shape = (n, l, 2, N_KV_HEADS, HEAD_DIM)
    return (q.reshape(n, l, N_KV_HEADS, GQA, HEAD_DIM), ckv.reshape(kv_shape),
            skv.reshape(kv_shape), wkv.reshape(kv_shape),
            gl.reshape(n, l, N_KV_HEADS, GQA, N_BRANCH), za, u, zp)


def compress_blocks(kv, pe, w1, w2):
    n, l = kv.shape[:2]
    r = CMP_BLOCK // CMP_STRIDE
    nc = (l - CMP_BLOCK) // CMP_STRIDE + 1
    n_chunk = nc + r - 1
    chunks = kv[:, :n_chunk * CMP_STRIDE].reshape(n, n_chunk, CMP_STRIDE, 2, N_KV_HEADS, HEAD_DIM)
    pe_r = pe.reshape(2, r, CMP_STRIDE, HEAD_DIM)
    w1_r = w1.reshape(2, r, CMP_STRIDE, HEAD_DIM, CMP_HIDDEN)
    h = None
    for j in range(r):
        part = chunks[:, j:j + nc] + jnp.transpose(pe_r[:, j], (1, 0, 2))[:, :, None, :]
        term = jnp.einsum('ncsjkd,jsdh->ncjkh', part, w1_r[:, j])
        h = term if h is None else h + term
    return jnp.einsum('ncjkh,jhd->ncjkd', jax.nn.silu(h), w2)


def nsa_attention(q, gate_logits, cmp_kv, slc_kv, win_kv, q_pos0, win_pos0, win_idx0, pe, w1, w2):
    f32 = jnp.float32
    n, lq = q.shape[:2]
    lk = cmp_kv.shape[1]
    qf = q.astype(f32) * (HEAD_DIM ** -0.5)
    gate = jax.nn.sigmoid(gate_logits.astype(f32))
    kv_c = compress_blocks(cmp_kv.astype(f32), pe, w1, w2)
    k_c, v_c = kv_c[:, :, 0], kv_c[:, :, 1]
    nc = kv_c.shape[1]
    c_start = jnp.arange(nc) * CMP_STRIDE
    c_last = c_start + CMP_BLOCK - 1
    ns = -(-lk // SEL_BLOCK)
    n_top = min(N_SEL, ns)
    s_start = jnp.arange(ns) * SEL_BLOCK
    cover = ((c_start[:, None] < s_start[None, :] + SEL_BLOCK)
             & (c_start[:, None] + CMP_BLOCK > s_start[None, :])).astype(f32)
    slc = jnp.pad(slc_kv.astype(f32), ((0, 0), (0, ns * SEL_BLOCK - lk), (0, 0), (0, 0), (0, 0)))
    k_s = jnp.transpose(slc[:, :, 0], (0, 2, 1, 3))
    v_s = jnp.transpose(slc[:, :, 1], (0, 2, 1, 3))
    win = jnp.pad(win_kv.astype(f32), ((0, 0), (WINDOW, 0), (0, 0), (0, 0), (0, 0)))
    qb = Q_BLOCK if lq % Q_BLOCK == 0 else lq
    n_blk = lq // qb
    gather_rows = jax.vmap(jax.vmap(lambda rows, idx: rows[idx]))

    def block(bi):
        qs = bi * qb
        qq = lax.dynamic_slice_in_dim(qf, qs, qb, axis=1)
        gg = lax.dynamic_slice_in_dim(gate, qs, qb, axis=1)
        t = q_pos0 + qs + jnp.arange(qb)
        s_c = jnp.einsum('nqkgd,nckd->nkgqc', qq, k_c)
        ok_c = c_last[None, :] <= t[:, None]
        p_c = jnp.where(ok_c, jax.nn.softmax(jnp.where(ok_c, s_c, -MASK_BIG), axis=-1), 0.0)
        o_c = jnp.einsum('nkgqc,nckd->nqkgd', p_c, v_c)
        imp = jnp.einsum('nkgqc,cs->nkqs', p_c, cover)
        blk = jnp.arange(ns)[None, :]
        cur = (t // SEL_BLOCK)[:, None]
        forced = (blk == 0) | (blk == cur) | (blk == cur - 1)
        imp = jnp.where(forced, MASK_BIG, jnp.where(s_start[None, :] <= t[:, None], imp, -MASK_BIG))
        _, top = lax.top_k(imp, n_top)
        tok = (top[..., None] * SEL_BLOCK + jnp.arange(SEL_BLOCK)).reshape(n, N_KV_HEADS, qb, n_top * SEL_BLOCK)
        kg = gather_rows(k_s, tok)
        vg = gather_rows(v_s, tok)
        s_s = jnp.einsum('nqkgd,nkqsd->nkgqs', qq, kg)
        ok_s = (tok <= t[None, None, :, None])[:, :, None]
        p_s = jax.nn.softmax(jnp.where(ok_s, s_s, -MASK_BIG), axis=-1)
        o_s = jnp.einsum('nkgqs,nkqsd->nqkgd', p_s, vg)
        wrows = lax.dynamic_slice_in_dim(win, win_idx0 + qs, WINDOW + qb, axis=1)
        kp = win_pos0 + win_idx0 + qs - WINDOW + jnp.arange(WINDOW + qb)
        ok_w = (kp[None, :] >= 0) & (kp[None, :] <= t[:, None]) & (kp[None, :] > t[:, None] - WINDOW)
        s_w = jnp.einsum('nqkgd,nskd->nkgqs', qq, wrows[:, :, 0])
        p_w = jax.nn.softmax(jnp.where(ok_w, s_w, -MASK_BIG), axis=-1)
        o_w = jnp.einsum('nkgqs,nskd->nqkgd', p_w, wrows[:, :, 1])
        return gg[..., 0:1] * o_c + gg[..., 1:2] * o_s + gg[..., 2:3] * o_w

    out = lax.map(block, jnp.arange(n_blk))
    return jnp.moveaxis(out, 0, 1).reshape(n, lq, N_KV_HEADS * GQA * HEAD_DIM)


def pool_mixer(u_ext, n_out, w_pool, pool_scale):
    f32 = jnp.float32
    uf = u_ext.astype(f32)
    n, le, _ = uf.shape
    csum = jnp.concatenate([jnp.zeros((n, 1, D_POOL), f32), jnp.cumsum(uf, axis=1)], axis=1)
    win = jnp.repeat(jnp.array(POOL_WINDOWS, jnp.int32), POOL_GROUP_DIM)
    end = le - n_out + 1 + jnp.arange(n_out)
    start = jnp.maximum(end[:, None] - win[None, :], 0)
    lo = jnp.take_along_axis(csum, jnp.broadcast_to(start[None], (n, n_out, D_POOL)), axis=1)
    mean = (csum[:, le - n_out + 1:] - lo) / (end[:, None] - start).astype(f32)
    d = (mean - uf[:, le - n_out:]).reshape(n, n_out, N_POOL_GROUPS, POOL_GROUP_DIM)
    y = jnp.einsum('nlgc,gcd->nlgd', d, w_pool).reshape(n, n_out, D_POOL)
    return y * pool_scale


def combine(x, o_attn, za, o_pool, zp, w_out):
    a = o_attn * jax.nn.silu(za.astype(jnp.float32))
    b = o_pool * jax.nn.silu(zp.astype(jnp.float32))
    mix = jnp.concatenate([a, b], axis=-1).astype(x.dtype)
    return x + jnp.einsum('nlm,md->nld', mix, w_out)


def gather_pages(cache, page_table):
    g = cache[page_table]
    return g.reshape(g.shape[0], g.shape[1] * g.shape[2], *g.shape[3:])


def setup_inputs(seed: int = 0) -> dict:
    key = jax.random.key(seed)
    ks = jax.random.split(key, 20)
    f = jnp.float32
    n_pages = PAST_LEN // PAGE_SIZE
    n_used = DEC_BATCH * n_pages
    n_phys = (5 * n_used) // 4
    win_c = min(WINDOW, PAST_LEN)
    nrm = lambda k, s: jax.random.normal(k, s, f)
    return {
        'x_prompt': nrm(ks[0], (BATCH, SEQ, D_MODEL)),
        'x_sample': nrm(ks[1], (DEC_BATCH, DEC_SEQ, D_MODEL)),
        'cache_cmp_kv': nrm(ks[2], (DEPTH, n_phys, PAGE_SIZE, 2, N_KV_HEADS, HEAD_DIM)),
        'cache_slc_kv': nrm(ks[3], (DEPTH, n_phys, PAGE_SIZE, 2, N_KV_HEADS, HEAD_DIM)),
        'cache_win_kv': nrm(ks[4], (DEPTH, DEC_BATCH, win_c, 2, N_KV_HEADS, HEAD_DIM)),
        'state_pool': nrm(ks[5], (DEPTH, DEC_BATCH, POOL_STATE, D_POOL)),
        'page_table': jax.random.permutation(ks[6], n_phys)[:n_used].reshape(DEC_BATCH, n_pages).astype(jnp.int32),
        'norm_g': 1.0 + 0.05 * nrm(ks[7], (DEPTH, D_MODEL)),
        'w_in': nrm(ks[8], (DEPTH, D_MODEL, P_TOT)) * D_MODEL ** -0.5,
        'cmp_pe': 0.1 * nrm(ks[9], (DEPTH, 2, CMP_BLOCK, HEAD_DIM)),
        'cmp_w1': nrm(ks[10], (DEPTH, 2, CMP_BLOCK, HEAD_DIM, CMP_HIDDEN)) * (CMP_BLOCK * HEAD_DIM) ** -0.5,
        'cmp_w2': nrm(ks[11], (DEPTH, 2, CMP_HIDDEN, HEAD_DIM)) * CMP_HIDDEN ** -0.5,
        'pool_w': nrm(ks[12], (DEPTH, N_POOL_GROUPS, POOL_GROUP_DIM, POOL_GROUP_DIM)) * POOL_GROUP_DIM ** -0.5,
        'pool_scale': 1.0 + 0.05 * nrm(ks[13], (DEPTH, D_POOL)),
        'w_out': nrm(ks[14], (DEPTH, D_MIX, D_MODEL)) * D_MIX ** -0.5,
        'final_g': 1.0 + 0.05 * nrm(ks[15], (D_MODEL,)),
    }


def reference(x_prompt, x_sample, cache_cmp_kv, cache_slc_kv, cache_win_kv, state_pool, page_table,
              norm_g, w_in, cmp_pe, cmp_w1, cmp_w2, pool_w, pool_scale, w_out, final_g):
    win_p = min(WINDOW, SEQ)
    win_c = cache_win_kv.shape[2]
    xp, xs = x_prompt, x_sample
    p_cmp, p_slc, p_win, p_pool = [], [], [], []
    s_cmp, s_slc, s_win, s_pool = [], [], [], []
    for l in range(DEPTH):
        q, ckv, skv, wkv, gl, za, u, zp = project(xp, norm_g[l], w_in[l])
        o_a = nsa_attention(q, gl, ckv, skv, wkv, 0, 0, 0, cmp_pe[l], cmp_w1[l], cmp_w2[l])
        o_p = pool_mixer(u, SEQ, pool_w[l], pool_scale[l])
        xp = combine(xp, o_a, za, o_p, zp, w_out[l])
        p_cmp.append(ckv)
        p_slc.append(skv)
        p_win.append(wkv[:, SEQ - win_p:])
        p_pool.append(u[:, SEQ - POOL_STATE:])
        q, ckv, skv, wkv, gl, za, u, zp = project(xs, norm_g[l], w_in[l])
        cmp_all = jnp.concatenate([gather_pages(cache_cmp_kv[l], page_table), ckv], axis=1)
        slc_all = jnp.concatenate([gather_pages(cache_slc_kv[l], page_table), skv], axis=1)
        win_all = jnp.concatenate([cache_win_kv[l], wkv], axis=1)
        o_a = nsa_attention(q, gl, cmp_all, slc_all, win_all, PAST_LEN, PAST_LEN - win_c, win_c,
                            cmp_pe[l], cmp_w1[l], cmp_w2[l])
        u_all = jnp.concatenate([state_pool[l], u], axis=1)
        o_p = pool_mixer(u_all, DEC_SEQ, pool_w[l], pool_scale[l])
        xs = combine(xs, o_a, za, o_p, zp, w_out[l])
        s_cmp.append(ckv)
        s_slc.append(skv)
        s_win.append(win_all[:, win_all.shape[1] - win_c:])
        s_pool.append(u_all[:, u_all.shape[1] - POOL_STATE:])
    y_prompt = rmsnorm(xp, final_g)
    y_sample = rmsnorm(xs, final_g)
    new_cmp_kv_prompt = jnp.stack(p_cmp)
    new_slc_kv_prompt = jnp.stack(p_slc)
    new_win_kv_prompt = jnp.stack(p_win)
    new_pool_prompt = jnp.stack(p_pool)
    new_cmp_kv_sample = jnp.stack(s_cmp)
    new_slc_kv_sample = jnp.stack(s_slc)
    new_win_kv_sample = jnp.stack(s_win)
    new_pool_sample = jnp.stack(s_pool)
    return (y_prompt, y_sample, new_cmp_kv_prompt, new_slc_kv_prompt, new_win_kv_prompt, new_pool_prompt,
            new_cmp_kv_sample, new_slc_kv_sample, new_win_kv_sample, new_pool_sample)
```

```python
from contextlib import ExitStack
import numpy as np
import ml_dtypes
import concourse.bass as bass
import concourse.mybir as mybir
from concourse.bass_utils import run_bass_kernel_spmd

F32 = mybir.dt.float32
BF16 = mybir.dt.bfloat16
I32 = mybir.dt.int32
AF = mybir.ActivationFunctionType
ALU = mybir.AluOpType
AX = mybir.AxisListType

NEG = -30000.0
DEBUG_NAMES = None
MMINFO = {}
D = 1024
PT = 2840
NM = 16
NSAMP = 16
NPHYS = 2560
NS_LOOP = 16
SKIP_SAMPLE = False
STAGE = 99
C_Q, C_CKV, C_SKV, C_WKV, C_GL, C_ZA, C_U, C_ZP = 0, 512, 768, 1024, 1280, 1304, 1816, 2328


def n_ct(m):
    return (32 * m + 30) // 128 + 1


CT_BASE = [sum(n_ct(i) for i in range(m)) for m in range(NM)]
N_CTT = sum(n_ct(i) for i in range(NM))


def cfg(nm, nsamp, nphys):
    global NM, NSAMP, NPHYS, CT_BASE, N_CTT, _NC
    NM, NSAMP, NPHYS = nm, nsamp, nphys
    CT_BASE = [sum(n_ct(i) for i in range(m)) for m in range(NM)]
    N_CTT = sum(n_ct(i) for i in range(NM))
    _NC = None


class Op:
    __slots__ = ("eng", "fn", "deps", "dma", "signal", "cnt", "sem", "idx", "guard", "src")


class Sched:
    ENG = ["pe", "act", "dve", "pool", "sp"]
    R = 14

    def __init__(self):
        self.ops = {e: [] for e in self.ENG}
        self.lastw = {}
        self.readers = {}

    def add(self, eng, fn, reads=(), writes=(), dma=False):
        op = Op()
        op.eng, op.fn, op.dma, op.signal, op.deps = eng, fn, dma, False, []
        op.cnt = op.sem = op.guard = None
        seen = set()

        def dep(o):
            if o is None or id(o) in seen:
                return
            seen.add(id(o))
            op.deps.append(o)

        for k in reads:
            dep(self.lastw.get(k))
            if isinstance(k, tuple) and k[0] in ("psg", "pacc", "pimp"):
                for r in self.readers.get(k, ()):
                    if r.eng != eng:
                        dep(r)
        for k in writes:
            dep(self.lastw.get(k))
            for r in self.readers.get(k, ()):
                dep(r)
        for k in reads:
            self.readers.setdefault(k, []).append(op)
        for k in writes:
            self.lastw[k] = op
            self.readers[k] = []
        op.idx = len(self.ops[eng])
        self.ops[eng].append(op)
        import sys
        fr = sys._getframe(1)
        chain = []
        while fr is not None and len(chain) < 4:
            chain.append(fr.f_lineno)
            fr = fr.f_back
        op.src = chain
        return op

    def finalize(self):
        for e in self.ENG:
            for op in self.ops[e]:
                best = {}
                dmas = []
                for d in op.deps:
                    if d.dma:
                        dmas.append(d)
                    else:
                        if d.eng == "pe" and op.eng == "pe" and not op.dma:
                            continue
                        if d.eng not in best or best[d.eng].idx < d.idx:
                            best[d.eng] = d
                op.deps = dmas + list(best.values())
                for d in op.deps:
                    d.signal = True
        for e in self.ENG:
            c = 0
            nd = 0
            for op in self.ops[e]:
                if op.dma:
                    op.sem = nd % self.R
                    op.cnt = 16 * (nd // self.R + 1)
                    op.guard = 16 * (nd // self.R)
                    nd += 1
                elif op.signal:
                    c += 1
                    op.cnt = c


def build_program():
    nc = bass.Bass("TRN2", target_bir_lowering=False)
    es = ExitStack()
    S = Sched()

    def din(name, shape, dt=F32):
        return nc.dram_tensor(name, list(shape), dt, kind="ExternalInput").ap()

    def dout(name, shape, dt=F32):
        return nc.dram_tensor(name, list(shape), dt, kind="ExternalOutput").ap()

    def sb(name, shape, dt):
        return es.enter_context(nc.sbuf_tensor(name, list(shape), dt))

    xb = din("xb", [NM * 512, D])
    x_own = din("x_own", [NM * 128, D])
    x_halo = din("x_halo", [NM * 64, D])
    xs = din("xs", [NSAMP, D])
    cache_cmp = din("cache_cmp", [NPHYS * 32, 1024])
    cache_slc = din("cache_slc", [NPHYS * 32, 1024])
    win_c = din("win_c", [NSAMP, 512 * 256])
    pool_c = din("pool_c", [NSAMP, 15, 512])
    ptx = din("ptx", [128, NSAMP], I32)
    norm_g = din("norm_g", [1, D])
    w_in = din("w_in", [D, PT])
    cmp_pe = din("cmp_pe", [64, 64])
    cmp_w1 = din("cmp_w1", [2, 32, 64, 128])
    cmp_w2 = din("cmp_w2", [2, 128, 64])
    pool_w = din("pool_w", [4, 128, 128])
    pool_scale = din("pool_scale", [1, 512])
    w_out = din("w_out", [D, D])
    final_g = din("final_g", [1, D])
    c_ident_bf = din("c_ident_bf", [128, 128], BF16)
    c_ident_f = din("c_ident_f", [128, 128])
    c_ident4 = din("c_ident4", [128, 512], BF16)
    c_triZ = din("c_triZ", [128, 4 * 128], BF16)
    c_winZ = din("c_winZ", [128, 8 * 128], BF16)
    c_cmpZ = din("c_cmpZ", [128, N_CTT * 128], BF16)
    c_cover = din("c_cover", [128, 4 * 128])
    c_cand = din("c_cand", [128, NM * 128], BF16)
    c_forced = din("c_forced", [128, NM * 128], BF16)
    c_poolcorr = din("c_poolcorr", [128, 4 * 128])
    c_cover33 = din("c_cover33", [128, 40])
    c_cand_s = din("c_cand_s", [16, 40])
    c_forced_s = din("c_forced_s", [16, 40])
    c_e16 = din("c_e16", [128, 64], BF16)
    c_sub8 = din("c_sub8", [128, 1])

    y_own = dout("y_own", [NM * 128, D])
    kv_own = dout("kv_own", [NM * 128, 768])
    pool_lastT = dout("pool_lastT", [128, 64])
    ys = dout("ys", [NSAMP, D])
    skv = dout("skv", [NSAMP, 768])
    win_out = dout("win_out", [NSAMP, 512 * 256])
    pool_out = dout("pool_out", [NSAMP, 15, 512])
    scr_o = nc.dram_tensor("scr_o", [NSAMP, 2, 3, 4, 65], F32, kind="Internal").ap()
    scr_i = nc.dram_tensor("scr_i", [NSAMP, 2, 4, 40], F32, kind="Internal").ap()

    W_bf = sb("W_bf", [128, 8, PT], BF16)
    W1_2 = sb("W1_2", [128, 2, 32, 128], BF16)
    w2p = sb("w2p", [128, 2, 2, 128], BF16)
    pw_bf = sb("pw_bf", [128, 4, 128], BF16)
    hbias = sb("hbias", [128, 2], F32)
    peT = sb("peT", [128, 64], BF16)
    g_col = sb("g_col", [128, 8], F32)
    fg_bc = sb("fg_bc", [128, D], F32)
    ident_bf = sb("ident_bf", [128, 128], BF16)
    ident_f = sb("ident_f", [128, 128], F32)
    ident4 = sb("ident4", [128, 512], BF16)
    triZ = sb("triZ", [128, 4, 128], BF16)
    winZ = sb("winZ", [128, 8, 128], BF16)
    cover = sb("cover", [128, 4, 128], F32)
    cover33 = sb("cover33", [128, 40], F32)
    cand_s = sb("cand_s", [16, 40], F32)
    forced_s = sb("forced_s", [16, 40], F32)
    e16 = sb("e16", [128, 16, 4], BF16)
    sub8 = sb("sub8", [128, 1], F32)
    cmpZ_t = [sb(f"cmpZ{i}", [128, 4, 128], BF16) for i in range(2)]
    cand_t = [sb(f"cand{i}", [128, 128], BF16) for i in range(2)]
    forced_t = [sb(f"forced{i}", [128, 128], BF16) for i in range(2)]
    Wo_t = [sb(f"Wo{i}", [128, D], BF16) for i in range(2)]

    KsT = sb("KsT", [128, 8192], BF16)
    Vs = sb("Vs", [128, 64, 2, 66], BF16)
    KwT = sb("KwT", [128, 8, 128], BF16)
    Vw = sb("Vw", [128, 8, 2, 66], BF16)
    CT = sb("CT", [128, 2, 528], BF16)
    kcT = sb("kcT", [128, 512], BF16)
    vcT = sb("vcT", [128, 512], BF16)
    vc_ext = sb("vc_ext", [128, 4, 2, 66], BF16)
    xst = [sb(f"xst{i}", [128, D], F32) for i in range(2)]
    xn_bf = sb("xn_bf", [128, D], BF16)
    ssq = sb("ssq", [128, 8], F32)
    xT_g = sb("xT_g", [128, 8, 512], BF16)
    xT_o = sb("xT_o", [128, 8, 128], BF16)
    xT_h = sb("xT_h", [128, 8, 64], BF16)
    a_act = sb("a_act", [128, 4, 128], BF16)
    qT = sb("qT", [128, 4, 128], BF16)
    uT = sb("uT", [128, 4, 4, 48], F32)
    uS = [sb(f"uS{i}", [128, 4, 48], F32) for i in range(2)]
    dT = sb("dT", [128, 4, 128], BF16)
    okv = sb("okv", [128, 768], F32)
    gate = sb("gate", [128, 24], F32)
    sza = sb("sza", [128, 512], F32)
    szp = sb("szp", [128, 512], F32)
    pT_f = [sb(f"pT_f{i}", [128, 4, 128], F32) for i in range(1)]
    pT_b = [sb(f"pT_b{i}", [128, 4, 128], BF16) for i in range(3)]
    Oall = sb("Oall", [128, 2, 3, 4, 65], F32)
    rdall = sb("rdall", [128, 2, 3, 4], F32)
    wgt = sb("wgt", [128, 2, 3, 4], F32)
    otmp = sb("otmp", [128, 2, 3, 4, 64], F32)
    osum = sb("osum", [128, 512], F32)
    mix_bf = sb("mix_bf", [128, D], BF16)
    mixT = sb("mixT", [128, 8, 128], BF16)
    xres = sb("xres", [128, D], F32)
    ypool_sb = sb("ypool_sb", [128, 512], F32)
    impt = sb("impt", [128, 128], F32)
    candt = sb("candt", [128, 128], F32)
    work = sb("work", [128, 128], F32)
    work2 = sb("work2", [128, 128], F32)
    mx8 = sb("mx8", [128, 16], F32)
    selt = sb("selt", [128, 128], F32)
    selneg = [sb(f"selneg{i}", [128, 128], BF16) for i in range(2)]
    selx_t = [sb(f"selx{i}", [128, 128], BF16) for i in range(4)]
    selx_s = sb("selx_s", [128, 2, 128], BF16)
    ptx_i = sb("ptx_i", [128, NSAMP], I32)
    ptx_f = sb("ptx_f", [128, NSAMP], F32)
    idx_i = sb("idx_i", [128, 4, NSAMP], I32)
    idx_f = sb("idx_f", [128, NSAMP], F32)
    Ps = sb("Ps", [16, 1280], F32)
    qT_s = sb("qT_s", [128, 4, 16], BF16)
    KsTn = sb("KsTn", [128, 16], BF16)
    vnew = sb("vnew", [1, NSAMP, 2, 66], BF16)
    vc_s = sb("vc_s", [128, 2, 66], BF16)
    pS_f = sb("pS_f", [128, 4], F32)
    pS_b = sb("pS_b", [128, 17, 4], BF16)
    pN_b = sb("pN_b", [1, 4], BF16)
    Osm = [sb(f"Osm{i}", [4, 2, 3, 65], F32) for i in range(2)]
    impst = [sb(f"impst{i}", [4, 2, 40], F32) for i in range(2)]
    impr = otmp[0:16, 0].rearrange("p b g d -> p (b g d)")[:, 0:320].rearrange("p (k g s) -> p k g s", k=2, g=4)
    seln_s = sb("seln_s", [16, 2, 40], BF16)
    sred = osum[0:16, :]
    d_s = mix_bf[0:16, 0:512]
    d_s_f = candt[0:16, 0:128]
    CTs = KsT[:, 4096:8192].rearrange("p (j t) -> p j t", j=2)
    CTs_keys = [("KsT", i) for i in range(32, 64)]
    cmp_bf = xT_g[:].rearrange("p k t -> p (k t)").rearrange("p (e c) -> p e c", e=16)
    slck_bf = xT_g[:, 4:8, :].rearrange("p k t -> p (k t)").rearrange("p (e c) -> p e c", e=16)

    PSUM = [es.enter_context(nc.psum_tensor(f"ps{i}", [128, 512], F32)) for i in range(8)]

    fresh = {}

    class Ring:
        def __init__(self, name, tiles):
            self.name, self.tiles, self.i = name, tiles, 0

        def next(self):
            k = self.i % len(self.tiles)
            self.i += 1
            if self.name in ("psg", "pacc", "pimp"):
                fresh[self.tiles[k].name] = True
            return self.tiles[k], (self.name, k)

    psg = Ring("psg", PSUM[0:4])
    pacc = Ring("pacc", PSUM[4:7])
    pimp_r = Ring("pimp", PSUM[7:8])
    xst_r = Ring("xst", xst)
    pTf_r = Ring("pTf", pT_f)
    pTb_r = Ring("pTb", pT_b)
    pTf_state = [0]

    def pTf_next():
        k = pTf_state[0] % 2
        pTf_state[0] += 1
        if k == 0:
            return pT_f[0][:], ("pTf", 0)
        return okv[:, 0:512].rearrange("p (g q) -> p g q", g=4), "okv"

    pTb_state = [0]

    def pTb_next():
        k = pTb_state[0] % 4
        pTb_state[0] += 1
        if k < 3:
            return pT_b[k], ("pTb", k)
        return a_act, "a_act"
    seln_r = Ring("seln", selneg)
    selx_r = Ring("selx", selx_t)
    uS_r = Ring("uS", uS)
    cmpZ_r = Ring("cmpZ", cmpZ_t)
    cand_r = Ring("cand", cand_t)
    forced_r = Ring("forced", forced_t)
    Wo_r = Ring("Wo", Wo_t)
    Osm_r = Ring("Osm", Osm)
    impst_r = Ring("impst", impst)
    xgk = "xT_g"

    def mm(out, lhsT, rhs, start, stop, rd, wr):
        nm_ = out.tensor.name
        st_ = fresh.get(nm_, True)
        fresh[nm_] = False
        op = S.add("pe", lambda e: e.matmul(out, lhsT, rhs, start=st_, stop=stop, skip_group_check=True), rd, wr)
        bp = lhsT.base_partition()
        MMINFO[id(op)] = (bp, lhsT.shape[0], nm_)

    def act(out, in_, func, rd, wr, scale=1.0, bias=0.0, accum=None):
        if accum is None:
            S.add("act", lambda e: e.activation(out, in_, func, bias=bias, scale=scale), rd, wr)
        else:
            S.add("act", lambda e: e.activation(out, in_, func, bias=bias, scale=scale, accum_out=accum), rd, wr)

    def cp(eng, out, in_, rd, wr):
        if eng == "act":
            S.add("act", lambda e: e.copy(out, in_), rd, wr)
        else:
            S.add(eng, lambda e: e.tensor_copy(out, in_), rd, wr)

    def tt(eng, out, a, b, op, rd, wr):
        S.add(eng, lambda e: e.tensor_tensor(out, a, b, op), rd, wr)

    def ts(eng, out, a, s1, s2, op0, op1, rd, wr):
        if eng == "act":
            assert s2 is None and op0 == ALU.mult
            S.add("act", lambda e: e.activation(out, a, AF.Copy, scale=s1), rd, wr)
            return
        if s2 is None:
            S.add(eng, lambda e: e.tensor_scalar(out, a, s1, None, op0), rd, wr)
        else:
            S.add(eng, lambda e: e.tensor_scalar(out, a, s1, s2, op0, op1), rd, wr)

    def stt(eng, out, a, sc, b, op0, op1, rd, wr):
        S.add(eng, lambda e: e.scalar_tensor_tensor(out, a, sc, b, op0, op1), rd, wr)

    def memset(eng, ap, v, wr):
        S.add(eng, lambda e: e.memset(ap, v), (), wr)

    def dma(out, in_, rd, wr, q="sp", slow=False):
        if slow:
            S.add(q, lambda e: e.dma_start(out=out, in_=in_, allow_slow_non_contiguous=True), rd, wr, dma=True)
        else:
            S.add(q, lambda e: e.dma_start(out=out, in_=in_), rd, wr, dma=True)

    alt = [0]

    def evac(out, in_, rd, wr):
        alt[0] ^= 1
        cp("act" if alt[0] else "dve", out, in_, rd, wr)

    K = lambda t: t.name

    for t, src, rs, kw in [
        (ident_bf, c_ident_bf, None, {}), (ident_f, c_ident_f, None, {}), (ident4, c_ident4, None, {}),
        (triZ, c_triZ, "p (a b) -> p a b", dict(a=4)), (winZ, c_winZ, "p (a b) -> p a b", dict(a=8)),
        (cover, c_cover, "p (a b) -> p a b", dict(a=4)), (cover33, c_cover33, None, {}),
        (cand_s, c_cand_s, None, {}), (forced_s, c_forced_s, None, {}),
        (e16, c_e16, "p (a b) -> p a b", dict(a=16)), (sub8, c_sub8, None, {}), (ptx_i, ptx, None, {}),
    ]:
        s_ap = src.rearrange(rs, **kw) if rs else src
        dma(t[:], s_ap, (), [K(t)])
    dma(fg_bc[:], final_g[0:1, :].partition_broadcast(128).rearrange("p a d -> p (a d)"), (), [K(fg_bc)])
    dma(g_col[:], norm_g.rearrange("o (k p) -> p (o k)", p=128), (), ["g_col"], slow=True)

    ci = [0]

    def cast_eng():
        ci[0] += 1
        return ["act", "dve"][ci[0] % 2]

    for kc in range(8):
        for c0, c1 in [(0, 1024), (1024, 2048), (2048, PT)]:
            st, sk = xst_r.next()
            dma(st[:, 0:c1 - c0], w_in[kc * 128:(kc + 1) * 128, c0:c1], (), [sk])
            if c0 == 0:
                ts(cast_eng(), W_bf[:, kc, 0:512].rearrange("p (g k d) -> p k g d", g=4, k=2),
                   st[:, 0:512].rearrange("p (k g d) -> p k g d", k=2, g=4), g_col[:, kc:kc + 1], None, ALU.mult, None,
                   [sk, "g_col"], [("W", kc)])
                ts(cast_eng(), W_bf[:, kc, 512:1024], st[:, 512:1024], g_col[:, kc:kc + 1], None, ALU.mult, None,
                   [sk, "g_col"], [("W", kc)])
            else:
                ts(cast_eng(), W_bf[:, kc, c0:c1], st[:, 0:c1 - c0], g_col[:, kc:kc + 1], None, ALU.mult, None,
                   [sk, "g_col"], [("W", kc)])
    for j in range(2):
        for iq in range(4):
            st, sk = xst_r.next()
            src = cmp_w1[j, iq * 8:(iq + 1) * 8].rearrange("i d h -> d i h")
            stv = st[:, 0:1024].rearrange("p (i h) -> p i h", i=8)
            dma(stv[0:64], src, (), [sk])
            dma(stv[64:128], src, (), [sk])
            cp(cast_eng(), W1_2[:, j, iq * 8:(iq + 1) * 8, :], stv, [sk], ["W1"])
    memset("pool", w2p[:], 0.0, ["w2p"])
    st, sk = xst_r.next()
    dma(st[:, 0:128].rearrange("p (j d) -> p j d", j=2), cmp_w2.rearrange("j h d -> h j d"), (), [sk])
    for j in range(2):
        cp("dve", w2p[:, j, 0, 0:64], st[:, j * 64:(j + 1) * 64], [sk], ["w2p"])
        cp("dve", w2p[:, j, 1, 64:128], st[:, j * 64:(j + 1) * 64], [sk], ["w2p"])
    st, sk = xst_r.next()
    dma(st[:, 0:512].rearrange("p (g d) -> p g d", g=4), pool_w.rearrange("g c d -> c g d"), (), [sk])
    dma(st[:, 512:1024], pool_scale[0:1, :].partition_broadcast(128).rearrange("p a d -> p (a d)"), (), [sk])
    tt("dve", pw_bf[:].rearrange("p g d -> p (g d)"), st[:, 0:512], st[:, 512:1024], ALU.mult, [sk], ["pw"])
    st, sk = xst_r.next()
    dma(st[0:64, 0:64], cmp_pe, (), [sk])
    pt_, pk = psg.next()
    mm(pt_[0:64, 0:64], st[0:64, 0:64], ident_f[0:64, 0:64], True, True, [sk, K(ident_f)], [pk])
    cp("dve", peT[0:64, :], pt_[0:64, 0:64], [pk], ["peT"])
    pt_, pk = psg.next()
    for j in range(2):
        for i in range(32):
            mm(pt_[:, j:j + 1], W1_2[0:64, j, i, :], peT[0:64, j * 32 + i:j * 32 + i + 1], i == 0, i == 31,
               ["W1", "peT"], [pk])
    cp("dve", hbias[:], pt_[:, 0:2], [pk], ["hbias"])
    memset("pool", Vs[:], 1.0, [("Vs", i) for i in range(64)])
    memset("pool", Vw[:], 1.0, [("Vw", i) for i in range(8)])
    memset("pool", vc_ext[:], 1.0, ["vc_ext"])
    memset("pool", vc_s[:], 1.0, ["vc_s"])
    memset("pool", vnew[:], 1.0, ["vnew"])
    memset("pool", kcT[:], 0.0, ["kcT"])
    memset("pool", vcT[:], 0.0, ["vcT"])
    memset("pool", CT[:], 0.0, ["CT"])

    def norm_xT(src, rows, dst_fn, dkey, dve_evac=False):
        xt, xk = xst_r.next()
        dma(xt[0:rows, :], src, (), [xk])
        memset("dve", ssq[0:rows, 0:1], 0.0, ["ssq"])
        act(xn_bf[0:rows, :], xt[0:rows, :], AF.Square, [xk], ["xn", "ssq"], accum=ssq[0:rows, 0:1])
        ts("dve", ssq[0:rows, 1:2], ssq[0:rows, 0:1], 1.0 / D, 1e-6, ALU.mult, ALU.add, ["ssq"], ["ssq"])
        act(ssq[0:rows, 3:4], ssq[0:rows, 1:2], AF.Ln, ["ssq"], ["ssq"])
        act(ssq[0:rows, 2:3], ssq[0:rows, 3:4], AF.Exp, ["ssq"], ["ssq"], scale=-0.5)
        ts("dve", xn_bf[0:rows, :], xt[0:rows, :], ssq[0:rows, 2:3], None, ALU.mult, None, [xk, "ssq"], ["xn"])
        for h in range(2):
            pt, pk = psg.next()
            for kk in range(4):
                kc = h * 4 + kk
                mm(pt[:, kk * rows:(kk + 1) * rows], xn_bf[0:rows, kc * 128:(kc + 1) * 128], ident_bf[0:rows, 0:rows],
                   True, True, ["xn", K(ident_bf)], [pk])
            if dve_evac:
                cp("dve", dst_fn(h * 4, 4), pt[:, 0:4 * rows].rearrange("p (k r) -> p k r", k=4), [pk], [dkey])
            else:
                evac(dst_fn(h * 4, 4), pt[:, 0:4 * rows].rearrange("p (k r) -> p k r", k=4), [pk], [dkey])

    def normA(g, b, pf=False):
        blk = 4 * g + b
        norm_xT(xb[blk * 128:(blk + 1) * 128, :], 128,
                lambda k0, n, b=b: xT_g[:, k0:k0 + n, b * 128:(b + 1) * 128], xgk, dve_evac=pf)

    def normB(m, which, pf=False):
        if which == 0:
            norm_xT(x_own[m * 128:(m + 1) * 128, :], 128, lambda k0, n: xT_o[:, k0:k0 + n, :], "xT_o", dve_evac=pf)
        else:
            norm_xT(x_halo[m * 64:(m + 1) * 64, :], 64, lambda k0, n: xT_h[:, k0:k0 + n, :], "xT_h", dve_evac=pf)

    def phaseA(g, do_norm=True):
        xg = xT_g
        if do_norm:
            for b in range(4):
                normA(g, b)
        if STAGE <= 0.3:
            return
        for c0, kind in [(C_SKV, "ks"), (C_WKV, "kw"), (C_CKV, "ck"), (C_CKV + 128, "cv")]:
            pt, pk = psg.next()
            for kc in range(8):
                mm(pt[:, :], W_bf[:, kc, c0:c0 + 128], xg[:, kc, :], kc == 0, kc == 7, [("W", kc), xgk], [pk])
            if kind == "ks":
                evac(KsT[:, 512 * g:512 * g + 512], pt[:, :], [pk], [("KsT", 4 * g + b) for b in range(4)])
            elif kind == "kw":
                s0 = (4 * g) % 8
                evac(KwT[:, s0:s0 + 4, :], pt[:, :].rearrange("p (a b) -> p a b", a=4), [pk],
                     [("KwT", s0 + b) for b in range(4)])
            else:
                evac(CT[:, 0 if kind == "ck" else 1, 16:528], pt[:, :], [pk], ["CT"])
        if STAGE <= 0.4:
            return
        for b in range(4):
            blk = 4 * g + b
            pt, pk = psg.next()
            for kc in range(8):
                mm(pt[:, 0:128], xg[:, kc, b * 128:(b + 1) * 128], W_bf[:, kc, C_SKV + 128:C_SKV + 256],
                   kc == 0, kc == 7, [("W", kc), xgk], [pk])
                mm(pt[:, 128:256], xg[:, kc, b * 128:(b + 1) * 128], W_bf[:, kc, C_WKV + 128:C_WKV + 256],
                   kc == 0, kc == 7, [("W", kc), xgk], [pk])
            if STAGE <= 0.5:
                continue
            evac(Vs[:, blk, :, 0:64], pt[:, 0:128].rearrange("p (k d) -> p k d", k=2), [pk], [("Vs", blk)])
            if STAGE <= 0.55:
                continue
            evac(Vw[:, blk % 8, :, 0:64], pt[:, 128:256].rearrange("p (k d) -> p k d", k=2), [pk], [("Vw", blk % 8)])
        if STAGE <= 0.6:
            return
        n0 = 1 if g == 0 else 0
        nn = 32 - n0
        pts = [psg.next(), psg.next()]
        for kv in range(2):
            P0 = kv * 64
            pt, pk = pts[kv]
            for j in range(2):
                for i in range(32):
                    rhs = CT[P0:P0 + 64, j, i + 16 * n0:i + 16 * n0 + 16 * (nn - 1) + 1:16]
                    mm(pt[:, j * 32 + n0:j * 32 + 32], W1_2[P0:P0 + 64, j, i, :], rhs,
                       i == 0, i == 31, ["W1", "CT"], [pk])
        if STAGE <= 0.7:
            return
        for kv in range(2):
            pt, pk = pts[kv]
            for j in range(2):
                act(a_act[:, j * 2 + kv, n0:32], pt[:, j * 32 + n0:j * 32 + 32],
                    AF.Silu, [pk, "hbias"], ["a_act"], bias=hbias[:, j:j + 1])
        if STAGE <= 0.8:
            return
        cbase = 32 * g - 1
        for j, dst, dk in [(0, kcT, "kcT"), (1, vcT, "vcT")]:
            pt2, pk2 = psg.next()
            for kv in range(2):
                mm(pt2[:, n0:32], w2p[:, j, kv, :], a_act[:, j * 2 + kv, n0:32], kv == 0, kv == 1,
                   ["w2p", "a_act"], [pk2])
            evac(dst[:, cbase + n0:cbase + 32], pt2[:, n0:32], [pk2], [dk])
        cp("dve", CT[:, :, 0:16], CT[:, :, 512:528], ["CT"], ["CT"])

    def topk(rows, ncol, imp_ap, cand_ap, forced_ap, out_ap, rd, wr):
        R = slice(0, rows)
        tt("dve", candt[R, 0:ncol], imp_ap, cand_ap, ALU.mult, rd, ["candt"])
        S.add("dve", lambda e: e.max(out=mx8[R, 0:8], in_=candt[R, 0:ncol]), ["candt"], ["mx8"])
        S.add("dve", lambda e: e.match_replace(out=work[R, 0:ncol], in_to_replace=mx8[R, 0:8],
                                               in_values=candt[R, 0:ncol], imm_value=0.0), ["candt", "mx8"], ["work"])
        S.add("dve", lambda e: e.max(out=mx8[R, 8:16], in_=work[R, 0:ncol]), ["work"], ["mx8"])
        memset("dve", mx8[R, 13:16], 0.0, ["mx8"])
        S.add("dve", lambda e: e.match_replace(out=work2[R, 0:ncol], in_to_replace=mx8[R, 8:16],
                                               in_values=work[R, 0:ncol], imm_value=0.0), ["work", "mx8"], ["work2"])
        tt("dve", selt[R, 0:ncol], candt[R, 0:ncol], work2[R, 0:ncol], ALU.not_equal, ["candt", "work2"], ["selt"])
        tt("dve", selt[R, 0:ncol], selt[R, 0:ncol], forced_ap, ALU.max, ["selt"] + list(rd), ["selt"])
        ts("dve", out_ap, selt[R, 0:ncol], 1.0, -NEG, ALU.subtract, ALU.mult, ["selt"], wr)

    def tail(rows, xsrc_dram, ypool_ap, ypk, dst_dram):
        R = slice(0, rows)
        ts("dve", rdall[R], Oall[R, :, :, :, 64], 1e-30, None, ALU.add, None, ["Oall"], ["rdall"])
        S.add("dve", lambda e: e.reciprocal(rdall[R], rdall[R]), ["rdall"], ["rdall"])
        tt("dve", wgt[R], rdall[R], gate[R, :].rearrange("p (k g b) -> p k b g", k=2, g=4), ALU.mult,
           ["rdall", "gate"], ["wgt"])
        tt("dve", otmp[R], Oall[R, :, :, :, 0:64], wgt[R].unsqueeze(4).to_broadcast([rows, 2, 3, 4, 64]), ALU.mult,
           ["Oall", "wgt"], ["otmp"])
        ov = osum[R, :].rearrange("p (k g d) -> p k g d", k=2, g=4)
        tt("dve", ov, otmp[R, :, 0], otmp[R, :, 1], ALU.add, ["otmp"], ["osum"])
        tt("dve", ov, ov, otmp[R, :, 2], ALU.add, ["otmp", "osum"], ["osum"])
        tt("dve", mix_bf[R, 0:512], osum[R, :], sza[R, :], ALU.mult, ["osum", "sza"], ["mix"])
        tt("dve", mix_bf[R, 512:1024], ypool_ap, szp[R, :], ALU.mult, [ypk, "szp"], ["mix"])
        for h in range(2):
            pt, pk = psg.next()
            for kk in range(4):
                kc = h * 4 + kk
                mm(pt[:, kk * rows:(kk + 1) * rows], mix_bf[R, kc * 128:(kc + 1) * 128], ident_bf[R, 0:rows],
                   True, True, ["mix", K(ident_bf)], [pk])
            evac(mixT[:, h * 4:h * 4 + 4, 0:rows], pt[:, 0:4 * rows].rearrange("p (k r) -> p k r", k=4), [pk], ["mixT"])
        p0, p0k = psg.next()
        p1, p1k = psg.next()
        for kc in range(8):
            st, sk = xst_r.next()
            dma(st[:, :], w_out[kc * 128:(kc + 1) * 128, :], (), [sk])
            wo, wok = Wo_r.next()
            cp(cast_eng(), wo[:], st[:, :], [sk], [wok])
            mm(p0[R, :], mixT[:, kc, 0:rows], wo[:, 0:512], kc == 0, kc == 7, ["mixT", wok], [p0k])
            mm(p1[R, :], mixT[:, kc, 0:rows], wo[:, 512:1024], kc == 0, kc == 7, ["mixT", wok], [p1k])
        st, sk = xst_r.next()
        dma(st[R, :], xsrc_dram, (), [sk])
        tt("dve", xres[R, 0:512], p0[R, :], st[R, 0:512], ALU.add, [p0k, sk], ["xres"])
        tt("dve", xres[R, 512:1024], p1[R, :], st[R, 512:1024], ALU.add, [p1k, sk], ["xres"])
        memset("dve", ssq[R, 4:5], 0.0, ["ssq2"])
        act(xn_bf[R, :], xres[R, :], AF.Square, ["xres"], ["xn", "ssq2"], accum=ssq[R, 4:5])
        ts("dve", ssq[R, 5:6], ssq[R, 4:5], 1.0 / D, 1e-6, ALU.mult, ALU.add, ["ssq2"], ["ssq2"])
        act(ssq[R, 7:8], ssq[R, 5:6], AF.Ln, ["ssq2"], ["ssq2"])
        act(ssq[R, 6:7], ssq[R, 7:8], AF.Exp, ["ssq2"], ["ssq2"], scale=-0.5)
        stt("dve", xres[R, :], xres[R, :], ssq[R, 6:7], fg_bc[R, :], ALU.mult, ALU.mult,
            ["xres", "ssq2", K(fg_bc)], ["xres"])
        dma(dst_dram, xres[R, :], ["xres"], ["ydram"])

    def tokmajor_proj(rows, xT_ap, xTk, kvdst, kvk):
        R = slice(0, rows)
        for c0, c1 in [(512, 1024), (1024, 1536), (1536, 1816), (2328, 2840)]:
            pt, pk = psg.next()
            for kc in range(8):
                mm(pt[R, 0:c1 - c0], xT_ap(kc), W_bf[:, kc, c0:c1], kc == 0, kc == 7, [("W", kc), xTk], [pk])
            if c0 == 512:
                cp("dve", kvdst[R, 0:512], pt[R, 0:512], [pk], [kvk])
            elif c0 == 1024:
                cp("dve", kvdst[R, 512:768], pt[R, 0:256], [pk], [kvk])
                act(gate[R, :], pt[R, 256:280], AF.Sigmoid, [pk], ["gate"])
                act(sza[R, 0:232], pt[R, 280:512], AF.Silu, [pk], ["sza"])
            elif c0 == 1536:
                act(sza[R, 232:512], pt[R, 0:280], AF.Silu, [pk], ["sza"])
            else:
                act(szp[R, :], pt[R, 0:512], AF.Silu, [pk], ["szp"])

    def qT_proj(xT_ap, xTk, n, dst, dk):
        pt, pk = psg.next()
        for gq in range(4):
            for kc in range(8):
                lhsT = W_bf[:, kc, gq * 128:(gq + 1) * 128]
                mm(pt[:, gq * n:(gq + 1) * n], lhsT, xT_ap(kc), kc == 0, kc == 7, [("W", kc), xTk], [pk])
        evac(dst, pt[:, 0:4 * n].rearrange("p (g n) -> p g n", g=4), [pk], [dk])

    def phaseB(m, do_norm=True, prefetch=()):
        nct = n_ct(m)
        cz, czk = cmpZ_r.next()
        dma(cz[:, 0:nct, :], c_cmpZ[:, CT_BASE[m] * 128:(CT_BASE[m] + nct) * 128].rearrange("p (a b) -> p a b", a=nct),
            (), [czk])
        cd, cdk = cand_r.next()
        dma(cd[:], c_cand[:, m * 128:(m + 1) * 128], (), [cdk])
        fo, fok = forced_r.next()
        dma(fo[:], c_forced[:, m * 128:(m + 1) * 128], (), [fok])
        if do_norm:
            normB(m, 0)
            normB(m, 1)
        tokmajor_proj(128, lambda kc: xT_o[:, kc, :], "xT_o", okv, "okv")
        dma(kv_own[m * 128:(m + 1) * 128, :], okv[:], ["okv"], ["kvdram"])
        qT_proj(lambda kc: xT_o[:, kc, :], "xT_o", 128, qT[:], "qT")
        pt, pk = psg.next()
        for gp in range(4):
            for kc in range(8):
                mm(pt[:, gp * 128:(gp + 1) * 128], W_bf[:, kc, C_U + gp * 128:C_U + (gp + 1) * 128], xT_o[:, kc, :],
                   kc == 0, kc == 7, [("W", kc), "xT_o"], [pk])
        cp("dve", uT[:, :, :, 16:48], pt[:, :].rearrange("p (g a i) -> p g a i", g=4, a=4), [pk], ["uT"])
        pt, pk = psg.next()
        for gp in range(4):
            for kc in range(8):
                mm(pt[:, gp * 64:(gp + 1) * 64], W_bf[:, kc, C_U + gp * 128:C_U + (gp + 1) * 128], xT_h[:, kc, :],
                   kc == 0, kc == 7, [("W", kc), "xT_h"], [pk])
        cp("dve", uT[:, :, :, 0:16], pt[:, 0:256].rearrange("p (g a i) -> p g a i", g=4, a=4), [pk], ["uT"])
        if m == NM - 1:
            dma(pool_lastT.rearrange("p (g i) -> p g i", g=4), uT[:, :, 3, 32:48], ["uT"], ["pldram"])
        if m == 0:
            stc, stck = xst_r.next()
            dma(stc[:, 0:512], c_poolcorr, (), [stck])
        for gp in range(4):
            w = 2 << gp
            cur, ck = uT[:, gp], "uT"
            sh = 1
            lo = 0
            while sh < w:
                nt, nk = uS_r.next()
                lo += sh
                tt("pool", nt[:, :, lo:48], cur[:, :, lo:48], cur[:, :, lo - sh:48 - sh], ALU.add, [ck], [nk])
                cur, ck = nt, nk
                sh *= 2
            if m == 0:
                tt("pool", cur[:, :, 16:48], cur[:, :, 16:48],
                   stc[:, gp * 128:(gp + 1) * 128].rearrange("p (a i) -> p a i", a=4), ALU.mult, [ck, stck], [ck])
            stt("dve", dT[:, gp, :].rearrange("p (a i) -> p a i", a=4), cur[:, :, 16:48], 1.0 / w, uT[:, gp, :, 16:48],
                ALU.mult, ALU.subtract, [ck, "uT"], ["dT"])
        yp, ypk = psg.next()
        for gp in range(4):
            mm(yp[:, gp * 128:(gp + 1) * 128], dT[:, gp, :], pw_bf[:, gp, :], True, True, ["dT", "pw"], [ypk])
        cp("dve", ypool_sb[:, :], yp[:, :], [ypk], ["ypool"])
        for ct in range(nct):
            pt, pk = psg.next()
            mm(pt[:, 0:128], vcT[:, ct * 128:(ct + 1) * 128], ident_bf[:], True, True, ["vcT", K(ident_bf)], [pk])
            evac(vc_ext[:, ct, :, 0:64], pt[:, 0:128].rearrange("p (k d) -> p k d", k=2), [pk], ["vc_ext"])
        cmpS, winS, slcS = [[], []], [[], []], [[], []]
        st_cmp_all = [{}, {}]
        for kv in range(2):
            P0 = kv * 64
            Pq = slice(P0, P0 + 64)
            qall = qT[Pq, :, :].rearrange("p g q -> p (g q)")
            st_cmp, st_win, st_slc = st_cmp_all[kv], {}, {}

            def acc_tile(st):
                if "pa" not in st:
                    st["pa"], st["pak"] = pacc.next()
                    st["pav"] = st["pa"][:, 0:260].rearrange("p (g e) -> p g e", g=4)
                return st["pav"], st["pak"]

            for ct in range(nct):
                def s1(ct=ct, Pq=Pq, qall=qall, st=st_cmp):
                    pt, pk = psg.next()
                    mm(pt[:, :], kcT[Pq, ct * 128:(ct + 1) * 128], qall, True, False, ["kcT", "qT"], [pk])
                    for gq in range(4):
                        mm(pt[:, gq * 128:(gq + 1) * 128], ident_bf[:], cz[:, ct, :], False, True,
                           [K(ident_bf), czk], [pk])
                    pf, pfk = pTf_next()
                    act(pf.rearrange("p g q -> p (g q)"), pt[:, :], AF.Exp, [pk], [pfk], scale=0.125)
                    pb, pbk = pTb_next()
                    cp("dve", pb[:], pf, [pfk], [pbk])
                    st[ct] = (pf, pfk, pb, pbk)

                def s2(ct=ct, kv=kv, st=st_cmp, m=m, acc_tile=acc_tile):
                    pf, pfk, pb, pbk = st[ct]
                    pav, pak = acc_tile(st)
                    if "pimp" not in st:
                        st["pimp"] = pimp_r.next()
                    pimp_t, pimp_k = st["pimp"]
                    for gq in range(4):
                        mm(pav[:, gq, :], pb[:, gq, :], vc_ext[:, ct, kv, 0:65], ct == 0, ct == nct - 1, [pbk, "vc_ext"], [pak])
                    for gq in range(4):
                        mm(pimp_t[:, gq * 128:(gq + 1) * 128], pf[:, gq, :], cover[:, ct, :], ct == 0, ct == nct - 1,
                           [pfk, K(cover)], [pimp_k])
                    if ct == nct - 1:
                        cp("dve", Oall[:, kv, 0], pav, [pak], ["Oall"])
                        ts("dve", rdall[:, kv, 0], Oall[:, kv, 0, :, 64], 1e-30, None, ALU.add, None, ["Oall"], ["rdall"])
                        S.add("dve", lambda e: e.reciprocal(rdall[:, kv, 0], rdall[:, kv, 0]), ["rdall"], ["rdall"])
                        ts("dve", impt[:], pimp_t[:, 0:128], rdall[:, kv, 0, 0:1], None, ALU.mult, None,
                           [pimp_k, "rdall"], ["impt"])
                        for gq in range(1, 4):
                            stt("dve", impt[:], pimp_t[:, gq * 128:(gq + 1) * 128], rdall[:, kv, 0, gq:gq + 1], impt[:],
                                ALU.mult, ALU.add, [pimp_k, "rdall", "impt"], ["impt"])
                        sn, snk = seln_r.next()
                        topk(128, 128, impt[:], cd[:], fo[:], sn[:], ["impt", cdk, fok], [snk])
                        st["sn"] = (sn, snk)
                cmpS[kv].append((s1, s2))
            jlist = [jj for jj in range(8) if 4 * m - 4 + jj >= 0]
            for jj in jlist:
                def s1a(jj=jj, Pq=Pq, qall=qall, st=st_win):
                    kt = 4 * m - 4 + jj
                    pt, pk = psg.next()
                    mm(pt[:, :], KwT[Pq, kt % 8, :], qall, True, False, [("KwT", kt % 8), "qT"], [pk])
                    st[("pt", jj)] = (pt, pk)

                def s1b(jj=jj, st=st_win):
                    pt, pk = st[("pt", jj)]
                    for gq in range(4):
                        mm(pt[:, gq * 128:(gq + 1) * 128], ident_bf[:], winZ[:, jj, :], False, True,
                           [K(ident_bf), K(winZ)], [pk])
                    pb, pbk = pTb_next()
                    act(pb[:].rearrange("p g q -> p (g q)"), pt[:, :], AF.Exp, [pk], [pbk], scale=0.125)
                    st[jj] = (pb, pbk)

                def s2(jj=jj, kv=kv, st=st_win, jlist=jlist, acc_tile=acc_tile):
                    kt = 4 * m - 4 + jj
                    pb, pbk = st[jj]
                    pav, pak = acc_tile(st)
                    for gq in range(4):
                        mm(pav[:, gq, :], pb[:, gq, :], Vw[:, kt % 8, kv, 0:65], jj == jlist[0], jj == jlist[-1],
                           [pbk, ("Vw", kt % 8)], [pak])
                    if jj == jlist[-1]:
                        cp("dve", Oall[:, kv, 2], pav, [pak], ["Oall"])
                winS[kv].append((s1a, s1b, s2))
            nkt = 4 * m + 4
            for kt in range(nkt):
                def s1a(kt=kt, Pq=Pq, qall=qall, st=st_slc, stc=st_cmp):
                    sn, snk = stc["sn"]
                    pt, pk = psg.next()
                    sx, sxk = selx_r.next()
                    cp("dve", sx[:].rearrange("p (b k) -> p b k", b=2),
                       sn[:, 2 * kt:2 * kt + 2].unsqueeze(2).to_broadcast([128, 2, 64]), [snk], [sxk])
                    mm(pt[:, :], KsT[Pq, kt * 128:(kt + 1) * 128], qall, True, False, [("KsT", kt), "qT"], [pk])
                    st[("pt", kt)] = (pt, pk, sx, sxk)

                def s1b(kt=kt, st=st_slc):
                    pt, pk, sx, sxk = st[("pt", kt)]
                    mm(pt[:, :], sx[:], ident4[:], False, kt < 4 * m, [sxk, K(ident4)], [pk])
                    if kt >= 4 * m:
                        for gq in range(4):
                            mm(pt[:, gq * 128:(gq + 1) * 128], ident_bf[:], triZ[:, kt - 4 * m, :], False, True,
                               [K(ident_bf), K(triZ)], [pk])
                    pb, pbk = pTb_next()
                    act(pb[:].rearrange("p g q -> p (g q)"), pt[:, :], AF.Exp, [pk], [pbk], scale=0.125)
                    st[kt] = (pb, pbk)

                def s2(kt=kt, kv=kv, st=st_slc, nkt=nkt, acc_tile=acc_tile):
                    pb, pbk = st[kt]
                    pav, pak = acc_tile(st)
                    for gq in range(4):
                        mm(pav[:, gq, :], pb[:, gq, :], Vs[:, kt, kv, 0:65], kt == 0, kt == nkt - 1, [pbk, ("Vs", kt)], [pak])
                    if kt == nkt - 1:
                        cp("dve", Oall[:, kv, 1], pav, [pak], ["Oall"])
                slcS[kv].append((s1a, s1b, s2))

        def pair(a, b):
            return (lambda: (a[0](), b[0](), a[1](), b[1]()), lambda: (a[2](), b[2]()))

        stages = cmpS[0] + cmpS[1]
        stages += [pair(winS[0][i], winS[1][i]) for i in range(len(winS[0]))]
        stages += [pair(slcS[0][i], slcS[1][i]) for i in range(len(slcS[0]))]
        SK = 1
        npf = len(prefetch)
        inject = {max(0, len(stages) - 3 * (npf - j)): j for j in range(npf)}
        done_pf = set()
        for i in range(len(stages) + SK):
            if i < len(stages):
                stages[i][0]()
            if i - SK >= 0:
                stages[i - SK][1]()
            if i in inject:
                for j in range(npf):
                    if j not in done_pf and inject.get(i) is not None and j <= inject[i]:
                        prefetch[j]()
                        done_pf.add(j)
        for j in range(npf):
            if j not in done_pf:
                prefetch[j]()

    if STAGE == 0:
        S.finalize()
        return _emit(nc, es, S)
    phaseA(0)
    for g in range(NM):
        pf = []
        if g + 1 < NM:
            pf = [lambda b=b, g=g: normA(g + 1, b, True) for b in range(4)] + \
                 [lambda g=g: normB(g + 1, 0, True), lambda g=g: normB(g + 1, 1, True)]
        phaseB(g, do_norm=(g == 0), prefetch=pf)
        if g + 1 < NM:
            phaseA(g + 1, do_norm=False)
        tail(128, x_own[g * 128:(g + 1) * 128, :], ypool_sb[:, :], "ypool", y_own[g * 128:(g + 1) * 128, :])

    if SKIP_SAMPLE:
        S.finalize()
        return _emit(nc, es, S)
    norm_xT(xs, NSAMP, lambda k0, n: xT_o[:, k0:k0 + n, 0:16], "xT_o")
    xTs = lambda kc: xT_o[:, kc, 0:16]
    tokmajor_proj(16, xTs, "xT_o", Ps[:, 0:768], "Ps")
    dma(skv, Ps[:, 0:768], ["Ps"], ["skv_d"])
    qT_proj(xTs, "xT_o", 16, qT_s[:], "qT_s")
    pt, pk = psg.next()
    for kc in range(8):
        mm(pt[0:16, :], xTs(kc), W_bf[:, kc, C_U:C_U + 512], kc == 0, kc == 7, [("W", kc), "xT_o"], [pk])
    cp("dve", Ps[:, 768:1280], pt[0:16, :], [pk], ["Ps_u"])
    pt, pk = psg.next()
    for kc in range(8):
        mm(pt[:, 0:16], W_bf[:, kc, C_SKV:C_SKV + 128], xTs(kc), kc == 0, kc == 7, [("W", kc), "xT_o"], [pk])
    cp("dve", KsTn[:], pt[:, 0:16], [pk], ["KsTn"])
    dma(win_out[:, 0:511 * 256], win_c[:, 256:512 * 256], (), ["win_d"])
    dma(win_out[:, 511 * 256:512 * 256], Ps[:, 512:768], ["Ps"], ["win_d"])
    dma(pool_out[:, 0:14, :], pool_c[:, 1:15, :], (), ["pool_d"])
    dma(pool_out[:, 14, :], Ps[:, 768:1280], ["Ps_u"], ["pool_d"])
    for kv in range(2):
        st, sk = xst_r.next()
        dma(st[0:1, 0:1024].rearrange("o (n d) -> o n d", n=16),
            skv[:, 384 + kv * 64:384 + (kv + 1) * 64].rearrange("(o n) c -> o n c", o=1), ["skv_d"], [sk])
        cp("dve", vnew[:, :, kv, 0:64], st[0:1, 0:1024].rearrange("o (n d) -> o n d", n=16), [sk], ["vnew"])
    for gp in range(4):
        w = 2 << gp
        nr = w - 1
        uu = Ps[:, 768 + gp * 128:768 + (gp + 1) * 128]
        sr = sred[:, gp * 128:(gp + 1) * 128]
        first = True
        for r0 in range(15 - nr, 15, 8):
            r1 = min(r0 + 8, 15)
            st, sk = xst_r.next()
            stv = st[0:16, 0:(r1 - r0) * 128].rearrange("p (r c) -> p r c", c=128)
            dma(stv, pool_c[:, r0:r1, gp * 128:(gp + 1) * 128], (), [sk])
            dstp = sr if first else d_s_f[:, 0:128]
            S.add("dve", lambda e, dstp=dstp, stv=stv: e.tensor_reduce(dstp, stv.rearrange("p r c -> p c r"), AX.X, ALU.add),
                  [sk], ["osum" if first else "candt"])
            if not first:
                tt("dve", sr, sr, d_s_f[:, 0:128], ALU.add, ["osum", "candt"], ["osum"])
            first = False
        tt("dve", sr, sr, uu, ALU.add, ["osum", "Ps_u"], ["osum"])
        stt("dve", d_s[:, gp * 128:(gp + 1) * 128], sr, 1.0 / w, uu, ALU.mult, ALU.subtract, ["osum", "Ps_u"], ["mix"])
    pt, pk = psg.next()
    for gp in range(4):
        mm(pt[:, gp * 16:(gp + 1) * 16], d_s[:, gp * 128:(gp + 1) * 128], ident_bf[0:16, 0:16], True, True,
           ["mix", K(ident_bf)], [pk])
    cp("dve", dT[:, :, 0:16], pt[:, 0:64].rearrange("p (g n) -> p g n", g=4), [pk], ["dT"])
    yps, ypsk = psg.next()
    for gp in range(4):
        mm(yps[0:16, gp * 128:(gp + 1) * 128], dT[:, gp, 0:16], pw_bf[:, gp, :], True, True, ["dT", "pw"], [ypsk])
    cp("dve", ypool_sb[0:16, :], yps[0:16, :], [ypsk], ["ypool"])
    cp("dve", ptx_f[:], ptx_i[:], [K(ptx_i)], ["ptx_f"])
    ts("dve", ptx_f[:], ptx_f[:], 8.0, sub8[:, 0:1], ALU.mult, ALU.add, ["ptx_f", K(sub8)], ["ptx_f"])
    for q4 in range(4):
        ts("dve", idx_f[:], ptx_f[:], 4.0, float(q4), ALU.mult, ALU.add, ["ptx_f"], ["idx_f"])
        cp("dve", idx_i[:, q4, :], idx_f[:], ["idx_f"], ["idx_i"])

    def gather(dst, dk, cache, n, q4):
        S.add("pool", lambda e: e.indirect_dma_start(
            out=dst, out_offset=None, in_=cache,
            in_offset=bass.IndirectOffsetOnAxis(ap=idx_i[:, q4, n:n + 1], axis=0)), ["idx_i"], [dk], dma=True)

    xgkeys = [xgk]

    class GRing:
        def __init__(self):
            self.t = [(xst[0][:, :], ("xst", 0)), (xst[1][:, :], ("xst", 1)), (xres[:, :], "xres"),
                      (otmp[:].rearrange("p k b g d -> p (k b g d)")[:, 0:1024], "otmp")]
            self.i = 0

        def next(self):
            r = self.t[self.i % 4]
            self.i += 1
            return r

    g_r = GRing()
    for n in range(NS_LOOP):
        for q4 in range(4):
            st, sk = g_r.next()
            gather(st, sk, cache_cmp, n, q4)
            cp("act", cmp_bf[:, q4 * 4:(q4 + 1) * 4, :], st.rearrange("p (e c) -> p e c", e=4), [sk], xgkeys)
        for j in range(2):
            for eh in range(4):
                pt, pk = psg.next()
                for ee in range(4):
                    e_ = eh * 4 + ee
                    mm(pt[:, ee * 128:(ee + 1) * 128], cmp_bf[:, e_, j * 128:(j + 1) * 128], ident_bf[:], True, True,
                       xgkeys + [K(ident_bf)], [pk])
                evac(CTs[:, j, eh * 512:(eh + 1) * 512], pt[:, :], [pk], CTs_keys)
        pts = [psg.next(), psg.next()]
        for kv in range(2):
            P0 = kv * 64
            pt, pk = pts[kv]
            for j in range(2):
                for i in range(32):
                    e_, o_ = i % 16, i // 16
                    mm(pt[:, j * 127:(j + 1) * 127], W1_2[P0:P0 + 64, j, i, :],
                       CTs[P0:P0 + 64, j, e_ * 128 + o_:e_ * 128 + o_ + 127], i == 0, i == 31, ["W1"] + CTs_keys, [pk])
        for kv in range(2):
            pt, pk = pts[kv]
            for j in range(2):
                act(a_act[:, j * 2 + kv, 0:127], pt[:, j * 127:(j + 1) * 127],
                    AF.Silu, [pk, "hbias"], ["a_act"], bias=hbias[:, j:j + 1])
        for j, dst, dk in [(0, kcT, "kcT"), (1, vcT, "vcT")]:
            pt2, pk2 = psg.next()
            for kv in range(2):
                mm(pt2[:, 0:127], w2p[:, j, kv, :], a_act[:, j * 2 + kv, 0:127], kv == 0, kv == 1, ["w2p", "a_act"], [pk2])
            evac(dst[:, 0:127], pt2[:, 0:127], [pk2], [dk])
        pt, pk = psg.next()
        mm(pt[0:127, 0:128], vcT[:, 0:127], ident_bf[:], True, True, ["vcT", K(ident_bf)], [pk])
        evac(vc_s[0:127, :, 0:64], pt[0:127, 0:128].rearrange("p (k d) -> p k d", k=2), [pk], ["vc_s"])
        om, omk = Osm_r.next()
        im, imk = impst_r.next()
        for kv in range(2):
            Pq = slice(kv * 64, kv * 64 + 64)
            pt, pk = psg.next()
            mm(pt[0:127, 0:4], kcT[Pq, 0:127], qT_s[Pq, :, n], True, True, ["kcT", "qT_s"], [pk])
            act(pS_f[0:127, :], pt[0:127, 0:4], AF.Exp, [pk], ["pS_f"], scale=0.125)
            cp("dve", pS_b[0:127, 0, :], pS_f[0:127, :], ["pS_f"], ["pS_b"])
            pa, pak = pacc.next()
            mm(pa[0:4, 0:65], pS_b[0:127, 0, :], vc_s[0:127, kv, 0:65], True, True, ["pS_b", "vc_s"], [pak])
            mm(pa[0:4, 128:168], pS_f[0:127, :], cover33[0:127, :], True, True, ["pS_f", K(cover33)], [pak])
            cp("dve", om[:, kv, 0, :], pa[0:4, 0:65], [pak], [omk])
            cp("dve", im[:, kv, :], pa[0:4, 128:168], [pak], [imk])
        dma(scr_i[n].rearrange("k g s -> g k s"), im[:], [imk], ["scr_i"])
        dma(scr_o[n, :, 0].rearrange("k g e -> g k e"), om[:, :, 0, :], [omk], ["scr_o0"])
    dma(impr.rearrange("p k g s -> p (k g s)"), scr_i.rearrange("n k g s -> n (k g s)"), ["scr_i"], ["otmp"])
    dma(Oall[0:16, :, 0].rearrange("p k g e -> p k (g e)"), scr_o[:, :, 0].rearrange("n k g e -> n k (g e)"), ["scr_o0"], ["Oall"])
    for kv in range(2):
        ts("dve", rdall[0:16, kv, 0], Oall[0:16, kv, 0, :, 64], 1e-30, None, ALU.add, None, ["Oall"], ["rdall"])
        S.add("dve", lambda e, kv=kv: e.reciprocal(rdall[0:16, kv, 0], rdall[0:16, kv, 0]), ["rdall"], ["rdall"])
        ts("dve", impt[0:16, 0:40], impr[:, kv, 0, :], rdall[0:16, kv, 0, 0:1], None, ALU.mult, None, ["otmp", "rdall"], ["impt"])
        for gq in range(1, 4):
            stt("dve", impt[0:16, 0:40], impr[:, kv, gq, :], rdall[0:16, kv, 0, gq:gq + 1], impt[0:16, 0:40],
                ALU.mult, ALU.add, ["otmp", "rdall", "impt"], ["impt"])
        topk(16, 40, impt[0:16, 0:40], cand_s[:, :], forced_s[:, :], seln_s[:, kv, :],
             ["impt", K(cand_s), K(forced_s)], ["seln_s"])
    memset("dve", selx_s[:], 0.0, ["selx_s"])
    for kv in range(2):
        cp("dve", selx_s[0:16, kv, :].rearrange("p (s k) -> p s k", k=4),
           seln_s[:, kv, 0:32].unsqueeze(2).to_broadcast([16, 32, 4]), ["seln_s"], ["selx_s"])
    for n in range(NS_LOOP):
        for q4 in range(4):
            st, sk = g_r.next()
            gather(st, sk, cache_slc, n, q4)
            stv = st.rearrange("p (e c) -> p e c", e=4)
            cp("act", slck_bf[:, q4 * 4:(q4 + 1) * 4, :], stv[:, :, 0:128], [sk], xgkeys)
            cp("dve", Vs[:, q4 * 4:(q4 + 1) * 4, :, 0:64], stv[:, :, 128:256].rearrange("p e (k d) -> p e k d", k=2), [sk],
               [("Vs", i) for i in range(q4 * 4, q4 * 4 + 4)])
        for eh in range(4):
            pt, pk = psg.next()
            for ee in range(4):
                mm(pt[:, ee * 128:(ee + 1) * 128], slck_bf[:, eh * 4 + ee, :], ident_bf[:], True, True,
                   xgkeys + [K(ident_bf)], [pk])
            evac(KsT[:, eh * 512:(eh + 1) * 512], pt[:, :], [pk], [("KsT", i) for i in range(4 * eh, 4 * eh + 4)])
        st, sk = xst_r.next()
        wkv_ = st[:, :].rearrange("p (t c) -> p t c", t=4)
        dma(wkv_, win_out[n].rearrange("(t p c) -> p t c", t=4, p=128), ["win_d"], [sk])
        cp("act", cmp_bf[:, 0:4, 0:128], wkv_[:, :, 0:128], [sk], xgkeys)
        cp("dve", Vw[:, 0:4, :, 0:64], wkv_[:, :, 128:256].rearrange("p t (k d) -> p t k d", k=2), [sk],
           [("Vw", i) for i in range(4)])
        pt, pk = psg.next()
        for t_ in range(4):
            mm(pt[:, t_ * 128:(t_ + 1) * 128], cmp_bf[:, t_, 0:128], ident_bf[:], True, True, xgkeys + [K(ident_bf)], [pk])
        evac(KwT[:, 0:4, :], pt[:, :].rearrange("p (a b) -> p a b", a=4), [pk], [("KwT", i) for i in range(4)])
        om, omk = Osm_r.next()
        for kv in range(2):
            Pq = slice(kv * 64, kv * 64 + 64)
            qn = qT_s[Pq, :, n]
            pt, pk = psg.next()
            for e_ in range(16):
                mm(pt[:, e_ * 4:(e_ + 1) * 4], KsT[Pq, e_ * 128:(e_ + 1) * 128], qn, True, False, [("KsT", e_), "qT_s"], [pk])
                mm(pt[:, e_ * 4:(e_ + 1) * 4], selx_s[:, kv, :], e16[:, n, :], False, True, ["selx_s", K(e16)], [pk])
            act(pS_b[:, 0:16, :].rearrange("p e g -> p (e g)"), pt[:, 0:64], AF.Exp, [pk], ["pS_b"], scale=0.125)
            pt2, pk2 = psg.next()
            mm(pt2[0:1, 0:4], KsTn[Pq, n:n + 1], qn, True, True, ["KsTn", "qT_s"], [pk2])
            act(pN_b[:, :], pt2[0:1, 0:4], AF.Exp, [pk2], ["pN_b"], scale=0.125)
            pa, pak = pacc.next()
            for e_ in range(16):
                mm(pa[0:4, 0:65], pS_b[:, e_, :], Vs[:, e_, kv, 0:65], e_ == 0, False, ["pS_b", ("Vs", e_)], [pak])
            mm(pa[0:4, 0:65], pN_b[:, :], vnew[0:1, n, kv, 0:65], False, True, ["pN_b", "vnew"], [pak])
            cp("dve", om[:, kv, 1, :], pa[0:4, 0:65], [pak], [omk])
            pt, pk = psg.next()
            for t_ in range(4):
                mm(pt[:, t_ * 4:(t_ + 1) * 4], KwT[Pq, t_, :], qn, True, True, [("KwT", t_), "qT_s"], [pk])
            act(pS_b[:, 0:4, :].rearrange("p e g -> p (e g)"), pt[:, 0:16], AF.Exp, [pk], ["pS_b"], scale=0.125)
            pa, pak = pacc.next()
            for t_ in range(4):
                mm(pa[0:4, 0:65], pS_b[:, t_, :], Vw[:, t_, kv, 0:65], t_ == 0, t_ == 3, ["pS_b", ("Vw", t_)], [pak])
            cp("dve", om[:, kv, 2, :], pa[0:4, 0:65], [pak], [omk])
        for kv in range(2):
            dma(scr_o[n, kv, 1:3].rearrange("b g e -> g b e"), om[:, kv, 1:3, :], [omk], ["scr_o1"])
    dma(Oall[0:16, :, 1:3].rearrange("p k b g e -> p k (b g e)"), scr_o[:, :, 1:3].rearrange("n k b g e -> n k (b g e)"), ["scr_o1"], ["Oall"])
    tail(16, xs, ypool_sb[0:16, :], "ypool", ys)

    S.finalize()
    return _emit(nc, es, S)


def _emit(nc, es, S):
    sems = {}
    for e in S.ENG:
        sems[e] = es.enter_context(nc.semaphore(f"s_{e}"))
    dsems = {}
    for e in S.ENG:
        if any(op.dma for op in S.ops[e]):
            dsems[e] = [es.enter_context(nc.semaphore(f"d_{e}{i}")) for i in range(S.R)]
    block = es.enter_context(nc.Block())

    def emit(eng_name):
        def body(e):
            waited = {}

            def wait(sem, val):
                if waited.get(sem.name, 0) >= val:
                    return
                waited[sem.name] = val
                e.wait_ge(sem, val)

            for op in S.ops[eng_name]:
                for d in op.deps:
                    if d.dma:
                        wait(dsems[d.eng][d.sem], d.cnt)
                    else:
                        wait(sems[d.eng], d.cnt)
                if op.dma:
                    if op.guard:
                        wait(dsems[eng_name][op.sem], op.guard)
                    op.fn(e).then_inc(dsems[eng_name][op.sem], 16)
                else:
                    if DEBUG_NAMES is not None:
                        DEBUG_NAMES[nc.get_next_instruction_name()] = (eng_name, op.idx, op.src)
                    ins = op.fn(e)
                    if op.signal:
                        ins.then_inc(sems[eng_name], 1)
            last = {}
            for op in S.ops[eng_name]:
                if op.dma:
                    last[op.sem] = op.cnt
            for s_, c_ in last.items():
                wait(dsems[eng_name][s_], c_)
        return body

    block.tensor(emit("pe"))
    block.scalar(emit("act"))
    block.vector(emit("dve"))
    block.gpsimd(emit("pool"))
    block.sync(emit("sp"))
    es.close()
    return nc


def _consts(r):
    bf = ml_dtypes.bfloat16
    c = {}
    c["c_ident_bf"] = np.eye(128, dtype=np.float32).astype(bf)
    c["c_ident_f"] = np.eye(128, dtype=np.float32)
    c["c_ident4"] = np.tile(np.eye(128, dtype=np.float32), (1, 4)).astype(bf)
    qp = np.arange(128)
    a, i = qp // 32, qp % 32
    trel = a * 128 + 32 * r + i
    kl = np.arange(128)[:, None]
    tri = np.zeros((128, 4, 128), np.float32)
    for j in range(4):
        tri[:, j, :] = np.where(j * 128 + kl <= trel[None, :], 0.0, NEG)
    c["c_triZ"] = tri.reshape(128, -1).astype(bf)
    wz = np.zeros((128, 8, 128), np.float32)
    for jj in range(8):
        kp = (jj - 4) * 128 + kl
        ok = (kp <= trel[None, :]) & (kp > trel[None, :] - 512)
        wz[:, jj, :] = np.where(ok, 0.0, NEG)
    c["c_winZ"] = wz.reshape(128, -1).astype(bf)
    cz = np.zeros((128, N_CTT, 128), np.float32)
    cand = np.zeros((128, NM, 128), np.float32)
    forced = np.zeros((128, NM, 128), np.float32)
    sidx = np.arange(128)[None, :]
    for m in range(NM):
        t = 4 * m * 128 + trel
        for ct in range(n_ct(m)):
            cc = 128 * ct + kl
            ok = (16 * cc + 31 <= t[None, :]) & (cc <= 510)
            cz[:, CT_BASE[m] + ct, :] = np.where(ok, 0.0, NEG)
        cur = (t // 64)[:, None]
        valid = 64 * sidx <= t[:, None]
        fo = valid & ((sidx == 0) | (sidx == cur) | (sidx == cur - 1))
        forced[:, m, :] = fo
        cand[:, m, :] = valid & ~fo
    c["c_cmpZ"] = cz.reshape(128, -1).astype(bf)
    c["c_cand"] = cand.reshape(128, -1).astype(bf)
    c["c_forced"] = forced.reshape(128, -1).astype(bf)
    cov = np.zeros((128, 4, 128), np.float32)
    for ct in range(4):
        cc = 128 * ct + np.arange(128)[:, None]
        cov[:, ct, :] = (cc <= 510) & (cc >= 4 * sidx - 1) & (cc <= 4 * sidx + 3)
    c["c_cover"] = cov.reshape(128, -1)
    pc = np.ones((128, 4, 128), np.float32)
    for gp in range(4):
        w = 2 << gp
        pc[:, gp, :] = (w / np.minimum(w, trel + 1))[None, :]
    c["c_poolcorr"] = pc.reshape(128, -1)
    c33 = np.zeros((128, 40), np.float32)
    cc = np.arange(128)[:, None]
    s40 = np.arange(40)[None, :]
    c33[:] = (cc <= 126) & (cc >= 4 * s40 - 1) & (cc <= 4 * s40 + 3) & (s40 <= 32)
    c["c_cover33"] = c33
    cs = np.zeros((16, 40), np.float32)
    cs[:, 1:31] = 1.0
    fs = np.zeros((16, 40), np.float32)
    fs[:, [0, 31, 32]] = 1.0
    c["c_cand_s"], c["c_forced_s"] = cs, fs
    e = np.zeros((128, 16, 4), np.float32)
    for k in range(16):
        e[k, k, :] = 1.0
    c["c_e16"] = e.reshape(128, 64).astype(bf)
    c["c_sub8"] = (np.arange(128) % 8).astype(np.float32).reshape(128, 1)
    return c


_NC = None


def _prep(x_prompt, x_sample, cache_cmp_kv, cache_slc_kv, cache_win_kv, state_pool, page_table,
          norm_g, w_in, cmp_pe, cmp_w1, cmp_w2, pool_w, pool_scale, w_out, final_g, cores=range(8)):
    f = lambda a: np.ascontiguousarray(np.asarray(a, dtype=np.float32))
    x_prompt = f(x_prompt)
    nblk = x_prompt.shape[1] // 128
    cc = f(cache_cmp_kv).reshape(-1, 1024)
    cs = f(cache_slc_kv).reshape(-1, 1024)
    cw = f(cache_win_kv).reshape(128, 512 * 256)
    sp = f(state_pool).reshape(128, 15, 512)
    pt = np.asarray(page_table, dtype=np.int32)
    xsamp = f(x_sample).reshape(128, D)
    shared = dict(cache_cmp=cc, cache_slc=cs, norm_g=f(norm_g).reshape(1, D), w_in=f(w_in).reshape(D, PT),
                  cmp_pe=f(cmp_pe).reshape(64, 64), cmp_w1=f(cmp_w1).reshape(2, 32, 64, 128),
                  cmp_w2=f(cmp_w2).reshape(2, 128, 64), pool_w=f(pool_w).reshape(4, 128, 128),
                  pool_scale=f(pool_scale).reshape(1, 512), w_out=f(w_out).reshape(D, D),
                  final_g=f(final_g).reshape(1, D))
    in_maps = []
    own_idx = {}
    for c in cores:
        b, r = c // 4, c % 4
        xbat = x_prompt[b]
        tok = (np.arange(nblk)[:, None] * 128 + 32 * r + np.arange(32)[None, :]).reshape(-1)
        own_idx[c] = tok
        hal = (np.arange(nblk)[:, None] * 128 + 32 * r - 16 + np.arange(16)[None, :]).reshape(-1)
        xh = np.where((hal >= 0)[:, None], xbat[np.maximum(hal, 0)], 0.0).astype(np.float32)
        sl = slice(16 * c, 16 * c + 16)
        ptx = np.ascontiguousarray(np.repeat(pt[sl].T, 8, axis=0)).astype(np.int32)
        d = dict(shared)
        d.update(xb=xbat, x_own=np.ascontiguousarray(xbat[tok]), x_halo=np.ascontiguousarray(xh),
                 xs=np.ascontiguousarray(xsamp[sl]), win_c=np.ascontiguousarray(cw[sl]),
                 pool_c=np.ascontiguousarray(sp[sl]), ptx=ptx)
        d.update(_consts(r))
        in_maps.append(d)
    return in_maps, own_idx


def kernel(x_prompt, x_sample, cache_cmp_kv, cache_slc_kv, cache_win_kv, state_pool, page_table,
           norm_g, w_in, cmp_pe, cmp_w1, cmp_w2, pool_w, pool_scale, w_out, final_g):
    global _NC
    if _NC is None:
        _NC = build_program()
    in_maps, own_idx = _prep(x_prompt, x_sample, cache_cmp_kv, cache_slc_kv, cache_win_kv, state_pool, page_table,
                             norm_g, w_in, cmp_pe, cmp_w1, cmp_w2, pool_w, pool_scale, w_out, final_g)
    res = run_bass_kernel_spmd(_NC, in_maps, core_ids=list(range(8))).results
    y_p = np.zeros((2, 8192, D), np.float32)
    kvp = np.zeros((2, 8192, 768), np.float32)
    pool_p = np.zeros((1, 2, 15, 512), np.float32)
    for c in range(8):
        b, r = c // 4, c % 4
        y_p[b, own_idx[c]] = res[c]["y_own"]
        kvp[b, own_idx[c]] = res[c]["kv_own"]
        if r == 3:
            pool_p[0, b] = res[c]["pool_lastT"].reshape(128, 4, 16).transpose(1, 0, 2).reshape(512, 16).T[1:16]
    y_s = np.concatenate([res[c]["ys"] for c in range(8)], 0).reshape(128, 1, D)
    skv = np.concatenate([res[c]["skv"] for c in range(8)], 0)
    win_s = np.concatenate([res[c]["win_out"] for c in range(8)], 0).reshape(1, 128, 512, 2, 2, 64)
    pool_s = np.concatenate([res[c]["pool_out"] for c in range(8)], 0).reshape(1, 128, 15, 512)
    kv6 = lambda a: np.ascontiguousarray(a)
    return (y_p, y_s,
            kv6(kvp[:, :, 0:256]).reshape(1, 2, 8192, 2, 2, 64),
            kv6(kvp[:, :, 256:512]).reshape(1, 2, 8192, 2, 2, 64),
            kv6(kvp[:, 8192 - 512:, 512:768]).reshape(1, 2, 512, 2, 2, 64),
            pool_p,
            kv6(skv[:, 0:256]).reshape(1, 128, 1, 2, 2, 64),
            kv6(skv[:, 256:512]).reshape(1, 128, 1, 2, 2, 64),
            win_s, pool_s)
```

```python
from contextlib import ExitStack
import numpy as np
import ml_dtypes
import concourse.bass as bass
import concourse.mybir as mybir
from concourse.bass_utils import run_bass_kernel_spmd

F32 = mybir.dt.float32
BF16 = mybir.dt.bfloat16
I32 = mybir.dt.int32
AF = mybir.ActivationFunctionType
ALU = mybir.AluOpType
AX = mybir.AxisListType

NEG = -30000.0
DEBUG_NAMES = None
MMINFO = {}
D = 1024
PT = 2840
NM = 16
NSAMP = 16
NPHYS = 2560
NS_LOOP = 16
SKIP_SAMPLE = False
STAGE = 99
C_Q, C_CKV, C_SKV, C_WKV, C_GL, C_ZA, C_U, C_ZP = 0, 512, 768, 1024, 1280, 1304, 1816, 2328


def n_ct(m):
    return (32 * m + 30) // 128 + 1


CT_BASE = [sum(n_ct(i) for i in range(m)) for m in range(NM)]
N_CTT = sum(n_ct(i) for i in range(NM))


def cfg(nm, nsamp, nphys):
    global NM, NSAMP, NPHYS, CT_BASE, N_CTT, _NC
    NM, NSAMP, NPHYS = nm, nsamp, nphys
    CT_BASE = [sum(n_ct(i) for i in range(m)) for m in range(NM)]
    N_CTT = sum(n_ct(i) for i in range(NM))
    _NC = None


class Op:
    __slots__ = ("eng", "fn", "deps", "dma", "signal", "cnt", "sem", "idx", "guard", "src")


class Sched:
    ENG = ["pe", "act", "dve", "pool", "sp"]
    R = 14

    def __init__(self):
        self.ops = {e: [] for e in self.ENG}
        self.lastw = {}
        self.readers = {}

    def add(self, eng, fn, reads=(), writes=(), dma=False):
        op = Op()
        op.eng, op.fn, op.dma, op.signal, op.deps = eng, fn, dma, False, []
        op.cnt = op.sem = op.guard = None
        seen = set()

        def dep(o):
            if o is None or id(o) in seen:
                return
            seen.add(id(o))
            op.deps.append(o)

        for k in reads:
            dep(self.lastw.get(k))
            if isinstance(k, tuple) and k[0] in ("psg", "pacc", "pimp"):
                for r in self.readers.get(k, ()):
                    if r.eng != eng:
                        dep(r)
        for k in writes:
            dep(self.lastw.get(k))
            for r in self.readers.get(k, ()):
                dep(r)
        for k in reads:
            self.readers.setdefault(k, []).append(op)
        for k in writes:
            self.lastw[k] = op
            self.readers[k] = []
        op.idx = len(self.ops[eng])
        self.ops[eng].append(op)
        import sys
        fr = sys._getframe(1)
        chain = []
        while fr is not None and len(chain) < 4:
            chain.append(fr.f_lineno)
            fr = fr.f_back
        op.src = chain
        return op

    def finalize(self):
        for e in self.ENG:
            for op in self.ops[e]:
                best = {}
                dmas = []
                for d in op.deps:
                    if d.dma:
                        dmas.append(d)
                    else:
                        if d.eng == "pe" and op.eng == "pe" and not op.dma:
                            continue
                        if d.eng not in best or best[d.eng].idx < d.idx:
                            best[d.eng] = d
                op.deps = dmas + list(best.values())
                for d in op.deps:
                    d.signal = True
        for e in self.ENG:
            c = 0
            nd = 0
            for op in self.ops[e]:
                if op.dma:
                    op.sem = nd % self.R
                    op.cnt = 16 * (nd // self.R + 1)
                    op.guard = 16 * (nd // self.R)
                    nd += 1
                elif op.signal:
                    c += 1
                    op.cnt = c


def build_program():
    nc = bass.Bass("TRN2", target_bir_lowering=False)
    es = ExitStack()
    S = Sched()

    def din(name, shape, dt=F32):
        return nc.dram_tensor(name, list(shape), dt, kind="ExternalInput").ap()

    def dout(name, shape, dt=F32):
        return nc.dram_tensor(name, list(shape), dt, kind="ExternalOutput").ap()

    def sb(name, shape, dt):
        return es.enter_context(nc.sbuf_tensor(name, list(shape), dt))

    xb = din("xb", [NM * 512, D])
    x_own = din("x_own", [NM * 128, D])
    x_halo = din("x_halo", [NM * 64, D])
    xs = din("xs", [NSAMP, D])
    cache_cmp = din("cache_cmp", [NPHYS * 32, 1024])
    cache_slc = din("cache_slc", [NPHYS * 32, 1024])
    win_c = din("win_c", [NSAMP, 512 * 256])
    pool_c = din("pool_c", [NSAMP, 15, 512])
    ptx = din("ptx", [128, NSAMP], I32)
    norm_g = din("norm_g", [1, D])
    w_in = din("w_in", [D, PT])
    cmp_pe = din("cmp_pe", [64, 64])
    cmp_w1 = din("cmp_w1", [2, 32, 64, 128])
    cmp_w2 = din("cmp_w2", [2, 128, 64])
    pool_w = din("pool_w", [4, 128, 128])
    pool_scale = din("pool_scale", [1, 512])
    w_out = din("w_out", [D, D])
    final_g = din("final_g", [1, D])
    c_ident_bf = din("c_ident_bf", [128, 128], BF16)
    c_ident_f = din("c_ident_f", [128, 128])
    c_ident4 = din("c_ident4", [128, 512], BF16)
    c_triZ = din("c_triZ", [128, 4 * 128], BF16)
    c_winZ = din("c_winZ", [128, 8 * 128], BF16)
    c_cmpZ = din("c_cmpZ", [128, N_CTT * 128], BF16)
    c_cover = din("c_cover", [128, 4 * 128])
    c_cand = din("c_cand", [128, NM * 128], BF16)
    c_forced = din("c_forced", [128, NM * 128], BF16)
    c_poolcorr = din("c_poolcorr", [128, 4 * 128])
    c_cover33 = din("c_cover33", [128, 40])
    c_cand_s = din("c_cand_s", [16, 40])
    c_forced_s = din("c_forced_s", [16, 40])
    c_e16 = din("c_e16", [128, 64], BF16)
    c_sub8 = din("c_sub8", [128, 1])

    y_own = dout("y_own", [NM * 128, D])
    kv_own = dout("kv_own", [NM * 128, 768])
    pool_lastT = dout("pool_lastT", [128, 64])
    ys = dout("ys", [NSAMP, D])
    skv = dout("skv", [NSAMP, 768])
    win_out = dout("win_out", [NSAMP, 512 * 256])
    pool_out = dout("pool_out", [NSAMP, 15, 512])
    scr_o = nc.dram_tensor("scr_o", [NSAMP, 2, 3, 4, 65], F32, kind="Internal").ap()
    scr_i = nc.dram_tensor("scr_i", [NSAMP, 2, 4, 40], F32, kind="Internal").ap()

    W_bf = sb("W_bf", [128, 8, PT], BF16)
    W1_2 = sb("W1_2", [128, 2, 32, 128], BF16)
    w2p = sb("w2p", [128, 2, 2, 128], BF16)
    pw_bf = sb("pw_bf", [128, 4, 128], BF16)
    hbias = sb("hbias", [128, 2], F32)
    peT = sb("peT", [128, 64], BF16)
    g_col = sb("g_col", [128, 8], F32)
    fg_bc = sb("fg_bc", [128, D], F32)
    ident_bf = sb("ident_bf", [128, 128], BF16)
    ident_f = sb("ident_f", [128, 128], F32)
    ident4 = sb("ident4", [128, 512], BF16)
    triZ = sb("triZ", [128, 4, 128], BF16)
    winZ = sb("winZ", [128, 8, 128], BF16)
    cover = sb("cover", [128, 4, 128], F32)
    cover33 = sb("cover33", [128, 40], F32)
    cand_s = sb("cand_s", [16, 40], F32)
    forced_s = sb("forced_s", [16, 40], F32)
    e16 = sb("e16", [128, 16, 4], BF16)
    sub8 = sb("sub8", [128, 1], F32)
    cmpZ_t = [sb(f"cmpZ{i}", [128, 4, 128], BF16) for i in range(2)]
    cand_t = [sb(f"cand{i}", [128, 128], BF16) for i in range(2)]
    forced_t = [sb(f"forced{i}", [128, 128], BF16) for i in range(2)]
    Wo_t = [sb(f"Wo{i}", [128, D], BF16) for i in range(2)]

    KsT = sb("KsT", [128, 8192], BF16)
    Vs = sb("Vs", [128, 64, 2, 66], BF16)
    KwT = sb("KwT", [128, 8, 128], BF16)
    Vw = sb("Vw", [128, 8, 2, 66], BF16)
    CT = sb("CT", [128, 2, 528], BF16)
    kcT = sb("kcT", [128, 512], BF16)
    vcT = sb("vcT", [128, 512], BF16)
    vc_ext = sb("vc_ext", [128, 4, 2, 66], BF16)
    xst = [sb(f"xst{i}", [128, D], F32) for i in range(2)]
    xn_bf = sb("xn_bf", [128, D], BF16)
    ssq = sb("ssq", [128, 8], F32)
    xT_g = sb("xT_g", [128, 8, 512], BF16)
    xT_o = sb("xT_o", [128, 8, 128], BF16)
    xT_h = sb("xT_h", [128, 8, 64], BF16)
    a_act = sb("a_act", [128, 4, 128], BF16)
    qT = sb("qT", [128, 4, 128], BF16)
    uT = sb("uT", [128, 4, 4, 48], F32)
    uS = [sb(f"uS{i}", [128, 4, 48], F32) for i in range(2)]
    dT = sb("dT", [128, 4, 128], BF16)
    okv = sb("okv", [128, 768], F32)
    gate = sb("gate", [128, 24], F32)
    sza = sb("sza", [128, 512], F32)
    szp = sb("szp", [128, 512], F32)
    pT_f = [sb(f"pT_f{i}", [128, 4, 128], F32) for i in range(1)]
    pT_b = [sb(f"pT_b{i}", [128, 4, 128], BF16) for i in range(3)]
    Oall = sb("Oall", [128, 2, 3, 4, 65], F32)
    rdall = sb("rdall", [128, 2, 3, 4], F32)
    wgt = sb("wgt", [128, 2, 3, 4], F32)
    otmp = sb("otmp", [128, 2, 3, 4, 64], F32)
    osum = sb("osum", [128, 512], F32)
    mix_bf = sb("mix_bf", [128, D], BF16)
    mixT = sb("mixT", [128, 8, 128], BF16)
    xres = sb("xres", [128, D], F32)
    ypool_sb = sb("ypool_sb", [128, 512], F32)
    impt = sb("impt", [128, 128], F32)
    candt = sb("candt", [128, 128], F32)
    work = sb("work", [128, 128], F32)
    work2 = sb("work2", [128, 128], F32)
    mx8 = sb("mx8", [128, 16], F32)
    selt = sb("selt", [128, 128], F32)
    selneg = [sb(f"selneg{i}", [128, 128], BF16) for i in range(2)]
    selx_t = [sb(f"selx{i}", [128, 128], BF16) for i in range(4)]
    selx_s = sb("selx_s", [128, 2, 128], BF16)
    ptx_i = sb("ptx_i", [128, NSAMP], I32)
    ptx_f = sb("ptx_f", [128, NSAMP], F32)
    idx_i = sb("idx_i", [128, 4, NSAMP], I32)
    idx_f = sb("idx_f", [128, NSAMP], F32)
    Ps = sb("Ps", [16, 1280], F32)
    qT_s = sb("qT_s", [128, 4, 16], BF16)
    KsTn = sb("KsTn", [128, 16], BF16)
    vnew = sb("vnew", [1, NSAMP, 2, 66], BF16)
    vc_s = sb("vc_s", [128, 2, 66], BF16)
    pS_f = sb("pS_f", [128, 4], F32)
    pS_b = sb("pS_b", [128, 17, 4], BF16)
    pN_b = sb("pN_b", [1, 4], BF16)
    Osm = [sb(f"Osm{i}", [4, 2, 3, 65], F32) for i in range(2)]
    impst = [sb(f"impst{i}", [4, 2, 40], F32) for i in range(2)]
    impr = otmp[0:16, 0].rearrange("p b g d -> p (b g d)")[:, 0:320].rearrange("p (k g s) -> p k g s", k=2, g=4)
    seln_s = sb("seln_s", [16, 2, 40], BF16)
    sred = osum[0:16, :]
    d_s = mix_bf[0:16, 0:512]
    d_s_f = candt[0:16, 0:128]
    CTs = KsT[:, 4096:8192].rearrange("p (j t) -> p j t", j=2)
    CTs_keys = [("KsT", i) for i in range(32, 64)]
    cmp_bf = xT_g[:].rearrange("p k t -> p (k t)").rearrange("p (e c) -> p e c", e=16)
    slck_bf = xT_g[:, 4:8, :].rearrange("p k t -> p (k t)").rearrange("p (e c) -> p e c", e=16)

    PSUM = [es.enter_context(nc.psum_tensor(f"ps{i}", [128, 512], F32)) for i in range(8)]

    fresh = {}

    class Ring:
        def __init__(self, name, tiles):
            self.name, self.tiles, self.i = name, tiles, 0

        def next(self):
            k = self.i % len(self.tiles)
            self.i += 1
            if self.name in ("psg", "pacc", "pimp"):
                fresh[self.tiles[k].name] = True
            return self.tiles[k], (self.name, k)

    psg = Ring("psg", PSUM[0:4])
    pacc = Ring("pacc", PSUM[4:7])
    pimp_r = Ring("pimp", PSUM[7:8])
    xst_r = Ring("xst", xst)
    pTf_r = Ring("pTf", pT_f)
    pTb_r = Ring("pTb", pT_b)
    pTf_state = [0]

    def pTf_next():
        k = pTf_state[0] % 2
        pTf_state[0] += 1
        if k == 0:
            return pT_f[0][:], ("pTf", 0)
        return okv[:, 0:512].rearrange("p (g q) -> p g q", g=4), "okv"

    pTb_state = [0]

    def pTb_next():
        k = pTb_state[0] % 4
        pTb_state[0] += 1
        if k < 3:
            return pT_b[k], ("pTb", k)
        return a_act, "a_act"
    seln_r = Ring("seln", selneg)
    selx_r = Ring("selx", selx_t)
    uS_r = Ring("uS", uS)
    cmpZ_r = Ring("cmpZ", cmpZ_t)
    cand_r = Ring("cand", cand_t)
    forced_r = Ring("forced", forced_t)
    Wo_r = Ring("Wo", Wo_t)
    Osm_r = Ring("Osm", Osm)
    impst_r = Ring("impst", impst)
    xgk = "xT_g"

    def mm(out, lhsT, rhs, start, stop, rd, wr):
        nm_ = out.tensor.name
        st_ = fresh.get(nm_, True)
        fresh[nm_] = False
        op = S.add("pe", lambda e: e.matmul(out, lhsT, rhs, start=st_, stop=stop, skip_group_check=True), rd, wr)
        bp = lhsT.base_partition()
        MMINFO[id(op)] = (bp, lhsT.shape[0], nm_)

    def act(out, in_, func, rd, wr, scale=1.0, bias=0.0, accum=None):
        if accum is None:
            S.add("act", lambda e: e.activation(out, in_, func, bias=bias, scale=scale), rd, wr)
        else:
            S.add("act", lambda e: e.activation(out, in_, func, bias=bias, scale=scale, accum_out=accum), rd, wr)

    def cp(eng, out, in_, rd, wr):
        if eng == "act":
            S.add("act", lambda e: e.copy(out, in_), rd, wr)
        else:
            S.add(eng, lambda e: e.tensor_copy(out, in_), rd, wr)

    def tt(eng, out, a, b, op, rd, wr):
        S.add(eng, lambda e: e.tensor_tensor(out, a, b, op), rd, wr)

    def ts(eng, out, a, s1, s2, op0, op1, rd, wr):
        if eng == "act":
            assert s2 is None and op0 == ALU.mult
            S.add("act", lambda e: e.activation(out, a, AF.Copy, scale=s1), rd, wr)
            return
        if s2 is None:
            S.add(eng, lambda e: e.tensor_scalar(out, a, s1, None, op0), rd, wr)
        else:
            S.add(eng, lambda e: e.tensor_scalar(out, a, s1, s2, op0, op1), rd, wr)

    def stt(eng, out, a, sc, b, op0, op1, rd, wr):
        S.add(eng, lambda e: e.scalar_tensor_tensor(out, a, sc, b, op0, op1), rd, wr)

    def memset(eng, ap, v, wr):
        S.add(eng, lambda e: e.memset(ap, v), (), wr)

    def dma(out, in_, rd, wr, q="sp", slow=False):
        if slow:
            S.add(q, lambda e: e.dma_start(out=out, in_=in_, allow_slow_non_contiguous=True), rd, wr, dma=True)
        else:
            S.add(q, lambda e: e.dma_start(out=out, in_=in_), rd, wr, dma=True)

    alt = [0]

    def evac(out, in_, rd, wr):
        alt[0] ^= 1
        cp("act" if alt[0] else "dve", out, in_, rd, wr)

    K = lambda t: t.name

    for t, src, rs, kw in [
        (ident_bf, c_ident_bf, None, {}), (ident_f, c_ident_f, None, {}), (ident4, c_ident4, None, {}),
        (triZ, c_triZ, "p (a b) -> p a b", dict(a=4)), (winZ, c_winZ, "p (a b) -> p a b", dict(a=8)),
        (cover, c_cover, "p (a b) -> p a b", dict(a=4)), (cover33, c_cover33, None, {}),
        (cand_s, c_cand_s, None, {}), (forced_s, c_forced_s, None, {}),
        (e16, c_e16, "p (a b) -> p a b", dict(a=16)), (sub8, c_sub8, None, {}), (ptx_i, ptx, None, {}),
    ]:
        s_ap = src.rearrange(rs, **kw) if rs else src
        dma(t[:], s_ap, (), [K(t)])
    dma(fg_bc[:], final_g[0:1, :].partition_broadcast(128).rearrange("p a d -> p (a d)"), (), [K(fg_bc)])
    dma(g_col[:], norm_g.rearrange("o (k p) -> p (o k)", p=128), (), ["g_col"], slow=True)

    ci = [0]

    def cast_eng():
        ci[0] += 1
        return ["act", "dve"][ci[0] % 2]

    class SetupRing:
        def __init__(self):
            self.t = [(xst[0], ("xst", 0)), (xst[1], ("xst", 1)), (xres, "xres"),
                      (otmp[:].rearrange("p k b g d -> p (k b g d)"), "otmp"),
                      (Oall[:].rearrange("p k b g e -> p (k b g e)"), "Oall")]
            self.i = 0

        def next(self):
            r = self.t[self.i % len(self.t)]
            self.i += 1
            return r

    sstg = SetupRing()
    for kc in range(8):
        for c0, c1 in [(0, 1024), (1024, 2048), (2048, PT)]:
            st, sk = sstg.next()
            dma(st[:, 0:c1 - c0], w_in[kc * 128:(kc + 1) * 128, c0:c1], (), [sk])
            if c0 == 0:
                ts(cast_eng(), W_bf[:, kc, 0:512].rearrange("p (g k d) -> p k g d", g=4, k=2),
                   st[:, 0:512].rearrange("p (k g d) -> p k g d", k=2, g=4), g_col[:, kc:kc + 1], None, ALU.mult, None,
                   [sk, "g_col"], [("W", kc)])
                ts(cast_eng(), W_bf[:, kc, 512:1024], st[:, 512:1024], g_col[:, kc:kc + 1], None, ALU.mult, None,
                   [sk, "g_col"], [("W", kc)])
            else:
                ts(cast_eng(), W_bf[:, kc, c0:c1], st[:, 0:c1 - c0], g_col[:, kc:kc + 1], None, ALU.mult, None,
                   [sk, "g_col"], [("W", kc)])
    for j in range(2):
        for iq in range(4):
            st, sk = sstg.next()
            src = cmp_w1[j, iq * 8:(iq + 1) * 8].rearrange("i d h -> d i h")
            stv = st[:, 0:1024].rearrange("p (i h) -> p i h", i=8)
            dma(stv[0:64], src, (), [sk])
            dma(stv[64:128], src, (), [sk])
            cp(cast_eng(), W1_2[:, j, iq * 8:(iq + 1) * 8, :], stv, [sk], ["W1"])
    memset("pool", w2p[:], 0.0, ["w2p"])
    st, sk = xst_r.next()
    dma(st[:, 0:128].rearrange("p (j d) -> p j d", j=2), cmp_w2.rearrange("j h d -> h j d"), (), [sk])
    for j in range(2):
        cp("dve", w2p[:, j, 0, 0:64], st[:, j * 64:(j + 1) * 64], [sk], ["w2p"])
        cp("dve", w2p[:, j, 1, 64:128], st[:, j * 64:(j + 1) * 64], [sk], ["w2p"])
    st, sk = xst_r.next()
    dma(st[:, 0:512].rearrange("p (g d) -> p g d", g=4), pool_w.rearrange("g c d -> c g d"), (), [sk])
    dma(st[:, 512:1024], pool_scale[0:1, :].partition_broadcast(128).rearrange("p a d -> p (a d)"), (), [sk])
    tt("dve", pw_bf[:].rearrange("p g d -> p (g d)"), st[:, 0:512], st[:, 512:1024], ALU.mult, [sk], ["pw"])
    st, sk = xst_r.next()
    dma(st[0:64, 0:64], cmp_pe, (), [sk])
    pt_, pk = psg.next()
    mm(pt_[0:64, 0:64], st[0:64, 0:64], ident_f[0:64, 0:64], True, True, [sk, K(ident_f)], [pk])
    cp("dve", peT[0:64, :], pt_[0:64, 0:64], [pk], ["peT"])
    pt_, pk = psg.next()
    for j in range(2):
        for i in range(32):
            mm(pt_[:, j:j + 1], W1_2[0:64, j, i, :], peT[0:64, j * 32 + i:j * 32 + i + 1], i == 0, i == 31,
               ["W1", "peT"], [pk])
    cp("dve", hbias[:], pt_[:, 0:2], [pk], ["hbias"])
    memset("pool", Vs[:], 1.0, [("Vs", i) for i in range(64)])
    memset("pool", Vw[:], 1.0, [("Vw", i) for i in range(8)])
    memset("pool", vc_ext[:], 1.0, ["vc_ext"])
    memset("pool", vc_s[:], 1.0, ["vc_s"])
    memset("pool", vnew[:], 1.0, ["vnew"])
    memset("pool", kcT[:], 0.0, ["kcT"])
    memset("pool", vcT[:], 0.0, ["vcT"])
    memset("pool", CT[:], 0.0, ["CT"])

    def norm_xT(src, rows, dst_fn, dkey):
        xt, xk = xst_r.next()
        dma(xt[0:rows, :], src, (), [xk])
        memset("dve", ssq[0:rows, 0:1], 0.0, ["ssq"])
        act(xn_bf[0:rows, :], xt[0:rows, :], AF.Square, [xk], ["xn", "ssq"], accum=ssq[0:rows, 0:1])
        ts("dve", ssq[0:rows, 1:2], ssq[0:rows, 0:1], 1.0 / D, 1e-6, ALU.mult, ALU.add, ["ssq"], ["ssq"])
        act(ssq[0:rows, 3:4], ssq[0:rows, 1:2], AF.Ln, ["ssq"], ["ssq"])
        act(ssq[0:rows, 2:3], ssq[0:rows, 3:4], AF.Exp, ["ssq"], ["ssq"], scale=-0.5)
        ts("dve", xn_bf[0:rows, :], xt[0:rows, :], ssq[0:rows, 2:3], None, ALU.mult, None, [xk, "ssq"], ["xn"])
        for h in range(2):
            pt, pk = psg.next()
            for kk in range(4):
                kc = h * 4 + kk
                mm(pt[:, kk * rows:(kk + 1) * rows], xn_bf[0:rows, kc * 128:(kc + 1) * 128], ident_bf[0:rows, 0:rows],
                   True, True, ["xn", K(ident_bf)], [pk])
            evac(dst_fn(h * 4, 4), pt[:, 0:4 * rows].rearrange("p (k r) -> p k r", k=4), [pk], [dkey])

    def normA(g, b):
        blk = 4 * g + b
        norm_xT(xb[blk * 128:(blk + 1) * 128, :], 128,
                lambda k0, n, b=b: xT_g[:, k0:k0 + n, b * 128:(b + 1) * 128], xgk)

    def normB(m, which):
        if which == 0:
            norm_xT(x_own[m * 128:(m + 1) * 128, :], 128, lambda k0, n: xT_o[:, k0:k0 + n, :], "xT_o")
        else:
            norm_xT(x_halo[m * 64:(m + 1) * 64, :], 64, lambda k0, n: xT_h[:, k0:k0 + n, :], "xT_h")

    def phaseA(g, do_norm=True):
        xg = xT_g
        if do_norm:
            for b in range(4):
                normA(g, b)
        if STAGE <= 0.3:
            return
        for c0, kind in [(C_SKV, "ks"), (C_WKV, "kw"), (C_CKV, "ck"), (C_CKV + 128, "cv")]:
            pt, pk = psg.next()
            for kc in range(8):
                mm(pt[:, :], W_bf[:, kc, c0:c0 + 128], xg[:, kc, :], kc == 0, kc == 7, [("W", kc), xgk], [pk])
            if kind == "ks":
                evac(KsT[:, 512 * g:512 * g + 512], pt[:, :], [pk], [("KsT", 4 * g + b) for b in range(4)])
            elif kind == "kw":
                s0 = (4 * g) % 8
                evac(KwT[:, s0:s0 + 4, :], pt[:, :].rearrange("p (a b) -> p a b", a=4), [pk],
                     [("KwT", s0 + b) for b in range(4)])
            else:
                evac(CT[:, 0 if kind == "ck" else 1, 16:528], pt[:, :], [pk], ["CT"])
        if STAGE <= 0.4:
            return
        for b in range(4):
            blk = 4 * g + b
            pt, pk = psg.next()
            for kc in range(8):
                mm(pt[:, 0:128], xg[:, kc, b * 128:(b + 1) * 128], W_bf[:, kc, C_SKV + 128:C_SKV + 256],
                   kc == 0, kc == 7, [("W", kc), xgk], [pk])
                mm(pt[:, 128:256], xg[:, kc, b * 128:(b + 1) * 128], W_bf[:, kc, C_WKV + 128:C_WKV + 256],
                   kc == 0, kc == 7, [("W", kc), xgk], [pk])
            if STAGE <= 0.5:
                continue
            evac(Vs[:, blk, :, 0:64], pt[:, 0:128].rearrange("p (k d) -> p k d", k=2), [pk], [("Vs", blk)])
            if STAGE <= 0.55:
                continue
            evac(Vw[:, blk % 8, :, 0:64], pt[:, 128:256].rearrange("p (k d) -> p k d", k=2), [pk], [("Vw", blk % 8)])
        if STAGE <= 0.6:
            return
        n0 = 1 if g == 0 else 0
        nn = 32 - n0
        pts = [psg.next(), psg.next()]
        for kv in range(2):
            P0 = kv * 64
            pt, pk = pts[kv]
            for j in range(2):
                for i in range(32):
                    rhs = CT[P0:P0 + 64, j, i + 16 * n0:i + 16 * n0 + 16 * (nn - 1) + 1:16]
                    mm(pt[:, j * 32 + n0:j * 32 + 32], W1_2[P0:P0 + 64, j, i, :], rhs,
                       i == 0, i == 31, ["W1", "CT"], [pk])
        if STAGE <= 0.7:
            return
        for kv in range(2):
            pt, pk = pts[kv]
            for j in range(2):
                act(a_act[:, j * 2 + kv, n0:32], pt[:, j * 32 + n0:j * 32 + 32],
                    AF.Silu, [pk, "hbias"], ["a_act"], bias=hbias[:, j:j + 1])
        if STAGE <= 0.8:
            return
        cbase = 32 * g - 1
        for j, dst, dk in [(0, kcT, "kcT"), (1, vcT, "vcT")]:
            pt2, pk2 = psg.next()
            for kv in range(2):
                mm(pt2[:, n0:32], w2p[:, j, kv, :], a_act[:, j * 2 + kv, n0:32], kv == 0, kv == 1,
                   ["w2p", "a_act"], [pk2])
            evac(dst[:, cbase + n0:cbase + 32], pt2[:, n0:32], [pk2], [dk])
        cp("dve", CT[:, :, 0:16], CT[:, :, 512:528], ["CT"], ["CT"])

    def topk(rows, ncol, imp_ap, cand_ap, forced_ap, out_ap, rd, wr):
        R = slice(0, rows)
        tt("dve", candt[R, 0:ncol], imp_ap, cand_ap, ALU.mult, rd, ["candt"])
        S.add("dve", lambda e: e.max(out=mx8[R, 0:8], in_=candt[R, 0:ncol]), ["candt"], ["mx8"])
        S.add("dve", lambda e: e.match_replace(out=work[R, 0:ncol], in_to_replace=mx8[R, 0:8],
                                               in_values=candt[R, 0:ncol], imm_value=0.0), ["candt", "mx8"], ["work"])
        S.add("dve", lambda e: e.max(out=mx8[R, 8:16], in_=work[R, 0:ncol]), ["work"], ["mx8"])
        memset("dve", mx8[R, 13:16], 0.0, ["mx8"])
        S.add("dve", lambda e: e.match_replace(out=work2[R, 0:ncol], in_to_replace=mx8[R, 8:16],
                                               in_values=work[R, 0:ncol], imm_value=0.0), ["work", "mx8"], ["work2"])
        tt("dve", selt[R, 0:ncol], candt[R, 0:ncol], work2[R, 0:ncol], ALU.not_equal, ["candt", "work2"], ["selt"])
        tt("dve", selt[R, 0:ncol], selt[R, 0:ncol], forced_ap, ALU.max, ["selt"] + list(rd), ["selt"])
        ts("dve", out_ap, selt[R, 0:ncol], 1.0, -NEG, ALU.subtract, ALU.mult, ["selt"], wr)

    def tail(rows, xsrc_dram, ypool_ap, ypk, dst_dram):
        R = slice(0, rows)
        ts("dve", rdall[R], Oall[R, :, :, :, 64], 1e-30, None, ALU.add, None, ["Oall"], ["rdall"])
        S.add("dve", lambda e: e.reciprocal(rdall[R], rdall[R]), ["rdall"], ["rdall"])
        tt("dve", wgt[R], rdall[R], gate[R, :].rearrange("p (k g b) -> p k b g", k=2, g=4), ALU.mult,
           ["rdall", "gate"], ["wgt"])
        tt("dve", otmp[R], Oall[R, :, :, :, 0:64], wgt[R].unsqueeze(4).to_broadcast([rows, 2, 3, 4, 64]), ALU.mult,
           ["Oall", "wgt"], ["otmp"])
        ov = osum[R, :].rearrange("p (k g d) -> p k g d", k=2, g=4)
        tt("dve", ov, otmp[R, :, 0], otmp[R, :, 1], ALU.add, ["otmp"], ["osum"])
        tt("dve", ov, ov, otmp[R, :, 2], ALU.add, ["otmp", "osum"], ["osum"])
        tt("dve", mix_bf[R, 0:512], osum[R, :], sza[R, :], ALU.mult, ["osum", "sza"], ["mix"])
        tt("dve", mix_bf[R, 512:1024], ypool_ap, szp[R, :], ALU.mult, [ypk, "szp"], ["mix"])
        for h in range(2):
            pt, pk = psg.next()
            for kk in range(4):
                kc = h * 4 + kk
                mm(pt[:, kk * rows:(kk + 1) * rows], mix_bf[R, kc * 128:(kc + 1) * 128], ident_bf[R, 0:rows],
                   True, True, ["mix", K(ident_bf)], [pk])
            evac(mixT[:, h * 4:h * 4 + 4, 0:rows], pt[:, 0:4 * rows].rearrange("p (k r) -> p k r", k=4), [pk], ["mixT"])
        p0, p0k = psg.next()
        p1, p1k = psg.next()
        for kc in range(8):
            st, sk = xst_r.next()
            dma(st[:, :], w_out[kc * 128:(kc + 1) * 128, :], (), [sk])
            wo, wok = Wo_r.next()
            cp(cast_eng(), wo[:], st[:, :], [sk], [wok])
            mm(p0[R, :], mixT[:, kc, 0:rows], wo[:, 0:512], kc == 0, kc == 7, ["mixT", wok], [p0k])
            mm(p1[R, :], mixT[:, kc, 0:rows], wo[:, 512:1024], kc == 0, kc == 7, ["mixT", wok], [p1k])
        st, sk = xst_r.next()
        dma(st[R, :], xsrc_dram, (), [sk])
        tt("dve", xres[R, 0:512], p0[R, :], st[R, 0:512], ALU.add, [p0k, sk], ["xres"])
        tt("dve", xres[R, 512:1024], p1[R, :], st[R, 512:1024], ALU.add, [p1k, sk], ["xres"])
        memset("dve", ssq[R, 4:5], 0.0, ["ssq2"])
        act(xn_bf[R, :], xres[R, :], AF.Square, ["xres"], ["xn", "ssq2"], accum=ssq[R, 4:5])
        ts("dve", ssq[R, 5:6], ssq[R, 4:5], 1.0 / D, 1e-6, ALU.mult, ALU.add, ["ssq2"], ["ssq2"])
        act(ssq[R, 7:8], ssq[R, 5:6], AF.Ln, ["ssq2"], ["ssq2"])
        act(ssq[R, 6:7], ssq[R, 7:8], AF.Exp, ["ssq2"], ["ssq2"], scale=-0.5)
        stt("dve", xres[R, :], xres[R, :], ssq[R, 6:7], fg_bc[R, :], ALU.mult, ALU.mult,
            ["xres", "ssq2", K(fg_bc)], ["xres"])
        dma(dst_dram, xres[R, :], ["xres"], ["ydram"])

    def tokmajor_proj(rows, xT_ap, xTk, kvdst, kvk):
        R = slice(0, rows)
        for c0, c1 in [(512, 1024), (1024, 1536), (1536, 1816), (2328, 2840)]:
            pt, pk = psg.next()
            for kc in range(8):
                mm(pt[R, 0:c1 - c0], xT_ap(kc), W_bf[:, kc, c0:c1], kc == 0, kc == 7, [("W", kc), xTk], [pk])
            if c0 == 512:
                cp("dve", kvdst[R, 0:512], pt[R, 0:512], [pk], [kvk])
            elif c0 == 1024:
                cp("dve", kvdst[R, 512:768], pt[R, 0:256], [pk], [kvk])
                act(gate[R, :], pt[R, 256:280], AF.Sigmoid, [pk], ["gate"])
                act(sza[R, 0:232], pt[R, 280:512], AF.Silu, [pk], ["sza"])
            elif c0 == 1536:
                act(sza[R, 232:512], pt[R, 0:280], AF.Silu, [pk], ["sza"])
            else:
                act(szp[R, :], pt[R, 0:512], AF.Silu, [pk], ["szp"])

    def qT_proj(xT_ap, xTk, n, dst, dk):
        pt, pk = psg.next()
        for gq in range(4):
            for kc in range(8):
                lhsT = W_bf[:, kc, gq * 128:(gq + 1) * 128]
                mm(pt[:, gq * n:(gq + 1) * n], lhsT, xT_ap(kc), kc == 0, kc == 7, [("W", kc), xTk], [pk])
        evac(dst, pt[:, 0:4 * n].rearrange("p (g n) -> p g n", g=4), [pk], [dk])

    def phaseB(m, do_norm=True, prefetch=()):
        nct = n_ct(m)
        cz, czk = cmpZ_r.next()
        dma(cz[:, 0:nct, :], c_cmpZ[:, CT_BASE[m] * 128:(CT_BASE[m] + nct) * 128].rearrange("p (a b) -> p a b", a=nct),
            (), [czk])
        cd, cdk = cand_r.next()
        dma(cd[:], c_cand[:, m * 128:(m + 1) * 128], (), [cdk])
        fo, fok = forced_r.next()
        dma(fo[:], c_forced[:, m * 128:(m + 1) * 128], (), [fok])
        if do_norm:
            normB(m, 0)
            normB(m, 1)
        tokmajor_proj(128, lambda kc: xT_o[:, kc, :], "xT_o", okv, "okv")
        dma(kv_own[m * 128:(m + 1) * 128, :], okv[:], ["okv"], ["kvdram"])
        qT_proj(lambda kc: xT_o[:, kc, :], "xT_o", 128, qT[:], "qT")
        pt, pk = psg.next()
        for gp in range(4):
            for kc in range(8):
                mm(pt[:, gp * 128:(gp + 1) * 128], W_bf[:, kc, C_U + gp * 128:C_U + (gp + 1) * 128], xT_o[:, kc, :],
                   kc == 0, kc == 7, [("W", kc), "xT_o"], [pk])
        cp("dve", uT[:, :, :, 16:48], pt[:, :].rearrange("p (g a i) -> p g a i", g=4, a=4), [pk], ["uT"])
        pt, pk = psg.next()
        for gp in range(4):
            for kc in range(8):
                mm(pt[:, gp * 64:(gp + 1) * 64], W_bf[:, kc, C_U + gp * 128:C_U + (gp + 1) * 128], xT_h[:, kc, :],
                   kc == 0, kc == 7, [("W", kc), "xT_h"], [pk])
        cp("dve", uT[:, :, :, 0:16], pt[:, 0:256].rearrange("p (g a i) -> p g a i", g=4, a=4), [pk], ["uT"])
        if m == NM - 1:
            dma(pool_lastT.rearrange("p (g i) -> p g i", g=4), uT[:, :, 3, 32:48], ["uT"], ["pldram"])
        if m == 0:
            stc, stck = xst_r.next()
            dma(stc[:, 0:512], c_poolcorr, (), [stck])
        for gp in range(4):
            w = 2 << gp
            cur, ck = uT[:, gp], "uT"
            sh = 1
            lo = 0
            while sh < w:
                nt, nk = uS_r.next()
                lo += sh
                tt("pool", nt[:, :, lo:48], cur[:, :, lo:48], cur[:, :, lo - sh:48 - sh], ALU.add, [ck], [nk])
                cur, ck = nt, nk
                sh *= 2
            if m == 0:
                tt("pool", cur[:, :, 16:48], cur[:, :, 16:48],
                   stc[:, gp * 128:(gp + 1) * 128].rearrange("p (a i) -> p a i", a=4), ALU.mult, [ck, stck], [ck])
            stt("dve", dT[:, gp, :].rearrange("p (a i) -> p a i", a=4), cur[:, :, 16:48], 1.0 / w, uT[:, gp, :, 16:48],
                ALU.mult, ALU.subtract, [ck, "uT"], ["dT"])
        yp, ypk = psg.next()
        for gp in range(4):
            mm(yp[:, gp * 128:(gp + 1) * 128], dT[:, gp, :], pw_bf[:, gp, :], True, True, ["dT", "pw"], [ypk])
        cp("dve", ypool_sb[:, :], yp[:, :], [ypk], ["ypool"])
        for ct in range(nct):
            pt, pk = psg.next()
            mm(pt[:, 0:128], vcT[:, ct * 128:(ct + 1) * 128], ident_bf[:], True, True, ["vcT", K(ident_bf)], [pk])
            evac(vc_ext[:, ct, :, 0:64], pt[:, 0:128].rearrange("p (k d) -> p k d", k=2), [pk], ["vc_ext"])
        cmpS, winS, slcS = [[], []], [[], []], [[], []]
        st_cmp_all = [{}, {}]
        for kv in range(2):
            P0 = kv * 64
            Pq = slice(P0, P0 + 64)
            qall = qT[Pq, :, :].rearrange("p g q -> p (g q)")
            st_cmp, st_win, st_slc = st_cmp_all[kv], {}, {}

            def acc_tile(st):
                if "pa" not in st:
                    st["pa"], st["pak"] = pacc.next()
                    st["pav"] = st["pa"][:, 0:260].rearrange("p (g e) -> p g e", g=4)
                return st["pav"], st["pak"]

            for ct in range(nct):
                def s1(ct=ct, Pq=Pq, qall=qall, st=st_cmp):
                    pt, pk = psg.next()
                    mm(pt[:, :], kcT[Pq, ct * 128:(ct + 1) * 128], qall, True, False, ["kcT", "qT"], [pk])
                    for gq in range(4):
                        mm(pt[:, gq * 128:(gq + 1) * 128], ident_bf[:], cz[:, ct, :], False, True,
                           [K(ident_bf), czk], [pk])
                    pf, pfk = pTf_next()
                    act(pf.rearrange("p g q -> p (g q)"), pt[:, :], AF.Exp, [pk], [pfk], scale=0.125)
                    pb, pbk = pTb_next()
                    cp("dve", pb[:], pf, [pfk], [pbk])
                    st[ct] = (pf, pfk, pb, pbk)

                def s2(ct=ct, kv=kv, st=st_cmp, m=m, acc_tile=acc_tile):
                    pf, pfk, pb, pbk = st[ct]
                    pav, pak = acc_tile(st)
                    if "pimp" not in st:
                        st["pimp"] = pimp_r.next()
                    pimp_t, pimp_k = st["pimp"]
                    for gq in range(4):
                        mm(pav[:, gq, :], pb[:, gq, :], vc_ext[:, ct, kv, 0:65], ct == 0, ct == nct - 1, [pbk, "vc_ext"], [pak])
                    for gq in range(4):
                        mm(pimp_t[:, gq * 128:(gq + 1) * 128], pf[:, gq, :], cover[:, ct, :], ct == 0, ct == nct - 1,
                           [pfk, K(cover)], [pimp_k])
                    if ct == nct - 1:
                        cp("dve", Oall[:, kv, 0], pav, [pak], ["Oall"])
                        ts("dve", rdall[:, kv, 0], Oall[:, kv, 0, :, 64], 1e-30, None, ALU.add, None, ["Oall"], ["rdall"])
                        S.add("dve", lambda e: e.reciprocal(rdall[:, kv, 0], rdall[:, kv, 0]), ["rdall"], ["rdall"])
                        ts("dve", impt[:], pimp_t[:, 0:128], rdall[:, kv, 0, 0:1], None, ALU.mult, None,
                           [pimp_k, "rdall"], ["impt"])
                        for gq in range(1, 4):
                            stt("dve", impt[:], pimp_t[:, gq * 128:(gq + 1) * 128], rdall[:, kv, 0, gq:gq + 1], impt[:],
                                ALU.mult, ALU.add, [pimp_k, "rdall", "impt"], ["impt"])
                        sn, snk = seln_r.next()
                        topk(128, 128, impt[:], cd[:], fo[:], sn[:], ["impt", cdk, fok], [snk])
                        st["sn"] = (sn, snk)
                cmpS[kv].append((s1, s2))
            jlist = [jj for jj in range(8) if 4 * m - 4 + jj >= 0]
            for jj in jlist:
                def s1a(jj=jj, Pq=Pq, qall=qall, st=st_win):
                    kt = 4 * m - 4 + jj
                    pt, pk = psg.next()
                    mm(pt[:, :], KwT[Pq, kt % 8, :], qall, True, False, [("KwT", kt % 8), "qT"], [pk])
                    st[("pt", jj)] = (pt, pk)

                def s1b(jj=jj, st=st_win):
                    pt, pk = st[("pt", jj)]
                    for gq in range(4):
                        mm(pt[:, gq * 128:(gq + 1) * 128], ident_bf[:], winZ[:, jj, :], False, True,
                           [K(ident_bf), K(winZ)], [pk])
                    pb, pbk = pTb_next()
                    act(pb[:].rearrange("p g q -> p (g q)"), pt[:, :], AF.Exp, [pk], [pbk], scale=0.125)
                    st[jj] = (pb, pbk)

                def s2(jj=jj, kv=kv, st=st_win, jlist=jlist, acc_tile=acc_tile):
                    kt = 4 * m - 4 + jj
                    pb, pbk = st[jj]
                    pav, pak = acc_tile(st)
                    for gq in range(4):
                        mm(pav[:, gq, :], pb[:, gq, :], Vw[:, kt % 8, kv, 0:65], jj == jlist[0], jj == jlist[-1],
                           [pbk, ("Vw", kt % 8)], [pak])
                    if jj == jlist[-1]:
                        cp("dve", Oall[:, kv, 2], pav, [pak], ["Oall"])
                winS[kv].append((s1a, s1b, s2))
            nkt = 4 * m + 4
            for kt in range(nkt):
                def s1a(kt=kt, Pq=Pq, qall=qall, st=st_slc, stc=st_cmp):
                    sn, snk = stc["sn"]
                    pt, pk = psg.next()
                    sx, sxk = selx_r.next()
                    cp("dve", sx[:].rearrange("p (b k) -> p b k", b=2),
                       sn[:, 2 * kt:2 * kt + 2].unsqueeze(2).to_broadcast([128, 2, 64]), [snk], [sxk])
                    mm(pt[:, :], KsT[Pq, kt * 128:(kt + 1) * 128], qall, True, False, [("KsT", kt), "qT"], [pk])
                    st[("pt", kt)] = (pt, pk, sx, sxk)

                def s1b(kt=kt, st=st_slc):
                    pt, pk, sx, sxk = st[("pt", kt)]
                    mm(pt[:, :], sx[:], ident4[:], False, kt < 4 * m, [sxk, K(ident4)], [pk])
                    if kt >= 4 * m:
                        for gq in range(4):
                            mm(pt[:, gq * 128:(gq + 1) * 128], ident_bf[:], triZ[:, kt - 4 * m, :], False, True,
                               [K(ident_bf), K(triZ)], [pk])
                    pb, pbk = pTb_next()
                    act(pb[:].rearrange("p g q -> p (g q)"), pt[:, :], AF.Exp, [pk], [pbk], scale=0.125)
                    st[kt] = (pb, pbk)

                def s2(kt=kt, kv=kv, st=st_slc, nkt=nkt, acc_tile=acc_tile):
                    pb, pbk = st[kt]
                    pav, pak = acc_tile(st)
                    for gq in range(4):
                        mm(pav[:, gq, :], pb[:, gq, :], Vs[:, kt, kv, 0:65], kt == 0, kt == nkt - 1, [pbk, ("Vs", kt)], [pak])
                    if kt == nkt - 1:
                        cp("dve", Oall[:, kv, 1], pav, [pak], ["Oall"])
                slcS[kv].append((s1a, s1b, s2))

        def pair(a, b):
            return (lambda: (a[0](), b[0](), a[1](), b[1]()), lambda: (a[2](), b[2]()))

        stages = cmpS[0] + cmpS[1]
        stages += [pair(winS[0][i], winS[1][i]) for i in range(len(winS[0]))]
        stages += [pair(slcS[0][i], slcS[1][i]) for i in range(len(slcS[0]))]
        SK = 1
        npf = len(prefetch)
        inject = {max(0, len(stages) - 3 * (npf - j)): j for j in range(npf)}
        done_pf = set()
        for i in range(len(stages) + SK):
            if i < len(stages):
                stages[i][0]()
            if i - SK >= 0:
                stages[i - SK][1]()
            if i in inject:
                for j in range(npf):
                    if j not in done_pf and inject.get(i) is not None and j <= inject[i]:
                        prefetch[j]()
                        done_pf.add(j)
        for j in range(npf):
            if j not in done_pf:
                prefetch[j]()

    if STAGE == 0:
        S.finalize()
        return _emit(nc, es, S)
    phaseA(0)
    for g in range(NM):
        pf = []
        if g + 1 < NM:
            pf = [lambda b=b, g=g: normA(g + 1, b) for b in range(4)] + [lambda g=g: normB(g + 1, 0), lambda g=g: normB(g + 1, 1)]
        phaseB(g, do_norm=(g == 0), prefetch=pf)
        if g + 1 < NM:
            phaseA(g + 1, do_norm=False)
        tail(128, x_own[g * 128:(g + 1) * 128, :], ypool_sb[:, :], "ypool", y_own[g * 128:(g + 1) * 128, :])

    if SKIP_SAMPLE:
        S.finalize()
        return _emit(nc, es, S)
    norm_xT(xs, NSAMP, lambda k0, n: xT_o[:, k0:k0 + n, 0:16], "xT_o")
    xTs = lambda kc: xT_o[:, kc, 0:16]
    tokmajor_proj(16, xTs, "xT_o", Ps[:, 0:768], "Ps")
    dma(skv, Ps[:, 0:768], ["Ps"], ["skv_d"])
    qT_proj(xTs, "xT_o", 16, qT_s[:], "qT_s")
    pt, pk = psg.next()
    for kc in range(8):
        mm(pt[0:16, :], xTs(kc), W_bf[:, kc, C_U:C_U + 512], kc == 0, kc == 7, [("W", kc), "xT_o"], [pk])
    cp("dve", Ps[:, 768:1280], pt[0:16, :], [pk], ["Ps_u"])
    pt, pk = psg.next()
    for kc in range(8):
        mm(pt[:, 0:16], W_bf[:, kc, C_SKV:C_SKV + 128], xTs(kc), kc == 0, kc == 7, [("W", kc), "xT_o"], [pk])
    cp("dve", KsTn[:], pt[:, 0:16], [pk], ["KsTn"])
    dma(win_out[:, 0:511 * 256], win_c[:, 256:512 * 256], (), ["win_d"])
    dma(win_out[:, 511 * 256:512 * 256], Ps[:, 512:768], ["Ps"], ["win_d"])
    dma(pool_out[:, 0:14, :], pool_c[:, 1:15, :], (), ["pool_d"])
    dma(pool_out[:, 14, :], Ps[:, 768:1280], ["Ps_u"], ["pool_d"])
    for kv in range(2):
        st, sk = xst_r.next()
        dma(st[0:1, 0:1024].rearrange("o (n d) -> o n d", n=16),
            skv[:, 384 + kv * 64:384 + (kv + 1) * 64].rearrange("(o n) c -> o n c", o=1), ["skv_d"], [sk])
        cp("dve", vnew[:, :, kv, 0:64], st[0:1, 0:1024].rearrange("o (n d) -> o n d", n=16), [sk], ["vnew"])
    for gp in range(4):
        w = 2 << gp
        nr = w - 1
        uu = Ps[:, 768 + gp * 128:768 + (gp + 1) * 128]
        sr = sred[:, gp * 128:(gp + 1) * 128]
        first = True
        for r0 in range(15 - nr, 15, 8):
            r1 = min(r0 + 8, 15)
            st, sk = xst_r.next()
            stv = st[0:16, 0:(r1 - r0) * 128].rearrange("p (r c) -> p r c", c=128)
            dma(stv, pool_c[:, r0:r1, gp * 128:(gp + 1) * 128], (), [sk])
            dstp = sr if first else d_s_f[:, 0:128]
            S.add("dve", lambda e, dstp=dstp, stv=stv: e.tensor_reduce(dstp, stv.rearrange("p r c -> p c r"), AX.X, ALU.add),
                  [sk], ["osum" if first else "candt"])
            if not first:
                tt("dve", sr, sr, d_s_f[:, 0:128], ALU.add, ["osum", "candt"], ["osum"])
            first = False
        tt("dve", sr, sr, uu, ALU.add, ["osum", "Ps_u"], ["osum"])
        stt("dve", d_s[:, gp * 128:(gp + 1) * 128], sr, 1.0 / w, uu, ALU.mult, ALU.subtract, ["osum", "Ps_u"], ["mix"])
    pt, pk = psg.next()
    for gp in range(4):
        mm(pt[:, gp * 16:(gp + 1) * 16], d_s[:, gp * 128:(gp + 1) * 128], ident_bf[0:16, 0:16], True, True,
           ["mix", K(ident_bf)], [pk])
    cp("dve", dT[:, :, 0:16], pt[:, 0:64].rearrange("p (g n) -> p g n", g=4), [pk], ["dT"])
    yps, ypsk = psg.next()
    for gp in range(4):
        mm(yps[0:16, gp * 128:(gp + 1) * 128], dT[:, gp, 0:16], pw_bf[:, gp, :], True, True, ["dT", "pw"], [ypsk])
    cp("dve", ypool_sb[0:16, :], yps[0:16, :], [ypsk], ["ypool"])
    cp("dve", ptx_f[:], ptx_i[:], [K(ptx_i)], ["ptx_f"])
    ts("dve", ptx_f[:], ptx_f[:], 8.0, sub8[:, 0:1], ALU.mult, ALU.add, ["ptx_f", K(sub8)], ["ptx_f"])
    for q4 in range(4):
        ts("dve", idx_f[:], ptx_f[:], 4.0, float(q4), ALU.mult, ALU.add, ["ptx_f"], ["idx_f"])
        cp("dve", idx_i[:, q4, :], idx_f[:], ["idx_f"], ["idx_i"])

    def gather(dst, dk, cache, n, q4):
        S.add("pool", lambda e: e.indirect_dma_start(
            out=dst, out_offset=None, in_=cache,
            in_offset=bass.IndirectOffsetOnAxis(ap=idx_i[:, q4, n:n + 1], axis=0)), ["idx_i"], [dk], dma=True)

    xgkeys = [xgk]

    class GRing:
        def __init__(self):
            self.t = [(xst[0][:, :], ("xst", 0)), (xst[1][:, :], ("xst", 1)), (xres[:, :], "xres"),
                      (otmp[:].rearrange("p k b g d -> p (k b g d)")[:, 0:1024], "otmp")]
            self.i = 0

        def next(self):
            r = self.t[self.i % 4]
            self.i += 1
            return r

    g_r = GRing()
    for n in range(NS_LOOP):
        for q4 in range(4):
            st, sk = g_r.next()
            gather(st, sk, cache_cmp, n, q4)
            cp("act", cmp_bf[:, q4 * 4:(q4 + 1) * 4, :], st.rearrange("p (e c) -> p e c", e=4), [sk], xgkeys)
        for j in range(2):
            for eh in range(4):
                pt, pk = psg.next()
                for ee in range(4):
                    e_ = eh * 4 + ee
                    mm(pt[:, ee * 128:(ee + 1) * 128], cmp_bf[:, e_, j * 128:(j + 1) * 128], ident_bf[:], True, True,
                       xgkeys + [K(ident_bf)], [pk])
                evac(CTs[:, j, eh * 512:(eh + 1) * 512], pt[:, :], [pk], CTs_keys)
        pts = [psg.next(), psg.next()]
        for kv in range(2):
            P0 = kv * 64
            pt, pk = pts[kv]
            for j in range(2):
                for i in range(32):
                    e_, o_ = i % 16, i // 16
                    mm(pt[:, j * 127:(j + 1) * 127], W1_2[P0:P0 + 64, j, i, :],
                       CTs[P0:P0 + 64, j, e_ * 128 + o_:e_ * 128 + o_ + 127], i == 0, i == 31, ["W1"] + CTs_keys, [pk])
        for kv in range(2):
            pt, pk = pts[kv]
            for j in range(2):
                act(a_act[:, j * 2 + kv, 0:127], pt[:, j * 127:(j + 1) * 127],
                    AF.Silu, [pk, "hbias"], ["a_act"], bias=hbias[:, j:j + 1])
        for j, dst, dk in [(0, kcT, "kcT"), (1, vcT, "vcT")]:
            pt2, pk2 = psg.next()
            for kv in range(2):
                mm(pt2[:, 0:127], w2p[:, j, kv, :], a_act[:, j * 2 + kv, 0:127], kv == 0, kv == 1, ["w2p", "a_act"], [pk2])
            evac(dst[:, 0:127], pt2[:, 0:127], [pk2], [dk])
        pt, pk = psg.next()
        mm(pt[0:127, 0:128], vcT[:, 0:127], ident_bf[:], True, True, ["vcT", K(ident_bf)], [pk])
        evac(vc_s[0:127, :, 0:64], pt[0:127, 0:128].rearrange("p (k d) -> p k d", k=2), [pk], ["vc_s"])
        om, omk = Osm_r.next()
        im, imk = impst_r.next()
        for kv in range(2):
            Pq = slice(kv * 64, kv * 64 + 64)
            pt, pk = psg.next()
            mm(pt[0:127, 0:4], kcT[Pq, 0:127], qT_s[Pq, :, n], True, True, ["kcT", "qT_s"], [pk])
            act(pS_f[0:127, :], pt[0:127, 0:4], AF.Exp, [pk], ["pS_f"], scale=0.125)
            cp("dve", pS_b[0:127, 0, :], pS_f[0:127, :], ["pS_f"], ["pS_b"])
            pa, pak = pacc.next()
            mm(pa[0:4, 0:65], pS_b[0:127, 0, :], vc_s[0:127, kv, 0:65], True, True, ["pS_b", "vc_s"], [pak])
            mm(pa[0:4, 128:168], pS_f[0:127, :], cover33[0:127, :], True, True, ["pS_f", K(cover33)], [pak])
            cp("dve", om[:, kv, 0, :], pa[0:4, 0:65], [pak], [omk])
            cp("dve", im[:, kv, :], pa[0:4, 128:168], [pak], [imk])
        dma(scr_i[n].rearrange("k g s -> g k s"), im[:], [imk], ["scr_i"])
        dma(scr_o[n, :, 0].rearrange("k g e -> g k e"), om[:, :, 0, :], [omk], ["scr_o0"])
    dma(impr.rearrange("p k g s -> p (k g s)"), scr_i.rearrange("n k g s -> n (k g s)"), ["scr_i"], ["otmp"])
    dma(Oall[0:16, :, 0].rearrange("p k g e -> p k (g e)"), scr_o[:, :, 0].rearrange("n k g e -> n k (g e)"), ["scr_o0"], ["Oall"])
    for kv in range(2):
        ts("dve", rdall[0:16, kv, 0], Oall[0:16, kv, 0, :, 64], 1e-30, None, ALU.add, None, ["Oall"], ["rdall"])
        S.add("dve", lambda e, kv=kv: e.reciprocal(rdall[0:16, kv, 0], rdall[0:16, kv, 0]), ["rdall"], ["rdall"])
        ts("dve", impt[0:16, 0:40], impr[:, kv, 0, :], rdall[0:16, kv, 0, 0:1], None, ALU.mult, None, ["otmp", "rdall"], ["impt"])
        for gq in range(1, 4):
            stt("dve", impt[0:16, 0:40], impr[:, kv, gq, :], rdall[0:16, kv, 0, gq:gq + 1], impt[0:16, 0:40],
                ALU.mult, ALU.add, ["otmp", "rdall", "impt"], ["impt"])
        topk(16, 40, impt[0:16, 0:40], cand_s[:, :], forced_s[:, :], seln_s[:, kv, :],
             ["impt", K(cand_s), K(forced_s)], ["seln_s"])
    memset("dve", selx_s[:], 0.0, ["selx_s"])
    for kv in range(2):
        cp("dve", selx_s[0:16, kv, :].rearrange("p (s k) -> p s k", k=4),
           seln_s[:, kv, 0:32].unsqueeze(2).to_broadcast([16, 32, 4]), ["seln_s"], ["selx_s"])
    for n in range(NS_LOOP):
        for q4 in range(4):
            st, sk = g_r.next()
            gather(st, sk, cache_slc, n, q4)
            stv = st.rearrange("p (e c) -> p e c", e=4)
            cp("act", slck_bf[:, q4 * 4:(q4 + 1) * 4, :], stv[:, :, 0:128], [sk], xgkeys)
            cp("dve", Vs[:, q4 * 4:(q4 + 1) * 4, :, 0:64], stv[:, :, 128:256].rearrange("p e (k d) -> p e k d", k=2), [sk],
               [("Vs", i) for i in range(q4 * 4, q4 * 4 + 4)])
        for eh in range(4):
            pt, pk = psg.next()
            for ee in range(4):
                mm(pt[:, ee * 128:(ee + 1) * 128], slck_bf[:, eh * 4 + ee, :], ident_bf[:], True, True,
                   xgkeys + [K(ident_bf)], [pk])
            evac(KsT[:, eh * 512:(eh + 1) * 512], pt[:, :], [pk], [("KsT", i) for i in range(4 * eh, 4 * eh + 4)])
        st, sk = xst_r.next()
        wkv_ = st[:, :].rearrange("p (t c) -> p t c", t=4)
        dma(wkv_, win_out[n].rearrange("(t p c) -> p t c", t=4, p=128), ["win_d"], [sk])
        cp("act", cmp_bf[:, 0:4, 0:128], wkv_[:, :, 0:128], [sk], xgkeys)
        cp("dve", Vw[:, 0:4, :, 0:64], wkv_[:, :, 128:256].rearrange("p t (k d) -> p t k d", k=2), [sk],
           [("Vw", i) for i in range(4)])
        pt, pk = psg.next()
        for t_ in range(4):
            mm(pt[:, t_ * 128:(t_ + 1) * 128], cmp_bf[:, t_, 0:128], ident_bf[:], True, True, xgkeys + [K(ident_bf)], [pk])
        evac(KwT[:, 0:4, :], pt[:, :].rearrange("p (a b) -> p a b", a=4), [pk], [("KwT", i) for i in range(4)])
        om, omk = Osm_r.next()
        for kv in range(2):
            Pq = slice(kv * 64, kv * 64 + 64)
            qn = qT_s[Pq, :, n]
            pt, pk = psg.next()
            for e_ in range(16):
                mm(pt[:, e_ * 4:(e_ + 1) * 4], KsT[Pq, e_ * 128:(e_ + 1) * 128], qn, True, False, [("KsT", e_), "qT_s"], [pk])
                mm(pt[:, e_ * 4:(e_ + 1) * 4], selx_s[:, kv, :], e16[:, n, :], False, True, ["selx_s", K(e16)], [pk])
            act(pS_b[:, 0:16, :].rearrange("p e g -> p (e g)"), pt[:, 0:64], AF.Exp, [pk], ["pS_b"], scale=0.125)
            pt2, pk2 = psg.next()
            mm(pt2[0:1, 0:4], KsTn[Pq, n:n + 1], qn, True, True, ["KsTn", "qT_s"], [pk2])
            act(pN_b[:, :], pt2[0:1, 0:4], AF.Exp, [pk2], ["pN_b"], scale=0.125)
            pa, pak = pacc.next()
            for e_ in range(16):
                mm(pa[0:4, 0:65], pS_b[:, e_, :], Vs[:, e_, kv, 0:65], e_ == 0, False, ["pS_b", ("Vs", e_)], [pak])
            mm(pa[0:4, 0:65], pN_b[:, :], vnew[0:1, n, kv, 0:65], False, True, ["pN_b", "vnew"], [pak])
            cp("dve", om[:, kv, 1, :], pa[0:4, 0:65], [pak], [omk])
            pt, pk = psg.next()
            for t_ in range(4):
                mm(pt[:, t_ * 4:(t_ + 1) * 4], KwT[Pq, t_, :], qn, True, True, [("KwT", t_), "qT_s"], [pk])
            act(pS_b[:, 0:4, :].rearrange("p e g -> p (e g)"), pt[:, 0:16], AF.Exp, [pk], ["pS_b"], scale=0.125)
            pa, pak = pacc.next()
            for t_ in range(4):
                mm(pa[0:4, 0:65], pS_b[:, t_, :], Vw[:, t_, kv, 0:65], t_ == 0, t_ == 3, ["pS_b", ("Vw", t_)], [pak])
            cp("dve", om[:, kv, 2, :], pa[0:4, 0:65], [pak], [omk])
        for kv in range(2):
            dma(scr_o[n, kv, 1:3].rearrange("b g e -> g b e"), om[:, kv, 1:3, :], [omk], ["scr_o1"])
    dma(Oall[0:16, :, 1:3].rearrange("p k b g e -> p k (b g e)"), scr_o[:, :, 1:3].rearrange("n k b g e -> n k (b g e)"), ["scr_o1"], ["Oall"])
    tail(16, xs, ypool_sb[0:16, :], "ypool", ys)

    S.finalize()
    return _emit(nc, es, S)


def _emit(nc, es, S):
    sems = {}
    for e in S.ENG:
        sems[e] = es.enter_context(nc.semaphore(f"s_{e}"))
    dsems = {}
    for e in S.ENG:
        if any(op.dma for op in S.ops[e]):
            dsems[e] = [es.enter_context(nc.semaphore(f"d_{e}{i}")) for i in range(S.R)]
    block = es.enter_context(nc.Block())

    def emit(eng_name):
        def body(e):
            waited = {}

            def wait(sem, val):
                if waited.get(sem.name, 0) >= val:
                    return
                waited[sem.name] = val
                e.wait_ge(sem, val)

            for op in S.ops[eng_name]:
                for d in op.deps:
                    if d.dma:
                        wait(dsems[d.eng][d.sem], d.cnt)
                    else:
                        wait(sems[d.eng], d.cnt)
                if op.dma:
                    if op.guard:
                        wait(dsems[eng_name][op.sem], op.guard)
                    op.fn(e).then_inc(dsems[eng_name][op.sem], 16)
                else:
                    if DEBUG_NAMES is not None:
                        DEBUG_NAMES[nc.get_next_instruction_name()] = (eng_name, op.idx, op.src)
                    ins = op.fn(e)
                    if op.signal:
                        ins.then_inc(sems[eng_name], 1)
            last = {}
            for op in S.ops[eng_name]:
                if op.dma:
                    last[op.sem] = op.cnt
            for s_, c_ in last.items():
                wait(dsems[eng_name][s_], c_)
        return body

    block.tensor(emit("pe"))
    block.scalar(emit("act"))
    block.vector(emit("dve"))
    block.gpsimd(emit("pool"))
    block.sync(emit("sp"))
    es.close()
    return nc


def _consts(r):
    bf = ml_dtypes.bfloat16
    c = {}
    c["c_ident_bf"] = np.eye(128, dtype=np.float32).astype(bf)
    c["c_ident_f"] = np.eye(128, dtype=np.float32)
    c["c_ident4"] = np.tile(np.eye(128, dtype=np.float32), (1, 4)).astype(bf)
    qp = np.arange(128)
    a, i = qp // 32, qp % 32
    trel = a * 128 + 32 * r + i
    kl = np.arange(128)[:, None]
    tri = np.zeros((128, 4, 128), np.float32)
    for j in range(4):
        tri[:, j, :] = np.where(j * 128 + kl <= trel[None, :], 0.0, NEG)
    c["c_triZ"] = tri.reshape(128, -1).astype(bf)
    wz = np.zeros((128, 8, 128), np.float32)
    for jj in range(8):
        kp = (jj - 4) * 128 + kl
        ok = (kp <= trel[None, :]) & (kp > trel[None, :] - 512)
        wz[:, jj, :] = np.where(ok, 0.0, NEG)
    c["c_winZ"] = wz.reshape(128, -1).astype(bf)
    cz = np.zeros((128, N_CTT, 128), np.float32)
    cand = np.zeros((128, NM, 128), np.float32)
    forced = np.zeros((128, NM, 128), np.float32)
    sidx = np.arange(128)[None, :]
    for m in range(NM):
        t = 4 * m * 128 + trel
        for ct in range(n_ct(m)):
            cc = 128 * ct + kl
            ok = (16 * cc + 31 <= t[None, :]) & (cc <= 510)
            cz[:, CT_BASE[m] + ct, :] = np.where(ok, 0.0, NEG)
        cur = (t // 64)[:, None]
        valid = 64 * sidx <= t[:, None]
        fo = valid & ((sidx == 0) | (sidx == cur) | (sidx == cur - 1))
        forced[:, m, :] = fo
        cand[:, m, :] = valid & ~fo
    c["c_cmpZ"] = cz.reshape(128, -1).astype(bf)
    c["c_cand"] = cand.reshape(128, -1).astype(bf)
    c["c_forced"] = forced.reshape(128, -1).astype(bf)
    cov = np.zeros((128, 4, 128), np.float32)
    for ct in range(4):
        cc = 128 * ct + np.arange(128)[:, None]
        cov[:, ct, :] = (cc <= 510) & (cc >= 4 * sidx - 1) & (cc <= 4 * sidx + 3)
    c["c_cover"] = cov.reshape(128, -1)
    pc = np.ones((128, 4, 128), np.float32)
    for gp in range(4):
        w = 2 << gp
        pc[:, gp, :] = (w / np.minimum(w, trel + 1))[None, :]
    c["c_poolcorr"] = pc.reshape(128, -1)
    c33 = np.zeros((128, 40), np.float32)
    cc = np.arange(128)[:, None]
    s40 = np.arange(40)[None, :]
    c33[:] = (cc <= 126) & (cc >= 4 * s40 - 1) & (cc <= 4 * s40 + 3) & (s40 <= 32)
    c["c_cover33"] = c33
    cs = np.zeros((16, 40), np.float32)
    cs[:, 1:31] = 1.0
    fs = np.zeros((16, 40), np.float32)
    fs[:, [0, 31, 32]] = 1.0
    c["c_cand_s"], c["c_forced_s"] = cs, fs
    e = np.zeros((128, 16, 4), np.float32)
    for k in range(16):
        e[k, k, :] = 1.0
    c["c_e16"] = e.reshape(128, 64).astype(bf)
    c["c_sub8"] = (np.arange(128) % 8).astype(np.float32).reshape(128, 1)
    return c


_NC = None


def _prep(x_prompt, x_sample, cache_cmp_kv, cache_slc_kv, cache_win_kv, state_pool, page_table,
          norm_g, w_in, cmp_pe, cmp_w1, cmp_w2, pool_w, pool_scale, w_out, final_g, cores=range(8)):
    f = lambda a: np.ascontiguousarray(np.asarray(a, dtype=np.float32))
    x_prompt = f(x_prompt)
    nblk = x_prompt.shape[1] // 128
    cc = f(cache_cmp_kv).reshape(-1, 1024)
    cs = f(cache_slc_kv).reshape(-1, 1024)
    cw = f(cache_win_kv).reshape(128, 512 * 256)
    sp = f(state_pool).reshape(128, 15, 512)
    pt = np.asarray(page_table, dtype=np.int32)
    xsamp = f(x_sample).reshape(128, D)
    shared = dict(cache_cmp=cc, cache_slc=cs, norm_g=f(norm_g).reshape(1, D), w_in=f(w_in).reshape(D, PT),
                  cmp_pe=f(cmp_pe).reshape(64, 64), cmp_w1=f(cmp_w1).reshape(2, 32, 64, 128),
                  cmp_w2=f(cmp_w2).reshape(2, 128, 64), pool_w=f(pool_w).reshape(4, 128, 128),
                  pool_scale=f(pool_scale).reshape(1, 512), w_out=f(w_out).reshape(D, D),
                  final_g=f(final_g).reshape(1, D))
    in_maps = []
    own_idx = {}
    for c in cores:
        b, r = c // 4, c % 4
        xbat = x_prompt[b]
        tok = (np.arange(nblk)[:, None] * 128 + 32 * r + np.arange(32)[None, :]).reshape(-1)
        own_idx[c] = tok
        hal = (np.arange(nblk)[:, None] * 128 + 32 * r - 16 + np.arange(16)[None, :]).reshape(-1)
        xh = np.where((hal >= 0)[:, None], xbat[np.maximum(hal, 0)], 0.0).astype(np.float32)
        sl = slice(16 * c, 16 * c + 16)
        ptx = np.ascontiguousarray(np.repeat(pt[sl].T, 8, axis=0)).astype(np.int32)
        d = dict(shared)
        d.update(xb=xbat, x_own=np.ascontiguousarray(xbat[tok]), x_halo=np.ascontiguousarray(xh),
                 xs=np.ascontiguousarray(xsamp[sl]), win_c=np.ascontiguousarray(cw[sl]),
                 pool_c=np.ascontiguousarray(sp[sl]), ptx=ptx)
        d.update(_consts(r))
        in_maps.append(d)
    return in_maps, own_idx


def kernel(x_prompt, x_sample, cache_cmp_kv, cache_slc_kv, cache_win_kv, state_pool, page_table,
           norm_g, w_in, cmp_pe, cmp_w1, cmp_w2, pool_w, pool_scale, w_out, final_g):
    global _NC
    if _NC is None:
        _NC = build_program()
    in_maps, own_idx = _prep(x_prompt, x_sample, cache_cmp_kv, cache_slc_kv, cache_win_kv, state_pool, page_table,
                             norm_g, w_in, cmp_pe, cmp_w1, cmp_w2, pool_w, pool_scale, w_out, final_g)
    res = run_bass_kernel_spmd(_NC, in_maps, core_ids=list(range(8))).results
    y_p = np.zeros((2, 8192, D), np.float32)
    kvp = np.zeros((2, 8192, 768), np.float32)
    pool_p = np.zeros((1, 2, 15, 512), np.float32)
    for c in range(8):
        b, r = c // 4, c % 4
        y_p[b, own_idx[c]] = res[c]["y_own"]
        kvp[b, own_idx[c]] = res[c]["kv_own"]
        if r == 3:
            pool_p[0, b] = res[c]["pool_lastT"].reshape(128, 4, 16).transpose(1, 0, 2).reshape(512, 16).T[1:16]
    y_s = np.concatenate([res[c]["ys"] for c in range(8)], 0).reshape(128, 1, D)
    skv = np.concatenate([res[c]["skv"] for c in range(8)], 0)
    win_s = np.concatenate([res[c]["win_out"] for c in range(8)], 0).reshape(1, 128, 512, 2, 2, 64)
    pool_s = np.concatenate([res[c]["pool_out"] for c in range(8)], 0).reshape(1, 128, 15, 512)
    kv6 = lambda a: np.ascontiguousarray(a)
    return (y_p, y_s,
            kv6(kvp[:, :, 0:256]).reshape(1, 2, 8192, 2, 2, 64),
            kv6(kvp[:, :, 256:512]).reshape(1, 2, 8192, 2, 2, 64),
            kv6(kvp[:, 8192 - 512:, 512:768]).reshape(1, 2, 512, 2, 2, 64),
            pool_p,
            kv6(skv[:, 0:256]).reshape(1, 128, 1, 2, 2, 64),
            kv6(skv[:, 256:512]).reshape(1, 128, 1, 2, 2, 64),
            win_s, pool_s)
```

```python
from contextlib import ExitStack
import numpy as np
import ml_dtypes
import concourse.bass as bass
import concourse.mybir as mybir
from concourse.bass_utils import run_bass_kernel_spmd

F32 = mybir.dt.float32
BF16 = mybir.dt.bfloat16
I32 = mybir.dt.int32
AF = mybir.ActivationFunctionType
ALU = mybir.AluOpType
AX = mybir.AxisListType

NEG = -30000.0
DEBUG_NAMES = None
MMINFO = {}
D = 1024
PT = 2840
NM = 16
NSAMP = 16
NPHYS = 2560
NS_LOOP = 16
SKIP_SAMPLE = False
STAGE = 99
C_Q, C_CKV, C_SKV, C_WKV, C_GL, C_ZA, C_U, C_ZP = 0, 512, 768, 1024, 1280, 1304, 1816, 2328


def n_ct(m):
    return (32 * m + 30) // 128 + 1


CT_BASE = [sum(n_ct(i) for i in range(m)) for m in range(NM)]
N_CTT = sum(n_ct(i) for i in range(NM))


def cfg(nm, nsamp, nphys):
    global NM, NSAMP, NPHYS, CT_BASE, N_CTT, _NC
    NM, NSAMP, NPHYS = nm, nsamp, nphys
    CT_BASE = [sum(n_ct(i) for i in range(m)) for m in range(NM)]
    N_CTT = sum(n_ct(i) for i in range(NM))
    _NC = None


class Op:
    __slots__ = ("eng", "fn", "deps", "dma", "signal", "cnt", "sem", "idx", "guard", "src")


class Sched:
    ENG = ["pe", "act", "dve", "pool", "sp"]
    R = 14

    def __init__(self):
        self.ops = {e: [] for e in self.ENG}
        self.lastw = {}
        self.readers = {}

    def add(self, eng, fn, reads=(), writes=(), dma=False):
        op = Op()
        op.eng, op.fn, op.dma, op.signal, op.deps = eng, fn, dma, False, []
        op.cnt = op.sem = op.guard = None
        seen = set()

        def dep(o):
            if o is None or id(o) in seen:
                return
            seen.add(id(o))
            op.deps.append(o)

        for k in reads:
            dep(self.lastw.get(k))
            if isinstance(k, tuple) and k[0] in ("psg", "pacc", "pimp"):
                for r in self.readers.get(k, ()):
                    if r.eng != eng:
                        dep(r)
        for k in writes:
            dep(self.lastw.get(k))
            for r in self.readers.get(k, ()):
                dep(r)
        for k in reads:
            self.readers.setdefault(k, []).append(op)
        for k in writes:
            self.lastw[k] = op
            self.readers[k] = []
        op.idx = len(self.ops[eng])
        self.ops[eng].append(op)
        import sys
        fr = sys._getframe(1)
        chain = []
        while fr is not None and len(chain) < 4:
            chain.append(fr.f_lineno)
            fr = fr.f_back
        op.src = chain
        return op

    def finalize(self):
        for e in self.ENG:
            for op in self.ops[e]:
                best = {}
                dmas = []
                for d in op.deps:
                    if d.dma:
                        dmas.append(d)
                    else:
                        if d.eng == "pe" and op.eng == "pe" and not op.dma:
                            continue
                        if d.eng not in best or best[d.eng].idx < d.idx:
                            best[d.eng] = d
                op.deps = dmas + list(best.values())
                for d in op.deps:
                    d.signal = True
        for e in self.ENG:
            c = 0
            nd = 0
            for op in self.ops[e]:
                if op.dma:
                    op.sem = nd % self.R
                    op.cnt = 16 * (nd // self.R + 1)
                    op.guard = 16 * (nd // self.R)
                    nd += 1
                elif op.signal:
                    c += 1
                    op.cnt = c


def build_program():
    nc = bass.Bass("TRN2", target_bir_lowering=False)
    es = ExitStack()
    S = Sched()

    def din(name, shape, dt=F32):
        return nc.dram_tensor(name, list(shape), dt, kind="ExternalInput").ap()

    def dout(name, shape, dt=F32):
        return nc.dram_tensor(name, list(shape), dt, kind="ExternalOutput").ap()

    def sb(name, shape, dt):
        return es.enter_context(nc.sbuf_tensor(name, list(shape), dt))

    xb = din("xb", [NM * 512, D])
    x_own = din("x_own", [NM * 128, D])
    x_halo = din("x_halo", [NM * 64, D])
    xs = din("xs", [NSAMP, D])
    cache_cmp = din("cache_cmp", [NPHYS * 32, 1024])
    cache_slc = din("cache_slc", [NPHYS * 32, 1024])
    win_c = din("win_c", [NSAMP, 512 * 256])
    pool_c = din("pool_c", [NSAMP, 15, 512])
    ptx = din("ptx", [128, NSAMP], I32)
    norm_g = din("norm_g", [1, D])
    w_in = din("w_in", [D, PT])
    cmp_pe = din("cmp_pe", [64, 64])
    cmp_w1 = din("cmp_w1", [2, 32, 64, 128])
    cmp_w2 = din("cmp_w2", [2, 128, 64])
    pool_w = din("pool_w", [4, 128, 128])
    pool_scale = din("pool_scale", [1, 512])
    w_out = din("w_out", [D, D])
    final_g = din("final_g", [1, D])
    c_ident_bf = din("c_ident_bf", [128, 128], BF16)
    c_ident_f = din("c_ident_f", [128, 128])
    c_ident4 = din("c_ident4", [128, 512], BF16)
    c_triZ = din("c_triZ", [128, 4 * 128], BF16)
    c_winZ = din("c_winZ", [128, 8 * 128], BF16)
    c_cmpZ = din("c_cmpZ", [128, N_CTT * 128], BF16)
    c_cover = din("c_cover", [128, 4 * 128])
    c_cand = din("c_cand", [128, NM * 128], BF16)
    c_forced = din("c_forced", [128, NM * 128], BF16)
    c_poolcorr = din("c_poolcorr", [128, 4 * 128])
    c_cover33 = din("c_cover33", [128, 40])
    c_cand_s = din("c_cand_s", [16, 40])
    c_forced_s = din("c_forced_s", [16, 40])
    c_e16 = din("c_e16", [128, 64], BF16)
    c_sub8 = din("c_sub8", [128, 1])

    y_own = dout("y_own", [NM * 128, D])
    kv_own = dout("kv_own", [NM * 128, 768])
    pool_lastT = dout("pool_lastT", [128, 64])
    ys = dout("ys", [NSAMP, D])
    skv = dout("skv", [NSAMP, 768])
    win_out = dout("win_out", [NSAMP, 512 * 256])
    pool_out = dout("pool_out", [NSAMP, 15, 512])
    scr_o = nc.dram_tensor("scr_o", [NSAMP, 2, 3, 4, 65], F32, kind="Internal").ap()
    scr_i = nc.dram_tensor("scr_i", [NSAMP, 2, 4, 40], F32, kind="Internal").ap()

    W_bf = sb("W_bf", [128, 8, PT], BF16)
    W1_2 = sb("W1_2", [128, 2, 32, 128], BF16)
    w2p = sb("w2p", [128, 2, 2, 128], BF16)
    pw_bf = sb("pw_bf", [128, 4, 128], BF16)
    hbias = sb("hbias", [128, 2], F32)
    peT = sb("peT", [128, 64], BF16)
    g_col = sb("g_col", [128, 8], F32)
    fg_bc = sb("fg_bc", [128, D], F32)
    ident_bf = sb("ident_bf", [128, 128], BF16)
    ident_f = sb("ident_f", [128, 128], F32)
    ident4 = sb("ident4", [128, 512], BF16)
    triZ = sb("triZ", [128, 4, 128], BF16)
    winZ = sb("winZ", [128, 8, 128], BF16)
    cover = sb("cover", [128, 4, 128], F32)
    cover33 = sb("cover33", [128, 40], F32)
    cand_s = sb("cand_s", [16, 40], F32)
    forced_s = sb("forced_s", [16, 40], F32)
    e16 = sb("e16", [128, 16, 4], BF16)
    sub8 = sb("sub8", [128, 1], F32)
    cmpZ_t = [sb(f"cmpZ{i}", [128, 4, 128], BF16) for i in range(2)]
    cand_t = [sb(f"cand{i}", [128, 128], BF16) for i in range(2)]
    forced_t = [sb(f"forced{i}", [128, 128], BF16) for i in range(2)]
    Wo_t = [sb(f"Wo{i}", [128, D], BF16) for i in range(2)]

    KsT = sb("KsT", [128, 8192], BF16)
    Vs = sb("Vs", [128, 64, 2, 66], BF16)
    KwT = sb("KwT", [128, 8, 128], BF16)
    Vw = sb("Vw", [128, 8, 2, 66], BF16)
    CT = sb("CT", [128, 2, 528], BF16)
    kcT = sb("kcT", [128, 512], BF16)
    vcT = sb("vcT", [128, 512], BF16)
    vc_ext = sb("vc_ext", [128, 4, 2, 66], BF16)
    xst = [sb(f"xst{i}", [128, D], F32) for i in range(2)]
    xn_bf = sb("xn_bf", [128, D], BF16)
    ssq = sb("ssq", [128, 8], F32)
    xT_g = sb("xT_g", [128, 8, 512], BF16)
    xT_o = sb("xT_o", [128, 8, 128], BF16)
    xT_h = sb("xT_h", [128, 8, 64], BF16)
    a_act = sb("a_act", [128, 4, 128], BF16)
    qT = sb("qT", [128, 4, 128], BF16)
    uT = sb("uT", [128, 4, 4, 48], F32)
    uS = [sb(f"uS{i}", [128, 4, 48], F32) for i in range(2)]
    dT = sb("dT", [128, 4, 128], BF16)
    okv = sb("okv", [128, 768], F32)
    gate = sb("gate", [128, 24], F32)
    sza = sb("sza", [128, 512], F32)
    szp = sb("szp", [128, 512], F32)
    pT_f = [sb(f"pT_f{i}", [128, 4, 128], F32) for i in range(1)]
    pT_b = [sb(f"pT_b{i}", [128, 4, 128], BF16) for i in range(3)]
    Oall = sb("Oall", [128, 2, 3, 4, 65], F32)
    rdall = sb("rdall", [128, 2, 3, 4], F32)
    wgt = sb("wgt", [128, 2, 3, 4], F32)
    otmp = sb("otmp", [128, 2, 3, 4, 64], F32)
    osum = sb("osum", [128, 512], F32)
    mix_bf = sb("mix_bf", [128, D], BF16)
    mixT = sb("mixT", [128, 8, 128], BF16)
    xres = sb("xres", [128, D], F32)
    ypool_sb = sb("ypool_sb", [128, 512], F32)
    impt = sb("impt", [128, 128], F32)
    candt = sb("candt", [128, 128], F32)
    work = sb("work", [128, 128], F32)
    work2 = sb("work2", [128, 128], F32)
    mx8 = sb("mx8", [128, 16], F32)
    selt = sb("selt", [128, 128], F32)
    selneg = [sb(f"selneg{i}", [128, 128], BF16) for i in range(2)]
    selx_t = [sb(f"selx{i}", [128, 128], BF16) for i in range(4)]
    selx_s = sb("selx_s", [128, 2, 128], BF16)
    ptx_i = sb("ptx_i", [128, NSAMP], I32)
    ptx_f = sb("ptx_f", [128, NSAMP], F32)
    idx_i = sb("idx_i", [128, 4, NSAMP], I32)
    idx_f = sb("idx_f", [128, NSAMP], F32)
    Ps = sb("Ps", [16, 1280], F32)
    qT_s = sb("qT_s", [128, 4, 16], BF16)
    KsTn = sb("KsTn", [128, 16], BF16)
    vnew = sb("vnew", [1, NSAMP, 2, 66], BF16)
    vc_s = sb("vc_s", [128, 2, 66], BF16)
    pS_f = sb("pS_f", [128, 4], F32)
    pS_b = sb("pS_b", [128, 17, 4], BF16)
    pN_b = sb("pN_b", [1, 4], BF16)
    Osm = [sb(f"Osm{i}", [4, 2, 3, 65], F32) for i in range(2)]
    impst = [sb(f"impst{i}", [4, 2, 40], F32) for i in range(2)]
    impr = otmp[0:16, 0].rearrange("p b g d -> p (b g d)")[:, 0:320].rearrange("p (k g s) -> p k g s", k=2, g=4)
    seln_s = sb("seln_s", [16, 2, 40], BF16)
    sred = osum[0:16, :]
    d_s = mix_bf[0:16, 0:512]
    d_s_f = candt[0:16, 0:128]
    CTs = KsT[:, 4096:8192].rearrange("p (j t) -> p j t", j=2)
    CTs_keys = [("KsT", i) for i in range(32, 64)]
    cmp_bf = xT_g[:].rearrange("p k t -> p (k t)").rearrange("p (e c) -> p e c", e=16)
    slck_bf = xT_g[:, 4:8, :].rearrange("p k t -> p (k t)").rearrange("p (e c) -> p e c", e=16)

    PSUM = [es.enter_context(nc.psum_tensor(f"ps{i}", [128, 512], F32)) for i in range(8)]

    fresh = {}

    class Ring:
        def __init__(self, name, tiles):
            self.name, self.tiles, self.i = name, tiles, 0

        def next(self):
            k = self.i % len(self.tiles)
            self.i += 1
            if self.name in ("psg", "pacc", "pimp"):
                fresh[self.tiles[k].name] = True
            return self.tiles[k], (self.name, k)

    psg = Ring("psg", PSUM[0:4])
    pacc = Ring("pacc", PSUM[4:7])
    pimp_r = Ring("pimp", PSUM[7:8])
    xst_r = Ring("xst", xst)
    pTf_r = Ring("pTf", pT_f)
    pTb_r = Ring("pTb", pT_b)
    pTf_state = [0]

    def pTf_next():
        k = pTf_state[0] % 2
        pTf_state[0] += 1
        if k == 0:
            return pT_f[0][:], ("pTf", 0)
        return okv[:, 0:512].rearrange("p (g q) -> p g q", g=4), "okv"

    pTb_state = [0]

    def pTb_next():
        k = pTb_state[0] % 4
        pTb_state[0] += 1
        if k < 3:
            return pT_b[k], ("pTb", k)
        return a_act, "a_act"
    seln_r = Ring("seln", selneg)
    selx_r = Ring("selx", selx_t)
    uS_r = Ring("uS", uS)
    cmpZ_r = Ring("cmpZ", cmpZ_t)
    cand_r = Ring("cand", cand_t)
    forced_r = Ring("forced", forced_t)
    Wo_r = Ring("Wo", Wo_t)
    Osm_r = Ring("Osm", Osm)
    impst_r = Ring("impst", impst)
    xgk = "xT_g"

    def mm(out, lhsT, rhs, start, stop, rd, wr):
        nm_ = out.tensor.name
        st_ = fresh.get(nm_, True)
        fresh[nm_] = False
        op = S.add("pe", lambda e: e.matmul(out, lhsT, rhs, start=st_, stop=stop, skip_group_check=True), rd, wr)
        bp = lhsT.base_partition()
        MMINFO[id(op)] = (bp, lhsT.shape[0], nm_)

    def act(out, in_, func, rd, wr, scale=1.0, bias=0.0, accum=None):
        if accum is None:
            S.add("act", lambda e: e.activation(out, in_, func, bias=bias, scale=scale), rd, wr)
        else:
            S.add("act", lambda e: e.activation(out, in_, func, bias=bias, scale=scale, accum_out=accum), rd, wr)

    def cp(eng, out, in_, rd, wr):
        if eng == "act":
            S.add("act", lambda e: e.copy(out, in_), rd, wr)
        else:
            S.add(eng, lambda e: e.tensor_copy(out, in_), rd, wr)

    def tt(eng, out, a, b, op, rd, wr):
        S.add(eng, lambda e: e.tensor_tensor(out, a, b, op), rd, wr)

    def ts(eng, out, a, s1, s2, op0, op1, rd, wr):
        if eng == "act":
            assert s2 is None and op0 == ALU.mult
            S.add("act", lambda e: e.activation(out, a, AF.Copy, scale=s1), rd, wr)
            return
        if s2 is None:
            S.add(eng, lambda e: e.tensor_scalar(out, a, s1, None, op0), rd, wr)
        else:
            S.add(eng, lambda e: e.tensor_scalar(out, a, s1, s2, op0, op1), rd, wr)

    def stt(eng, out, a, sc, b, op0, op1, rd, wr):
        S.add(eng, lambda e: e.scalar_tensor_tensor(out, a, sc, b, op0, op1), rd, wr)

    def memset(eng, ap, v, wr):
        S.add(eng, lambda e: e.memset(ap, v), (), wr)

    def dma(out, in_, rd, wr, q="sp", slow=False):
        if slow:
            S.add(q, lambda e: e.dma_start(out=out, in_=in_, allow_slow_non_contiguous=True), rd, wr, dma=True)
        else:
            S.add(q, lambda e: e.dma_start(out=out, in_=in_), rd, wr, dma=True)

    alt = [0]

    def evac(out, in_, rd, wr):
        alt[0] ^= 1
        cp("act" if alt[0] else "dve", out, in_, rd, wr)

    K = lambda t: t.name

    for t, src, rs, kw in [
        (ident_bf, c_ident_bf, None, {}), (ident_f, c_ident_f, None, {}), (ident4, c_ident4, None, {}),
        (triZ, c_triZ, "p (a b) -> p a b", dict(a=4)), (winZ, c_winZ, "p (a b) -> p a b", dict(a=8)),
        (cover, c_cover, "p (a b) -> p a b", dict(a=4)), (cover33, c_cover33, None, {}),
        (cand_s, c_cand_s, None, {}), (forced_s, c_forced_s, None, {}),
        (e16, c_e16, "p (a b) -> p a b", dict(a=16)), (sub8, c_sub8, None, {}), (ptx_i, ptx, None, {}),
    ]:
        s_ap = src.rearrange(rs, **kw) if rs else src
        dma(t[:], s_ap, (), [K(t)])
    dma(fg_bc[:], final_g[0:1, :].partition_broadcast(128).rearrange("p a d -> p (a d)"), (), [K(fg_bc)])
    dma(g_col[:], norm_g.rearrange("o (k p) -> p (o k)", p=128), (), ["g_col"], slow=True)

    ci = [0]

    def cast_eng():
        ci[0] += 1
        return ["act", "dve"][ci[0] % 2]

    class SetupRing:
        def __init__(self):
            self.t = [(xst[0], ("xst", 0)), (xst[1], ("xst", 1)), (xres, "xres"),
                      (otmp[:].rearrange("p k b g d -> p (k b g d)"), "otmp"),
                      (Oall[:].rearrange("p k b g e -> p (k b g e)"), "Oall")]
            self.i = 0

        def next(self):
            r = self.t[self.i % len(self.t)]
            self.i += 1
            return r

    sstg = SetupRing()
    for kc in range(8):
        for c0, c1 in [(0, 1024), (1024, 2048), (2048, PT)]:
            st, sk = sstg.next()
            dma(st[:, 0:c1 - c0], w_in[kc * 128:(kc + 1) * 128, c0:c1], (), [sk])
            if c0 == 0:
                ts(cast_eng(), W_bf[:, kc, 0:512].rearrange("p (g k d) -> p k g d", g=4, k=2),
                   st[:, 0:512].rearrange("p (k g d) -> p k g d", k=2, g=4), g_col[:, kc:kc + 1], None, ALU.mult, None,
                   [sk, "g_col"], [("W", kc)])
                ts(cast_eng(), W_bf[:, kc, 512:1024], st[:, 512:1024], g_col[:, kc:kc + 1], None, ALU.mult, None,
                   [sk, "g_col"], [("W", kc)])
            else:
                ts(cast_eng(), W_bf[:, kc, c0:c1], st[:, 0:c1 - c0], g_col[:, kc:kc + 1], None, ALU.mult, None,
                   [sk, "g_col"], [("W", kc)])
    for j in range(2):
        for iq in range(4):
            st, sk = sstg.next()
            src = cmp_w1[j, iq * 8:(iq + 1) * 8].rearrange("i d h -> d i h")
            stv = st[:, 0:1024].rearrange("p (i h) -> p i h", i=8)
            dma(stv[0:64], src, (), [sk])
            dma(stv[64:128], src, (), [sk])
            cp(cast_eng(), W1_2[:, j, iq * 8:(iq + 1) * 8, :], stv, [sk], ["W1"])
    memset("pool", w2p[:], 0.0, ["w2p"])
    st, sk = xst_r.next()
    dma(st[:, 0:128].rearrange("p (j d) -> p j d", j=2), cmp_w2.rearrange("j h d -> h j d"), (), [sk])
    for j in range(2):
        cp("dve", w2p[:, j, 0, 0:64], st[:, j * 64:(j + 1) * 64], [sk], ["w2p"])
        cp("dve", w2p[:, j, 1, 64:128], st[:, j * 64:(j + 1) * 64], [sk], ["w2p"])
    st, sk = xst_r.next()
    dma(st[:, 0:512].rearrange("p (g d) -> p g d", g=4), pool_w.rearrange("g c d -> c g d"), (), [sk])
    dma(st[:, 512:1024], pool_scale[0:1, :].partition_broadcast(128).rearrange("p a d -> p (a d)"), (), [sk])
    tt("dve", pw_bf[:].rearrange("p g d -> p (g d)"), st[:, 0:512], st[:, 512:1024], ALU.mult, [sk], ["pw"])
    st, sk = xst_r.next()
    dma(st[0:64, 0:64], cmp_pe, (), [sk])
    pt_, pk = psg.next()
    mm(pt_[0:64, 0:64], st[0:64, 0:64], ident_f[0:64, 0:64], True, True, [sk, K(ident_f)], [pk])
    cp("dve", peT[0:64, :], pt_[0:64, 0:64], [pk], ["peT"])
    pt_, pk = psg.next()
    for j in range(2):
        for i in range(32):
            mm(pt_[:, j:j + 1], W1_2[0:64, j, i, :], peT[0:64, j * 32 + i:j * 32 + i + 1], i == 0, i == 31,
               ["W1", "peT"], [pk])
    cp("dve", hbias[:], pt_[:, 0:2], [pk], ["hbias"])
    memset("pool", Vs[:], 1.0, [("Vs", i) for i in range(64)])
    memset("pool", Vw[:], 1.0, [("Vw", i) for i in range(8)])
    memset("pool", vc_ext[:], 1.0, ["vc_ext"])
    memset("pool", vc_s[:], 1.0, ["vc_s"])
    memset("pool", vnew[:], 1.0, ["vnew"])
    memset("pool", kcT[:], 0.0, ["kcT"])
    memset("pool", vcT[:], 0.0, ["vcT"])
    memset("pool", CT[:], 0.0, ["CT"])

    def norm_xT(src, rows, dst_fn, dkey):
        xt, xk = xst_r.next()
        dma(xt[0:rows, :], src, (), [xk])
        memset("dve", ssq[0:rows, 0:1], 0.0, ["ssq"])
        act(xn_bf[0:rows, :], xt[0:rows, :], AF.Square, [xk], ["xn", "ssq"], accum=ssq[0:rows, 0:1])
        ts("dve", ssq[0:rows, 1:2], ssq[0:rows, 0:1], 1.0 / D, 1e-6, ALU.mult, ALU.add, ["ssq"], ["ssq"])
        act(ssq[0:rows, 3:4], ssq[0:rows, 1:2], AF.Ln, ["ssq"], ["ssq"])
        act(ssq[0:rows, 2:3], ssq[0:rows, 3:4], AF.Exp, ["ssq"], ["ssq"], scale=-0.5)
        ts("dve", xn_bf[0:rows, :], xt[0:rows, :], ssq[0:rows, 2:3], None, ALU.mult, None, [xk, "ssq"], ["xn"])
        for h in range(2):
            pt, pk = psg.next()
            for kk in range(4):
                kc = h * 4 + kk
                mm(pt[:, kk * rows:(kk + 1) * rows], xn_bf[0:rows, kc * 128:(kc + 1) * 128], ident_bf[0:rows, 0:rows],
                   True, True, ["xn", K(ident_bf)], [pk])
            evac(dst_fn(h * 4, 4), pt[:, 0:4 * rows].rearrange("p (k r) -> p k r", k=4), [pk], [dkey])

    def normA(g, b):
        blk = 4 * g + b
        norm_xT(xb[blk * 128:(blk + 1) * 128, :], 128,
                lambda k0, n, b=b: xT_g[:, k0:k0 + n, b * 128:(b + 1) * 128], xgk)

    def normB(m, which):
        if which == 0:
            norm_xT(x_own[m * 128:(m + 1) * 128, :], 128, lambda k0, n: xT_o[:, k0:k0 + n, :], "xT_o")
        else:
            norm_xT(x_halo[m * 64:(m + 1) * 64, :], 64, lambda k0, n: xT_h[:, k0:k0 + n, :], "xT_h")

    def phaseA(g, do_norm=True):
        xg = xT_g
        if do_norm:
            for b in range(4):
                normA(g, b)
        if STAGE <= 0.3:
            return
        for c0, kind in [(C_SKV, "ks"), (C_WKV, "kw"), (C_CKV, "ck"), (C_CKV + 128, "cv")]:
            pt, pk = psg.next()
            for kc in range(8):
                mm(pt[:, :], W_bf[:, kc, c0:c0 + 128], xg[:, kc, :], kc == 0, kc == 7, [("W", kc), xgk], [pk])
            if kind == "ks":
                evac(KsT[:, 512 * g:512 * g + 512], pt[:, :], [pk], [("KsT", 4 * g + b) for b in range(4)])
            elif kind == "kw":
                s0 = (4 * g) % 8
                evac(KwT[:, s0:s0 + 4, :], pt[:, :].rearrange("p (a b) -> p a b", a=4), [pk],
                     [("KwT", s0 + b) for b in range(4)])
            else:
                evac(CT[:, 0 if kind == "ck" else 1, 16:528], pt[:, :], [pk], ["CT"])
        if STAGE <= 0.4:
            return
        for b in range(4):
            blk = 4 * g + b
            pt, pk = psg.next()
            for kc in range(8):
                mm(pt[:, 0:128], xg[:, kc, b * 128:(b + 1) * 128], W_bf[:, kc, C_SKV + 128:C_SKV + 256],
                   kc == 0, kc == 7, [("W", kc), xgk], [pk])
                mm(pt[:, 128:256], xg[:, kc, b * 128:(b + 1) * 128], W_bf[:, kc, C_WKV + 128:C_WKV + 256],
                   kc == 0, kc == 7, [("W", kc), xgk], [pk])
            if STAGE <= 0.5:
                continue
            evac(Vs[:, blk, :, 0:64], pt[:, 0:128].rearrange("p (k d) -> p k d", k=2), [pk], [("Vs", blk)])
            if STAGE <= 0.55:
                continue
            evac(Vw[:, blk % 8, :, 0:64], pt[:, 128:256].rearrange("p (k d) -> p k d", k=2), [pk], [("Vw", blk % 8)])
        if STAGE <= 0.6:
            return
        n0 = 1 if g == 0 else 0
        nn = 32 - n0
        pts = [psg.next(), psg.next()]
        for kv in range(2):
            P0 = kv * 64
            pt, pk = pts[kv]
            for j in range(2):
                for i in range(32):
                    rhs = CT[P0:P0 + 64, j, i + 16 * n0:i + 16 * n0 + 16 * (nn - 1) + 1:16]
                    mm(pt[:, j * 32 + n0:j * 32 + 32], W1_2[P0:P0 + 64, j, i, :], rhs,
                       i == 0, i == 31, ["W1", "CT"], [pk])
        if STAGE <= 0.7:
            return
        for kv in range(2):
            pt, pk = pts[kv]
            for j in range(2):
                act(a_act[:, j * 2 + kv, n0:32], pt[:, j * 32 + n0:j * 32 + 32],
                    AF.Silu, [pk, "hbias"], ["a_act"], bias=hbias[:, j:j + 1])
        if STAGE <= 0.8:
            return
        cbase = 32 * g - 1
        for j, dst, dk in [(0, kcT, "kcT"), (1, vcT, "vcT")]:
            pt2, pk2 = psg.next()
            for kv in range(2):
                mm(pt2[:, n0:32], w2p[:, j, kv, :], a_act[:, j * 2 + kv, n0:32], kv == 0, kv == 1,
                   ["w2p", "a_act"], [pk2])
            evac(dst[:, cbase + n0:cbase + 32], pt2[:, n0:32], [pk2], [dk])
        cp("dve", CT[:, :, 0:16], CT[:, :, 512:528], ["CT"], ["CT"])

    def topk(rows, ncol, imp_ap, cand_ap, forced_ap, out_ap, rd, wr):
        R = slice(0, rows)
        tt("dve", candt[R, 0:ncol], imp_ap, cand_ap, ALU.mult, rd, ["candt"])
        S.add("dve", lambda e: e.max(out=mx8[R, 0:8], in_=candt[R, 0:ncol]), ["candt"], ["mx8"])
        S.add("dve", lambda e: e.match_replace(out=work[R, 0:ncol], in_to_replace=mx8[R, 0:8],
                                               in_values=candt[R, 0:ncol], imm_value=0.0), ["candt", "mx8"], ["work"])
        S.add("dve", lambda e: e.max(out=mx8[R, 8:16], in_=work[R, 0:ncol]), ["work"], ["mx8"])
        memset("dve", mx8[R, 13:16], 0.0, ["mx8"])
        S.add("dve", lambda e: e.match_replace(out=work2[R, 0:ncol], in_to_replace=mx8[R, 8:16],
                                               in_values=work[R, 0:ncol], imm_value=0.0), ["work", "mx8"], ["work2"])
        tt("dve", selt[R, 0:ncol], candt[R, 0:ncol], work2[R, 0:ncol], ALU.not_equal, ["candt", "work2"], ["selt"])
        tt("dve", selt[R, 0:ncol], selt[R, 0:ncol], forced_ap, ALU.max, ["selt"] + list(rd), ["selt"])
        ts("dve", out_ap, selt[R, 0:ncol], 1.0, -NEG, ALU.subtract, ALU.mult, ["selt"], wr)

    def tail(rows, xsrc_dram, ypool_ap, ypk, dst_dram):
        R = slice(0, rows)
        ts("dve", rdall[R], Oall[R, :, :, :, 64], 1e-30, None, ALU.add, None, ["Oall"], ["rdall"])
        S.add("dve", lambda e: e.reciprocal(rdall[R], rdall[R]), ["rdall"], ["rdall"])
        tt("dve", wgt[R], rdall[R], gate[R, :].rearrange("p (k g b) -> p k b g", k=2, g=4), ALU.mult,
           ["rdall", "gate"], ["wgt"])
        tt("dve", otmp[R], Oall[R, :, :, :, 0:64], wgt[R].unsqueeze(4).to_broadcast([rows, 2, 3, 4, 64]), ALU.mult,
           ["Oall", "wgt"], ["otmp"])
        ov = osum[R, :].rearrange("p (k g d) -> p k g d", k=2, g=4)
        tt("dve", ov, otmp[R, :, 0], otmp[R, :, 1], ALU.add, ["otmp"], ["osum"])
        tt("dve", ov, ov, otmp[R, :, 2], ALU.add, ["otmp", "osum"], ["osum"])
        tt("dve", mix_bf[R, 0:512], osum[R, :], sza[R, :], ALU.mult, ["osum", "sza"], ["mix"])
        tt("dve", mix_bf[R, 512:1024], ypool_ap, szp[R, :], ALU.mult, [ypk, "szp"], ["mix"])
        for h in range(2):
            pt, pk = psg.next()
            for kk in range(4):
                kc = h * 4 + kk
                mm(pt[:, kk * rows:(kk + 1) * rows], mix_bf[R, kc * 128:(kc + 1) * 128], ident_bf[R, 0:rows],
                   True, True, ["mix", K(ident_bf)], [pk])
            evac(mixT[:, h * 4:h * 4 + 4, 0:rows], pt[:, 0:4 * rows].rearrange("p (k r) -> p k r", k=4), [pk], ["mixT"])
        p0, p0k = psg.next()
        p1, p1k = psg.next()
        wst = [(xst[0], ("xst", 0)), (xst[1], ("xst", 1)),
               (otmp[:].rearrange("p k b g d -> p (k b g d)"), "otmp"),
               (Oall[:].rearrange("p k b g e -> p (k b g e)"), "Oall")]
        for kc in range(8):
            st, sk = wst[kc % 4]
            dma(st[:, 0:D], w_out[kc * 128:(kc + 1) * 128, :], (), [sk])
            wo, wok = Wo_r.next()
            cp(cast_eng(), wo[:], st[:, 0:D], [sk], [wok])
            mm(p0[R, :], mixT[:, kc, 0:rows], wo[:, 0:512], kc == 0, kc == 7, ["mixT", wok], [p0k])
            mm(p1[R, :], mixT[:, kc, 0:rows], wo[:, 512:1024], kc == 0, kc == 7, ["mixT", wok], [p1k])
        st, sk = xst_r.next()
        dma(st[R, :], xsrc_dram, (), [sk])
        tt("dve", xres[R, 0:512], p0[R, :], st[R, 0:512], ALU.add, [p0k, sk], ["xres"])
        tt("dve", xres[R, 512:1024], p1[R, :], st[R, 512:1024], ALU.add, [p1k, sk], ["xres"])
        memset("dve", ssq[R, 4:5], 0.0, ["ssq2"])
        act(xn_bf[R, :], xres[R, :], AF.Square, ["xres"], ["xn", "ssq2"], accum=ssq[R, 4:5])
        ts("dve", ssq[R, 5:6], ssq[R, 4:5], 1.0 / D, 1e-6, ALU.mult, ALU.add, ["ssq2"], ["ssq2"])
        act(ssq[R, 7:8], ssq[R, 5:6], AF.Ln, ["ssq2"], ["ssq2"])
        act(ssq[R, 6:7], ssq[R, 7:8], AF.Exp, ["ssq2"], ["ssq2"], scale=-0.5)
        stt("dve", xres[R, :], xres[R, :], ssq[R, 6:7], fg_bc[R, :], ALU.mult, ALU.mult,
            ["xres", "ssq2", K(fg_bc)], ["xres"])
        dma(dst_dram, xres[R, :], ["xres"], ["ydram"])

    def tokmajor_proj(rows, xT_ap, xTk, kvdst, kvk):
        R = slice(0, rows)
        for c0, c1 in [(512, 1024), (1024, 1536), (1536, 1816), (2328, 2840)]:
            pt, pk = psg.next()
            for kc in range(8):
                mm(pt[R, 0:c1 - c0], xT_ap(kc), W_bf[:, kc, c0:c1], kc == 0, kc == 7, [("W", kc), xTk], [pk])
            if c0 == 512:
                cp("dve", kvdst[R, 0:512], pt[R, 0:512], [pk], [kvk])
            elif c0 == 1024:
                cp("dve", kvdst[R, 512:768], pt[R, 0:256], [pk], [kvk])
                act(gate[R, :], pt[R, 256:280], AF.Sigmoid, [pk], ["gate"])
                act(sza[R, 0:232], pt[R, 280:512], AF.Silu, [pk], ["sza"])
            elif c0 == 1536:
                act(sza[R, 232:512], pt[R, 0:280], AF.Silu, [pk], ["sza"])
            else:
                act(szp[R, :], pt[R, 0:512], AF.Silu, [pk], ["szp"])

    def qT_proj(xT_ap, xTk, n, dst, dk):
        pt, pk = psg.next()
        for gq in range(4):
            for kc in range(8):
                lhsT = W_bf[:, kc, gq * 128:(gq + 1) * 128]
                mm(pt[:, gq * n:(gq + 1) * n], lhsT, xT_ap(kc), kc == 0, kc == 7, [("W", kc), xTk], [pk])
        evac(dst, pt[:, 0:4 * n].rearrange("p (g n) -> p g n", g=4), [pk], [dk])

    def phaseB(m, do_norm=True, prefetch=()):
        nct = n_ct(m)
        cz, czk = cmpZ_r.next()
        dma(cz[:, 0:nct, :], c_cmpZ[:, CT_BASE[m] * 128:(CT_BASE[m] + nct) * 128].rearrange("p (a b) -> p a b", a=nct),
            (), [czk])
        cd, cdk = cand_r.next()
        dma(cd[:], c_cand[:, m * 128:(m + 1) * 128], (), [cdk])
        fo, fok = forced_r.next()
        dma(fo[:], c_forced[:, m * 128:(m + 1) * 128], (), [fok])
        if do_norm:
            normB(m, 0)
            normB(m, 1)
        tokmajor_proj(128, lambda kc: xT_o[:, kc, :], "xT_o", okv, "okv")
        dma(kv_own[m * 128:(m + 1) * 128, :], okv[:], ["okv"], ["kvdram"])
        qT_proj(lambda kc: xT_o[:, kc, :], "xT_o", 128, qT[:], "qT")
        pt, pk = psg.next()
        for gp in range(4):
            for kc in range(8):
                mm(pt[:, gp * 128:(gp + 1) * 128], W_bf[:, kc, C_U + gp * 128:C_U + (gp + 1) * 128], xT_o[:, kc, :],
                   kc == 0, kc == 7, [("W", kc), "xT_o"], [pk])
        cp("dve", uT[:, :, :, 16:48], pt[:, :].rearrange("p (g a i) -> p g a i", g=4, a=4), [pk], ["uT"])
        pt, pk = psg.next()
        for gp in range(4):
            for kc in range(8):
                mm(pt[:, gp * 64:(gp + 1) * 64], W_bf[:, kc, C_U + gp * 128:C_U + (gp + 1) * 128], xT_h[:, kc, :],
                   kc == 0, kc == 7, [("W", kc), "xT_h"], [pk])
        cp("dve", uT[:, :, :, 0:16], pt[:, 0:256].rearrange("p (g a i) -> p g a i", g=4, a=4), [pk], ["uT"])
        if m == NM - 1:
            dma(pool_lastT.rearrange("p (g i) -> p g i", g=4), uT[:, :, 3, 32:48], ["uT"], ["pldram"])
        if m == 0:
            stc, stck = xst_r.next()
            dma(stc[:, 0:512], c_poolcorr, (), [stck])
        for gp in range(4):
            w = 2 << gp
            cur, ck = uT[:, gp], "uT"
            sh = 1
            lo = 0
            while sh < w:
                nt, nk = uS_r.next()
                lo += sh
                tt("pool", nt[:, :, lo:48], cur[:, :, lo:48], cur[:, :, lo - sh:48 - sh], ALU.add, [ck], [nk])
                cur, ck = nt, nk
                sh *= 2
            if m == 0:
                tt("pool", cur[:, :, 16:48], cur[:, :, 16:48],
                   stc[:, gp * 128:(gp + 1) * 128].rearrange("p (a i) -> p a i", a=4), ALU.mult, [ck, stck], [ck])
            stt("dve", dT[:, gp, :].rearrange("p (a i) -> p a i", a=4), cur[:, :, 16:48], 1.0 / w, uT[:, gp, :, 16:48],
                ALU.mult, ALU.subtract, [ck, "uT"], ["dT"])
        yp, ypk = psg.next()
        for gp in range(4):
            mm(yp[:, gp * 128:(gp + 1) * 128], dT[:, gp, :], pw_bf[:, gp, :], True, True, ["dT", "pw"], [ypk])
        cp("dve", ypool_sb[:, :], yp[:, :], [ypk], ["ypool"])
        for ct in range(nct):
            pt, pk = psg.next()
            mm(pt[:, 0:128], vcT[:, ct * 128:(ct + 1) * 128], ident_bf[:], True, True, ["vcT", K(ident_bf)], [pk])
            evac(vc_ext[:, ct, :, 0:64], pt[:, 0:128].rearrange("p (k d) -> p k d", k=2), [pk], ["vc_ext"])
        cmpS, winS, slcS = [[], []], [[], []], [[], []]
        st_cmp_all = [{}, {}]
        for kv in range(2):
            P0 = kv * 64
            Pq = slice(P0, P0 + 64)
            qall = qT[Pq, :, :].rearrange("p g q -> p (g q)")
            st_cmp, st_win, st_slc = st_cmp_all[kv], {}, {}

            def acc_tile(st):
                if "pa" not in st:
                    st["pa"], st["pak"] = pacc.next()
                    st["pav"] = st["pa"][:, 0:260].rearrange("p (g e) -> p g e", g=4)
                return st["pav"], st["pak"]

            for ct in range(nct):
                def s1(ct=ct, Pq=Pq, qall=qall, st=st_cmp):
                    pt, pk = psg.next()
                    mm(pt[:, :], kcT[Pq, ct * 128:(ct + 1) * 128], qall, True, False, ["kcT", "qT"], [pk])
                    for gq in range(4):
                        mm(pt[:, gq * 128:(gq + 1) * 128], ident_bf[:], cz[:, ct, :], False, True,
                           [K(ident_bf), czk], [pk])
                    pf, pfk = pTf_next()
                    act(pf.rearrange("p g q -> p (g q)"), pt[:, :], AF.Exp, [pk], [pfk], scale=0.125)
                    pb, pbk = pTb_next()
                    cp("dve", pb[:], pf, [pfk], [pbk])
                    st[ct] = (pf, pfk, pb, pbk)

                def s2(ct=ct, kv=kv, st=st_cmp, m=m, acc_tile=acc_tile):
                    pf, pfk, pb, pbk = st[ct]
                    pav, pak = acc_tile(st)
                    if "pimp" not in st:
                        st["pimp"] = pimp_r.next()
                    pimp_t, pimp_k = st["pimp"]
                    for gq in range(4):
                        mm(pav[:, gq, :], pb[:, gq, :], vc_ext[:, ct, kv, 0:65], ct == 0, ct == nct - 1, [pbk, "vc_ext"], [pak])
                    for gq in range(4):
                        mm(pimp_t[:, gq * 128:(gq + 1) * 128], pf[:, gq, :], cover[:, ct, :], ct == 0, ct == nct - 1,
                           [pfk, K(cover)], [pimp_k])
                    if ct == nct - 1:
                        cp("dve", Oall[:, kv, 0], pav, [pak], ["Oall"])
                        ts("dve", rdall[:, kv, 0], Oall[:, kv, 0, :, 64], 1e-30, None, ALU.add, None, ["Oall"], ["rdall"])
                        S.add("dve", lambda e: e.reciprocal(rdall[:, kv, 0], rdall[:, kv, 0]), ["rdall"], ["rdall"])
                        ts("dve", impt[:], pimp_t[:, 0:128], rdall[:, kv, 0, 0:1], None, ALU.mult, None,
                           [pimp_k, "rdall"], ["impt"])
                        for gq in range(1, 4):
                            stt("dve", impt[:], pimp_t[:, gq * 128:(gq + 1) * 128], rdall[:, kv, 0, gq:gq + 1], impt[:],
                                ALU.mult, ALU.add, [pimp_k, "rdall", "impt"], ["impt"])
                        sn, snk = seln_r.next()
                        topk(128, 128, impt[:], cd[:], fo[:], sn[:], ["impt", cdk, fok], [snk])
                        st["sn"] = (sn, snk)
                cmpS[kv].append((s1, s2))
            jlist = [jj for jj in range(8) if 4 * m - 4 + jj >= 0]
            for jj in jlist:
                def s1a(jj=jj, Pq=Pq, qall=qall, st=st_win):
                    kt = 4 * m - 4 + jj
                    pt, pk = psg.next()
                    mm(pt[:, :], KwT[Pq, kt % 8, :], qall, True, False, [("KwT", kt % 8), "qT"], [pk])
                    st[("pt", jj)] = (pt, pk)

                def s1b(jj=jj, st=st_win):
                    pt, pk = st[("pt", jj)]
                    for gq in range(4):
                        mm(pt[:, gq * 128:(gq + 1) * 128], ident_bf[:], winZ[:, jj, :], False, True,
                           [K(ident_bf), K(winZ)], [pk])
                    pb, pbk = pTb_next()
                    act(pb[:].rearrange("p g q -> p (g q)"), pt[:, :], AF.Exp, [pk], [pbk], scale=0.125)
                    st[jj] = (pb, pbk)

                def s2(jj=jj, kv=kv, st=st_win, jlist=jlist, acc_tile=acc_tile):
                    kt = 4 * m - 4 + jj
                    pb, pbk = st[jj]
                    pav, pak = acc_tile(st)
                    for gq in range(4):
                        mm(pav[:, gq, :], pb[:, gq, :], Vw[:, kt % 8, kv, 0:65], jj == jlist[0], jj == jlist[-1],
                           [pbk, ("Vw", kt % 8)], [pak])
                    if jj == jlist[-1]:
                        cp("dve", Oall[:, kv, 2], pav, [pak], ["Oall"])
                winS[kv].append((s1a, s1b, s2))
            nkt = 4 * m + 4
            for kt in range(nkt):
                def s1a(kt=kt, Pq=Pq, qall=qall, st=st_slc, stc=st_cmp):
                    sn, snk = stc["sn"]
                    pt, pk = psg.next()
                    sx, sxk = selx_r.next()
                    cp("dve", sx[:].rearrange("p (b k) -> p b k", b=2),
                       sn[:, 2 * kt:2 * kt + 2].unsqueeze(2).to_broadcast([128, 2, 64]), [snk], [sxk])
                    mm(pt[:, :], KsT[Pq, kt * 128:(kt + 1) * 128], qall, True, False, [("KsT", kt), "qT"], [pk])
                    st[("pt", kt)] = (pt, pk, sx, sxk)

                def s1b(kt=kt, st=st_slc):
                    pt, pk, sx, sxk = st[("pt", kt)]
                    mm(pt[:, :], sx[:], ident4[:], False, kt < 4 * m, [sxk, K(ident4)], [pk])
                    if kt >= 4 * m:
                        for gq in range(4):
                            mm(pt[:, gq * 128:(gq + 1) * 128], ident_bf[:], triZ[:, kt - 4 * m, :], False, True,
                               [K(ident_bf), K(triZ)], [pk])
                    pb, pbk = pTb_next()
                    act(pb[:].rearrange("p g q -> p (g q)"), pt[:, :], AF.Exp, [pk], [pbk], scale=0.125)
                    st[kt] = (pb, pbk)

                def s2(kt=kt, kv=kv, st=st_slc, nkt=nkt, acc_tile=acc_tile):
                    pb, pbk = st[kt]
                    pav, pak = acc_tile(st)
                    for gq in range(4):
                        mm(pav[:, gq, :], pb[:, gq, :], Vs[:, kt, kv, 0:65], kt == 0, kt == nkt - 1, [pbk, ("Vs", kt)], [pak])
                    if kt == nkt - 1:
                        cp("dve", Oall[:, kv, 1], pav, [pak], ["Oall"])
                slcS[kv].append((s1a, s1b, s2))

        def pair(a, b):
            return (lambda: (a[0](), b[0](), a[1](), b[1]()), lambda: (a[2](), b[2]()))

        stages = cmpS[0] + cmpS[1]
        stages += [pair(winS[0][i], winS[1][i]) for i in range(len(winS[0]))]
        stages += [pair(slcS[0][i], slcS[1][i]) for i in range(len(slcS[0]))]
        SK = 1
        npf = len(prefetch)
        inject = {max(0, len(stages) - 3 * (npf - j)): j for j in range(npf)}
        done_pf = set()
        for i in range(len(stages) + SK):
            if i < len(stages):
                stages[i][0]()
            if i - SK >= 0:
                stages[i - SK][1]()
            if i in inject:
                for j in range(npf):
                    if j not in done_pf and inject.get(i) is not None and j <= inject[i]:
                        prefetch[j]()
                        done_pf.add(j)
        for j in range(npf):
            if j not in done_pf:
                prefetch[j]()

    if STAGE == 0:
        S.finalize()
        return _emit(nc, es, S)
    phaseA(0)
    for g in range(NM):
        pf = []
        if g + 1 < NM:
            pf = [lambda b=b, g=g: normA(g + 1, b) for b in range(4)] + [lambda g=g: normB(g + 1, 0), lambda g=g: normB(g + 1, 1)]
        phaseB(g, do_norm=(g == 0), prefetch=pf)
        if g + 1 < NM:
            phaseA(g + 1, do_norm=False)
        tail(128, x_own[g * 128:(g + 1) * 128, :], ypool_sb[:, :], "ypool", y_own[g * 128:(g + 1) * 128, :])

    if SKIP_SAMPLE:
        S.finalize()
        return _emit(nc, es, S)
    norm_xT(xs, NSAMP, lambda k0, n: xT_o[:, k0:k0 + n, 0:16], "xT_o")
    xTs = lambda kc: xT_o[:, kc, 0:16]
    tokmajor_proj(16, xTs, "xT_o", Ps[:, 0:768], "Ps")
    dma(skv, Ps[:, 0:768], ["Ps"], ["skv_d"])
    qT_proj(xTs, "xT_o", 16, qT_s[:], "qT_s")
    pt, pk = psg.next()
    for kc in range(8):
        mm(pt[0:16, :], xTs(kc), W_bf[:, kc, C_U:C_U + 512], kc == 0, kc == 7, [("W", kc), "xT_o"], [pk])
    cp("dve", Ps[:, 768:1280], pt[0:16, :], [pk], ["Ps_u"])
    pt, pk = psg.next()
    for kc in range(8):
        mm(pt[:, 0:16], W_bf[:, kc, C_SKV:C_SKV + 128], xTs(kc), kc == 0, kc == 7, [("W", kc), "xT_o"], [pk])
    cp("dve", KsTn[:], pt[:, 0:16], [pk], ["KsTn"])
    dma(win_out[:, 0:511 * 256], win_c[:, 256:512 * 256], (), ["win_d"])
    dma(win_out[:, 511 * 256:512 * 256], Ps[:, 512:768], ["Ps"], ["win_d"])
    dma(pool_out[:, 0:14, :], pool_c[:, 1:15, :], (), ["pool_d"])
    dma(pool_out[:, 14, :], Ps[:, 768:1280], ["Ps_u"], ["pool_d"])
    for kv in range(2):
        st, sk = xst_r.next()
        dma(st[0:1, 0:1024].rearrange("o (n d) -> o n d", n=16),
            skv[:, 384 + kv * 64:384 + (kv + 1) * 64].rearrange("(o n) c -> o n c", o=1), ["skv_d"], [sk])
        cp("dve", vnew[:, :, kv, 0:64], st[0:1, 0:1024].rearrange("o (n d) -> o n d", n=16), [sk], ["vnew"])
    for gp in range(4):
        w = 2 << gp
        nr = w - 1
        uu = Ps[:, 768 + gp * 128:768 + (gp + 1) * 128]
        sr = sred[:, gp * 128:(gp + 1) * 128]
        first = True
        for r0 in range(15 - nr, 15, 8):
            r1 = min(r0 + 8, 15)
            st, sk = xst_r.next()
            stv = st[0:16, 0:(r1 - r0) * 128].rearrange("p (r c) -> p r c", c=128)
            dma(stv, pool_c[:, r0:r1, gp * 128:(gp + 1) * 128], (), [sk])
            dstp = sr if first else d_s_f[:, 0:128]
            S.add("dve", lambda e, dstp=dstp, stv=stv: e.tensor_reduce(dstp, stv.rearrange("p r c -> p c r"), AX.X, ALU.add),
                  [sk], ["osum" if first else "candt"])
            if not first:
                tt("dve", sr, sr, d_s_f[:, 0:128], ALU.add, ["osum", "candt"], ["osum"])
            first = False
        tt("dve", sr, sr, uu, ALU.add, ["osum", "Ps_u"], ["osum"])
        stt("dve", d_s[:, gp * 128:(gp + 1) * 128], sr, 1.0 / w, uu, ALU.mult, ALU.subtract, ["osum", "Ps_u"], ["mix"])
    pt, pk = psg.next()
    for gp in range(4):
        mm(pt[:, gp * 16:(gp + 1) * 16], d_s[:, gp * 128:(gp + 1) * 128], ident_bf[0:16, 0:16], True, True,
           ["mix", K(ident_bf)], [pk])
    cp("dve", dT[:, :, 0:16], pt[:, 0:64].rearrange("p (g n) -> p g n", g=4), [pk], ["dT"])
    yps, ypsk = psg.next()
    for gp in range(4):
        mm(yps[0:16, gp * 128:(gp + 1) * 128], dT[:, gp, 0:16], pw_bf[:, gp, :], True, True, ["dT", "pw"], [ypsk])
    cp("dve", ypool_sb[0:16, :], yps[0:16, :], [ypsk], ["ypool"])
    cp("dve", ptx_f[:], ptx_i[:], [K(ptx_i)], ["ptx_f"])
    ts("dve", ptx_f[:], ptx_f[:], 8.0, sub8[:, 0:1], ALU.mult, ALU.add, ["ptx_f", K(sub8)], ["ptx_f"])
    for q4 in range(4):
        ts("dve", idx_f[:], ptx_f[:], 4.0, float(q4), ALU.mult, ALU.add, ["ptx_f"], ["idx_f"])
        cp("dve", idx_i[:, q4, :], idx_f[:], ["idx_f"], ["idx_i"])

    def gather(dst, dk, cache, n, q4):
        S.add("pool", lambda e: e.indirect_dma_start(
            out=dst, out_offset=None, in_=cache,
            in_offset=bass.IndirectOffsetOnAxis(ap=idx_i[:, q4, n:n + 1], axis=0)), ["idx_i"], [dk], dma=True)

    xgkeys = [xgk]

    class GRing:
        def __init__(self):
            self.t = [(xst[0][:, :], ("xst", 0)), (xst[1][:, :], ("xst", 1)), (xres[:, :], "xres"),
                      (otmp[:].rearrange("p k b g d -> p (k b g d)")[:, 0:1024], "otmp")]
            self.i = 0

        def next(self):
            r = self.t[self.i % 4]
            self.i += 1
            return r

    g_r = GRing()
    for n in range(NS_LOOP):
        for q4 in range(4):
            st, sk = g_r.next()
            gather(st, sk, cache_cmp, n, q4)
            cp("act", cmp_bf[:, q4 * 4:(q4 + 1) * 4, :], st.rearrange("p (e c) -> p e c", e=4), [sk], xgkeys)
        for j in range(2):
            for eh in range(4):
                pt, pk = psg.next()
                for ee in range(4):
                    e_ = eh * 4 + ee
                    mm(pt[:, ee * 128:(ee + 1) * 128], cmp_bf[:, e_, j * 128:(j + 1) * 128], ident_bf[:], True, True,
                       xgkeys + [K(ident_bf)], [pk])
                evac(CTs[:, j, eh * 512:(eh + 1) * 512], pt[:, :], [pk], CTs_keys)
        pts = [psg.next(), psg.next()]
        for kv in range(2):
            P0 = kv * 64
            pt, pk = pts[kv]
            for j in range(2):
                for i in range(32):
                    e_, o_ = i % 16, i // 16
                    mm(pt[:, j * 127:(j + 1) * 127], W1_2[P0:P0 + 64, j, i, :],
                       CTs[P0:P0 + 64, j, e_ * 128 + o_:e_ * 128 + o_ + 127], i == 0, i == 31, ["W1"] + CTs_keys, [pk])
        for kv in range(2):
            pt, pk = pts[kv]
            for j in range(2):
                act(a_act[:, j * 2 + kv, 0:127], pt[:, j * 127:(j + 1) * 127],
                    AF.Silu, [pk, "hbias"], ["a_act"], bias=hbias[:, j:j + 1])
        for j, dst, dk in [(0, kcT, "kcT"), (1, vcT, "vcT")]:
            pt2, pk2 = psg.next()
            for kv in range(2):
                mm(pt2[:, 0:127], w2p[:, j, kv, :], a_act[:, j * 2 + kv, 0:127], kv == 0, kv == 1, ["w2p", "a_act"], [pk2])
            evac(dst[:, 0:127], pt2[:, 0:127], [pk2], [dk])
        pt, pk = psg.next()
        mm(pt[0:127, 0:128], vcT[:, 0:127], ident_bf[:], True, True, ["vcT", K(ident_bf)], [pk])
        evac(vc_s[0:127, :, 0:64], pt[0:127, 0:128].rearrange("p (k d) -> p k d", k=2), [pk], ["vc_s"])
        om, omk = Osm_r.next()
        im, imk = impst_r.next()
        for kv in range(2):
            Pq = slice(kv * 64, kv * 64 + 64)
            pt, pk = psg.next()
            mm(pt[0:127, 0:4], kcT[Pq, 0:127], qT_s[Pq, :, n], True, True, ["kcT", "qT_s"], [pk])
            act(pS_f[0:127, :], pt[0:127, 0:4], AF.Exp, [pk], ["pS_f"], scale=0.125)
            cp("dve", pS_b[0:127, 0, :], pS_f[0:127, :], ["pS_f"], ["pS_b"])
            pa, pak = pacc.next()
            mm(pa[0:4, 0:65], pS_b[0:127, 0, :], vc_s[0:127, kv, 0:65], True, True, ["pS_b", "vc_s"], [pak])
            mm(pa[0:4, 128:168], pS_f[0:127, :], cover33[0:127, :], True, True, ["pS_f", K(cover33)], [pak])
            cp("dve", om[:, kv, 0, :], pa[0:4, 0:65], [pak], [omk])
            cp("dve", im[:, kv, :], pa[0:4, 128:168], [pak], [imk])
        dma(scr_i[n].rearrange("k g s -> g k s"), im[:], [imk], ["scr_i"])
        dma(scr_o[n, :, 0].rearrange("k g e -> g k e"), om[:, :, 0, :], [omk], ["scr_o0"])
    dma(impr.rearrange("p k g s -> p (k g s)"), scr_i.rearrange("n k g s -> n (k g s)"), ["scr_i"], ["otmp"])
    dma(Oall[0:16, :, 0].rearrange("p k g e -> p k (g e)"), scr_o[:, :, 0].rearrange("n k g e -> n k (g e)"), ["scr_o0"], ["Oall"])
    for kv in range(2):
        ts("dve", rdall[0:16, kv, 0], Oall[0:16, kv, 0, :, 64], 1e-30, None, ALU.add, None, ["Oall"], ["rdall"])
        S.add("dve", lambda e, kv=kv: e.reciprocal(rdall[0:16, kv, 0], rdall[0:16, kv, 0]), ["rdall"], ["rdall"])
        ts("dve", impt[0:16, 0:40], impr[:, kv, 0, :], rdall[0:16, kv, 0, 0:1], None, ALU.mult, None, ["otmp", "rdall"], ["impt"])
        for gq in range(1, 4):
            stt("dve", impt[0:16, 0:40], impr[:, kv, gq, :], rdall[0:16, kv, 0, gq:gq + 1], impt[0:16, 0:40],
                ALU.mult, ALU.add, ["otmp", "rdall", "impt"], ["impt"])
        topk(16, 40, impt[0:16, 0:40], cand_s[:, :], forced_s[:, :], seln_s[:, kv, :],
             ["impt", K(cand_s), K(forced_s)], ["seln_s"])
    memset("dve", selx_s[:], 0.0, ["selx_s"])
    for kv in range(2):
        cp("dve", selx_s[0:16, kv, :].rearrange("p (s k) -> p s k", k=4),
           seln_s[:, kv, 0:32].unsqueeze(2).to_broadcast([16, 32, 4]), ["seln_s"], ["selx_s"])
    for n in range(NS_LOOP):
        for q4 in range(4):
            st, sk = g_r.next()
            gather(st, sk, cache_slc, n, q4)
            stv = st.rearrange("p (e c) -> p e c", e=4)
            cp("act", slck_bf[:, q4 * 4:(q4 + 1) * 4, :], stv[:, :, 0:128], [sk], xgkeys)
            cp("dve", Vs[:, q4 * 4:(q4 + 1) * 4, :, 0:64], stv[:, :, 128:256].rearrange("p e (k d) -> p e k d", k=2), [sk],
               [("Vs", i) for i in range(q4 * 4, q4 * 4 + 4)])
        for eh in range(4):
            pt, pk = psg.next()
            for ee in range(4):
                mm(pt[:, ee * 128:(ee + 1) * 128], slck_bf[:, eh * 4 + ee, :], ident_bf[:], True, True,
                   xgkeys + [K(ident_bf)], [pk])
            evac(KsT[:, eh * 512:(eh + 1) * 512], pt[:, :], [pk], [("KsT", i) for i in range(4 * eh, 4 * eh + 4)])
        st, sk = xst_r.next()
        wkv_ = st[:, :].rearrange("p (t c) -> p t c", t=4)
        dma(wkv_, win_out[n].rearrange("(t p c) -> p t c", t=4, p=128), ["win_d"], [sk])
        cp("act", cmp_bf[:, 0:4, 0:128], wkv_[:, :, 0:128], [sk], xgkeys)
        cp("dve", Vw[:, 0:4, :, 0:64], wkv_[:, :, 128:256].rearrange("p t (k d) -> p t k d", k=2), [sk],
           [("Vw", i) for i in range(4)])
        pt, pk = psg.next()
        for t_ in range(4):
            mm(pt[:, t_ * 128:(t_ + 1) * 128], cmp_bf[:, t_, 0:128], ident_bf[:], True, True, xgkeys + [K(ident_bf)], [pk])
        evac(KwT[:, 0:4, :], pt[:, :].rearrange("p (a b) -> p a b", a=4), [pk], [("KwT", i) for i in range(4)])
        om, omk = Osm_r.next()
        for kv in range(2):
            Pq = slice(kv * 64, kv * 64 + 64)
            qn = qT_s[Pq, :, n]
            pt, pk = psg.next()
            for e_ in range(16):
                mm(pt[:, e_ * 4:(e_ + 1) * 4], KsT[Pq, e_ * 128:(e_ + 1) * 128], qn, True, False, [("KsT", e_), "qT_s"], [pk])
                mm(pt[:, e_ * 4:(e_ + 1) * 4], selx_s[:, kv, :], e16[:, n, :], False, True, ["selx_s", K(e16)], [pk])
            act(pS_b[:, 0:16, :].rearrange("p e g -> p (e g)"), pt[:, 0:64], AF.Exp, [pk], ["pS_b"], scale=0.125)
            pt2, pk2 = psg.next()
            mm(pt2[0:1, 0:4], KsTn[Pq, n:n + 1], qn, True, True, ["KsTn", "qT_s"], [pk2])
            act(pN_b[:, :], pt2[0:1, 0:4], AF.Exp, [pk2], ["pN_b"], scale=0.125)
            pa, pak = pacc.next()
            for e_ in range(16):
                mm(pa[0:4, 0:65], pS_b[:, e_, :], Vs[:, e_, kv, 0:65], e_ == 0, False, ["pS_b", ("Vs", e_)], [pak])
            mm(pa[0:4, 0:65], pN_b[:, :], vnew[0:1, n, kv, 0:65], False, True, ["pN_b", "vnew"], [pak])
            cp("dve", om[:, kv, 1, :], pa[0:4, 0:65], [pak], [omk])
            pt, pk = psg.next()
            for t_ in range(4):
                mm(pt[:, t_ * 4:(t_ + 1) * 4], KwT[Pq, t_, :], qn, True, True, [("KwT", t_), "qT_s"], [pk])
            act(pS_b[:, 0:4, :].rearrange("p e g -> p (e g)"), pt[:, 0:16], AF.Exp, [pk], ["pS_b"], scale=0.125)
            pa, pak = pacc.next()
            for t_ in range(4):
                mm(pa[0:4, 0:65], pS_b[:, t_, :], Vw[:, t_, kv, 0:65], t_ == 0, t_ == 3, ["pS_b", ("Vw", t_)], [pak])
            cp("dve", om[:, kv, 2, :], pa[0:4, 0:65], [pak], [omk])
        for kv in range(2):
            dma(scr_o[n, kv, 1:3].rearrange("b g e -> g b e"), om[:, kv, 1:3, :], [omk], ["scr_o1"])
    dma(Oall[0:16, :, 1:3].rearrange("p k b g e -> p k (b g e)"), scr_o[:, :, 1:3].rearrange("n k b g e -> n k (b g e)"), ["scr_o1"], ["Oall"])
    tail(16, xs, ypool_sb[0:16, :], "ypool", ys)

    S.finalize()
    return _emit(nc, es, S)


def _emit(nc, es, S):
    sems = {}
    for e in S.ENG:
        sems[e] = es.enter_context(nc.semaphore(f"s_{e}"))
    dsems = {}
    for e in S.ENG:
        if any(op.dma for op in S.ops[e]):
            dsems[e] = [es.enter_context(nc.semaphore(f"d_{e}{i}")) for i in range(S.R)]
    block = es.enter_context(nc.Block())

    def emit(eng_name):
        def body(e):
            waited = {}

            def wait(sem, val):
                if waited.get(sem.name, 0) >= val:
                    return
                waited[sem.name] = val
                e.wait_ge(sem, val)

            for op in S.ops[eng_name]:
                for d in op.deps:
                    if d.dma:
                        wait(dsems[d.eng][d.sem], d.cnt)
                    else:
                        wait(sems[d.eng], d.cnt)
                if op.dma:
                    if op.guard:
                        wait(dsems[eng_name][op.sem], op.guard)
                    op.fn(e).then_inc(dsems[eng_name][op.sem], 16)
                else:
                    if DEBUG_NAMES is not None:
                        DEBUG_NAMES[nc.get_next_instruction_name()] = (eng_name, op.idx, op.src)
                    ins = op.fn(e)
                    if op.signal:
                        ins.then_inc(sems[eng_name], 1)
            last = {}
            for op in S.ops[eng_name]:
                if op.dma:
                    last[op.sem] = op.cnt
            for s_, c_ in last.items():
                wait(dsems[eng_name][s_], c_)
        return body

    block.tensor(emit("pe"))
    block.scalar(emit("act"))
    block.vector(emit("dve"))
    block.gpsimd(emit("pool"))
    block.sync(emit("sp"))
    es.close()
    return nc


def _consts(r):
    bf = ml_dtypes.bfloat16
    c = {}
    c["c_ident_bf"] = np.eye(128, dtype=np.float32).astype(bf)
    c["c_ident_f"] = np.eye(128, dtype=np.float32)
    c["c_ident4"] = np.tile(np.eye(128, dtype=np.float32), (1, 4)).astype(bf)
    qp = np.arange(128)
    a, i = qp // 32, qp % 32
    trel = a * 128 + 32 * r + i
    kl = np.arange(128)[:, None]
    tri = np.zeros((128, 4, 128), np.float32)
    for j in range(4):
        tri[:, j, :] = np.where(j * 128 + kl <= trel[None, :], 0.0, NEG)
    c["c_triZ"] = tri.reshape(128, -1).astype(bf)
    wz = np.zeros((128, 8, 128), np.float32)
    for jj in range(8):
        kp = (jj - 4) * 128 + kl
        ok = (kp <= trel[None, :]) & (kp > trel[None, :] - 512)
        wz[:, jj, :] = np.where(ok, 0.0, NEG)
    c["c_winZ"] = wz.reshape(128, -1).astype(bf)
    cz = np.zeros((128, N_CTT, 128), np.float32)
    cand = np.zeros((128, NM, 128), np.float32)
    forced = np.zeros((128, NM, 128), np.float32)
    sidx = np.arange(128)[None, :]
    for m in range(NM):
        t = 4 * m * 128 + trel
        for ct in range(n_ct(m)):
            cc = 128 * ct + kl
            ok = (16 * cc + 31 <= t[None, :]) & (cc <= 510)
            cz[:, CT_BASE[m] + ct, :] = np.where(ok, 0.0, NEG)
        cur = (t // 64)[:, None]
        valid = 64 * sidx <= t[:, None]
        fo = valid & ((sidx == 0) | (sidx == cur) | (sidx == cur - 1))
        forced[:, m, :] = fo
        cand[:, m, :] = valid & ~fo
    c["c_cmpZ"] = cz.reshape(128, -1).astype(bf)
    c["c_cand"] = cand.reshape(128, -1).astype(bf)
    c["c_forced"] = forced.reshape(128, -1).astype(bf)
    cov = np.zeros((128, 4, 128), np.float32)
    for ct in range(4):
        cc = 128 * ct + np.arange(128)[:, None]
        cov[:, ct, :] = (cc <= 510) & (cc >= 4 * sidx - 1) & (cc <= 4 * sidx + 3)
    c["c_cover"] = cov.reshape(128, -1)
    pc = np.ones((128, 4, 128), np.float32)
    for gp in range(4):
        w = 2 << gp
        pc[:, gp, :] = (w / np.minimum(w, trel + 1))[None, :]
    c["c_poolcorr"] = pc.reshape(128, -1)
    c33 = np.zeros((128, 40), np.float32)
    cc = np.arange(128)[:, None]
    s40 = np.arange(40)[None, :]
    c33[:] = (cc <= 126) & (cc >= 4 * s40 - 1) & (cc <= 4 * s40 + 3) & (s40 <= 32)
    c["c_cover33"] = c33
    cs = np.zeros((16, 40), np.float32)
    cs[:, 1:31] = 1.0
    fs = np.zeros((16, 40), np.float32)
    fs[:, [0, 31, 32]] = 1.0
    c["c_cand_s"], c["c_forced_s"] = cs, fs
    e = np.zeros((128, 16, 4), np.float32)
    for k in range(16):
        e[k, k, :] = 1.0
    c["c_e16"] = e.reshape(128, 64).astype(bf)
    c["c_sub8"] = (np.arange(128) % 8).astype(np.float32).reshape(128, 1)
    return c


_NC = None


def _prep(x_prompt, x_sample, cache_cmp_kv, cache_slc_kv, cache_win_kv, state_pool, page_table,
          norm_g, w_in, cmp_pe, cmp_w1, cmp_w2, pool_w, pool_scale, w_out, final_g, cores=range(8)):
    f = lambda a: np.ascontiguousarray(np.asarray(a, dtype=np.float32))
    x_prompt = f(x_prompt)
    nblk = x_prompt.shape[1] // 128
    cc = f(cache_cmp_kv).reshape(-1, 1024)
    cs = f(cache_slc_kv).reshape(-1, 1024)
    cw = f(cache_win_kv).reshape(128, 512 * 256)
    sp = f(state_pool).reshape(128, 15, 512)
    pt = np.asarray(page_table, dtype=np.int32)
    xsamp = f(x_sample).reshape(128, D)
    shared = dict(cache_cmp=cc, cache_slc=cs, norm_g=f(norm_g).reshape(1, D), w_in=f(w_in).reshape(D, PT),
                  cmp_pe=f(cmp_pe).reshape(64, 64), cmp_w1=f(cmp_w1).reshape(2, 32, 64, 128),
                  cmp_w2=f(cmp_w2).reshape(2, 128, 64), pool_w=f(pool_w).reshape(4, 128, 128),
                  pool_scale=f(pool_scale).reshape(1, 512), w_out=f(w_out).reshape(D, D),
                  final_g=f(final_g).reshape(1, D))
    in_maps = []
    own_idx = {}
    for c in cores:
        b, r = c // 4, c % 4
        xbat = x_prompt[b]
        tok = (np.arange(nblk)[:, None] * 128 + 32 * r + np.arange(32)[None, :]).reshape(-1)
        own_idx[c] = tok
        hal = (np.arange(nblk)[:, None] * 128 + 32 * r - 16 + np.arange(16)[None, :]).reshape(-1)
        xh = np.where((hal >= 0)[:, None], xbat[np.maximum(hal, 0)], 0.0).astype(np.float32)
        sl = slice(16 * c, 16 * c + 16)
        ptx = np.ascontiguousarray(np.repeat(pt[sl].T, 8, axis=0)).astype(np.int32)
        d = dict(shared)
        d.update(xb=xbat, x_own=np.ascontiguousarray(xbat[tok]), x_halo=np.ascontiguousarray(xh),
                 xs=np.ascontiguousarray(xsamp[sl]), win_c=np.ascontiguousarray(cw[sl]),
                 pool_c=np.ascontiguousarray(sp[sl]), ptx=ptx)
        d.update(_consts(r))
        in_maps.append(d)
    return in_maps, own_idx


def kernel(x_prompt, x_sample, cache_cmp_kv, cache_slc_kv, cache_win_kv, state_pool, page_table,
           norm_g, w_in, cmp_pe, cmp_w1, cmp_w2, pool_w, pool_scale, w_out, final_g):
    global _NC
    if _NC is None:
        _NC = build_program()
    in_maps, own_idx = _prep(x_prompt, x_sample, cache_cmp_kv, cache_slc_kv, cache_win_kv, state_pool, page_table,
                             norm_g, w_in, cmp_pe, cmp_w1, cmp_w2, pool_w, pool_scale, w_out, final_g)
    res = run_bass_kernel_spmd(_NC, in_maps, core_ids=list(range(8))).results
    y_p = np.zeros((2, 8192, D), np.float32)
    kvp = np.zeros((2, 8192, 768), np.float32)
    pool_p = np.zeros((1, 2, 15, 512), np.float32)
    for c in range(8):
        b, r = c // 4, c % 4
        y_p[b, own_idx[c]] = res[c]["y_own"]
        kvp[b, own_idx[c]] = res[c]["kv_own"]
        if r == 3:
            pool_p[0, b] = res[c]["pool_lastT"].reshape(128, 4, 16).transpose(1, 0, 2).reshape(512, 16).T[1:16]
    y_s = np.concatenate([res[c]["ys"] for c in range(8)], 0).reshape(128, 1, D)
    skv = np.concatenate([res[c]["skv"] for c in range(8)], 0)
    win_s = np.concatenate([res[c]["win_out"] for c in range(8)], 0).reshape(1, 128, 512, 2, 2, 64)
    pool_s = np.concatenate([res[c]["pool_out"] for c in range(8)], 0).reshape(1, 128, 15, 512)
    kv6 = lambda a: np.ascontiguousarray(a)
    return (y_p, y_s,
            kv6(kvp[:, :, 0:256]).reshape(1, 2, 8192, 2, 2, 64),
            kv6(kvp[:, :, 256:512]).reshape(1, 2, 8192, 2, 2, 64),
            kv6(kvp[:, 8192 - 512:, 512:768]).reshape(1, 2, 512, 2, 2, 64),
            pool_p,
            kv6(skv[:, 0:256]).reshape(1, 128, 1, 2, 2, 64),
            kv6(skv[:, 256:512]).reshape(1, 128, 1, 2, 2, 64),
            win_s, pool_s)
```
